# Optimizing a Trainium2 kernel written in Bass

```python
import math
import jax, jax.numpy as jnp
from jax import lax
import numpy as np

D_MODEL = 1024
BATCH = 4
SEQ = 4096
DEPTH = 2

CTX_LEN = 256
GRID_W = 64
HEAD_DIM = 64
A_GROUPS = 4
A_WIDTH = A_GROUPS * HEAD_DIM
MLP_CHUNK = 128
ATT_Q_HEADS = 6
ATT_KV_HEADS = 2
ATT_GROUP = ATT_Q_HEADS // ATT_KV_HEADS
ATT_WIDTH = ATT_Q_HEADS * HEAD_DIM
KV_WIDTH = ATT_KV_HEADS * HEAD_DIM
Q_BLOCK = 128
ROPE_THETA = 10000.0
DN_HEADS = 6
DN_WIDTH = DN_HEADS * HEAD_DIM
DN_CONV = 5
DN_CHUNK = 64
MIX_WIDTH = A_WIDTH + ATT_WIDTH + DN_WIDTH
A_COLS = 2 * A_WIDTH
B_COLS = ATT_WIDTH + 2 * KV_WIDTH
C_COLS = 4 * DN_WIDTH + 4 * DN_HEADS
IN_COLS = A_COLS + B_COLS + C_COLS
D_FF = 4 * D_MODEL
DEEPNORM_ALPHA = (2 * DEPTH) ** 0.25
DEEPNORM_BETA = (8 * DEPTH) ** -0.25
EPS = 1e-6

kernel_name = "hybrid_gmlp_gqa_deltanet_dit_block"


def _norm(x):
    xf = x.astype(jnp.float32)
    mu = jnp.mean(xf, -1, keepdims=True)
    var = jnp.mean(jnp.square(xf - mu), -1, keepdims=True)
    return ((xf - mu) * lax.rsqrt(var + EPS)).astype(x.dtype)


def layer_norm(x, g, b):
    return _norm(x) * g + b


def rms_norm(x, g):
    xf = x.astype(jnp.float32)
    return (xf * lax.rsqrt(jnp.mean(xf * xf, -1, keepdims=True) + EPS)).astype(x.dtype) * g


def l2_normalize(x):
    xf = x.astype(jnp.float32)
    return (xf * lax.rsqrt(jnp.sum(xf * xf, -1, keepdims=True) + EPS)).astype(x.dtype)


def modulate(x, shift, scale):
    return _norm(x) * (1.0 + scale) + shift


def axial_rope_angles(rows):
    row = jnp.repeat(jnp.arange(rows, dtype=jnp.float32), GRID_W)
    col = jnp.tile(jnp.arange(GRID_W, dtype=jnp.float32), rows)
    half = HEAD_DIM // 2
    inv = 1.0 / (ROPE_THETA ** (jnp.arange(0, half, 2, dtype=jnp.float32) / half))
    return row[:, None] * inv, col[:, None] * inv


def _rotate(x, ang):
    cos = jnp.cos(ang)[None, :, None, :].astype(x.dtype)
    sin = jnp.sin(ang)[None, :, None, :].astype(x.dtype)
    x1, x2 = jnp.split(x, 2, -1)
    return jnp.concatenate([x1 * cos - x2 * sin, x2 * cos + x1 * sin], -1)


def apply_axial_rope(x, ang_row, ang_col):
    xr, xc = jnp.split(x, 2, -1)
    return jnp.concatenate([_rotate(xr, ang_row), _rotate(xc, ang_col)], -1)


def chunk_token_mlp(p, ln_g, ln_b, w_s, b_s):
    bsz, n, _ = p.shape
    u, v = jnp.split(jax.nn.gelu(p, approximate=False), 2, -1)
    v = layer_norm(v, ln_g, ln_b).reshape(bsz, n // MLP_CHUNK, MLP_CHUNK, A_GROUPS, HEAD_DIM)
    mixed = jnp.einsum('bcsgd,gts->bctgd', v, w_s) + b_s.T[None, None, :, :, None]
    return u * mixed.reshape(bsz, n, A_WIDTH)


def _attend(q, k, v):
    s = jnp.einsum('bqhgd,bkhd->bhgqk', q, k).astype(jnp.float32) * (HEAD_DIM ** -0.5)
    pr = jax.nn.softmax(s, axis=-1).astype(v.dtype)
    return jnp.einsum('bhgqk,bkhd->bqhgd', pr, v)


def gqa_mixer(p_ctx, p_lat, q_g, k_g, ang_row, ang_col, need_ctx):
    def split(p):
        bsz, n, _ = p.shape
        q = p[..., :ATT_WIDTH].reshape(bsz, n, ATT_Q_HEADS, HEAD_DIM)
        k = p[..., ATT_WIDTH:ATT_WIDTH + KV_WIDTH].reshape(bsz, n, ATT_KV_HEADS, HEAD_DIM)
        v = p[..., ATT_WIDTH + KV_WIDTH:].reshape(bsz, n, ATT_KV_HEADS, HEAD_DIM)
        return rms_norm(q, q_g), rms_norm(k, k_g), v

    qc, kc, vc = split(p_ctx)
    ql, kl, vl = split(p_lat)
    ql = apply_axial_rope(ql, ang_row, ang_col)
    kl = apply_axial_rope(kl, ang_row, ang_col)
    bsz, n = p_lat.shape[:2]
    k_all = jnp.concatenate([kl, kc], 1)
    v_all = jnp.concatenate([vl, vc], 1)
    qb = ql.reshape(bsz, n // Q_BLOCK, Q_BLOCK, ATT_KV_HEADS, ATT_GROUP, HEAD_DIM).transpose(1, 0, 2, 3, 4, 5)
    ol = lax.map(lambda qblk: _attend(qblk, k_all, v_all), qb)
    ol = ol.transpose(1, 0, 2, 3, 4, 5).reshape(bsz, n, ATT_WIDTH)
    oc = None
    if need_ctx:
        nc = p_ctx.shape[1]
        oc = _attend(qc.reshape(bsz, nc, ATT_KV_HEADS, ATT_GROUP, HEAD_DIM), kc, vc).reshape(bsz, nc, ATT_WIDTH)
    return oc, ol


def short_conv(x, w):
    out = lax.conv_general_dilated(x, w[:, None, :], window_strides=(1,),
                                   padding=[(DN_CONV // 2, DN_CONV // 2)],
                                   dimension_numbers=('NWC', 'WIO', 'NWC'),
                                   feature_group_count=x.shape[-1])
    return jax.nn.silu(out)


def gated_delta_rule(q, k, v, g, beta, state):
    bsz, n, h, d = q.shape
    nc = n // DN_CHUNK

    def chunks(t):
        t = jnp.moveaxis(t.astype(jnp.float32), 2, 1)
        return t.reshape(bsz, h, nc, DN_CHUNK, *t.shape[3:])

    qf, kf, vf = chunks(q), chunks(k), chunks(v)
    gc = jnp.cumsum(chunks(g), -1)
    bf = chunks(beta)
    incl = jnp.tril(jnp.ones((DN_CHUNK, DN_CHUNK), bool))
    strict = jnp.tril(jnp.ones((DN_CHUNK, DN_CHUNK), bool), -1)
    diff = gc[..., :, None] - gc[..., None, :]
    decay = jnp.where(incl, jnp.exp(jnp.minimum(diff, 0.0)), 0.0)
    kb = kf * bf[..., None]
    lower = jnp.where(strict, jnp.einsum('bhcid,bhcjd->bhcij', kb, kf) * decay, 0.0)
    eye = jnp.eye(DN_CHUNK, dtype=jnp.float32)
    rhs = jnp.concatenate([vf * bf[..., None], kb * jnp.exp(gc)[..., None]], -1)
    sol = lax.linalg.triangular_solve(eye + lower, rhs, left_side=True, lower=True, unit_diagonal=True)
    u, w = jnp.split(sol, 2, -1)
    intra = jnp.where(incl, jnp.einsum('bhcid,bhcjd->bhcij', qf, kf) * decay, 0.0)
    g_last = gc[..., -1]
    q_dec = qf * jnp.exp(gc)[..., None]
    k_dec = kf * jnp.exp(g_last[..., None] - gc)[..., None]
    xs = tuple(jnp.moveaxis(t, 2, 0) for t in (q_dec, k_dec, u, w, intra, g_last))

    def step(s, inp):
        qd, kd, u_i, w_i, a_i, gl = inp
        v_new = u_i - jnp.einsum('bhck,bhkv->bhcv', w_i, s)
        o = jnp.einsum('bhck,bhkv->bhcv', qd, s) + jnp.einsum('bhij,bhjv->bhiv', a_i, v_new)
        s = s * jnp.exp(gl)[..., None, None] + jnp.einsum('bhck,bhcv->bhkv', kd, v_new)
        return s, o

    s_final, o = lax.scan(step, state, xs)
    o = jnp.moveaxis(o, 0, 2).reshape(bsz, h, n, d)
    return jnp.moveaxis(o, 1, 2).astype(v.dtype), s_final


def deltanet_mixer(p_ctx, p_lat, conv_w, a_log, dt_bias, norm_g, need_ctx):
    def prep(p):
        bsz, n, _ = p.shape
        qkv = short_conv(p[..., :3 * DN_WIDTH], conv_w)
        q, k, v = jnp.split(qkv, 3, -1)
        q = l2_normalize(q.reshape(bsz, n, DN_HEADS, HEAD_DIM)) * (HEAD_DIM ** -0.5)
        k = l2_normalize(k.reshape(bsz, n, DN_HEADS, HEAD_DIM))
        v = v.reshape(bsz, n, DN_HEADS, HEAD_DIM)
        z = p[..., 3 * DN_WIDTH:4 * DN_WIDTH]
        a = p[..., 4 * DN_WIDTH:4 * DN_WIDTH + 2 * DN_HEADS].reshape(bsz, n, 2, DN_HEADS).astype(jnp.float32)
        bt = p[..., 4 * DN_WIDTH + 2 * DN_HEADS:].reshape(bsz, n, 2, DN_HEADS).astype(jnp.float32)
        g = -jnp.exp(a_log.astype(jnp.float32)) * jax.nn.softplus(a + dt_bias.astype(jnp.float32))
        return q, k, v, z, g, jax.nn.sigmoid(bt)

    def out_gate(o, z):
        bsz, n = z.shape[:2]
        zz = z.reshape(bsz, n, DN_HEADS, HEAD_DIM)
        return (rms_norm(o, norm_g) * jax.nn.silu(zz)).reshape(bsz, n, DN_WIDTH)

    flip = lambda t: jnp.flip(t, 1)
    qc, kc, vc, zc, gc, bc = prep(p_ctx)
    ql, kl, vl, zl, gl, bl = prep(p_lat)
    bsz = p_lat.shape[0]
    zero = jnp.zeros((bsz, DN_HEADS, HEAD_DIM, HEAD_DIM), jnp.float32)
    oc_f, s_f = gated_delta_rule(qc, kc, vc, gc[:, :, 0], bc[:, :, 0], zero)
    oc_b, s_b = gated_delta_rule(flip(qc), flip(kc), flip(vc), flip(gc[:, :, 1]), flip(bc[:, :, 1]), zero)
    ol_f, _ = gated_delta_rule(ql, kl, vl, gl[:, :, 0], bl[:, :, 0], s_f)
    ol_b, _ = gated_delta_rule(flip(ql), flip(kl), flip(vl), flip(gl[:, :, 1]), flip(bl[:, :, 1]), s_b)
    ol = out_gate(ol_f + flip(ol_b), zl)
    oc = out_gate(oc_f + flip(oc_b), zc) if need_ctx else None
    return oc, ol


def token_mixers(p_ctx, p_lat, ang_row, ang_col, gmlp_ln_g, gmlp_ln_b, gmlp_w_s, gmlp_b_s,
                 attn_q_g, attn_k_g, dn_conv_w, dn_a_log, dn_dt_bias, dn_norm_g, need_ctx):
    s_a, s_b = A_COLS, A_COLS + B_COLS
    ya_l = chunk_token_mlp(p_lat[..., :s_a], gmlp_ln_g, gmlp_ln_b, gmlp_w_s, gmlp_b_s)
    yb_c, yb_l = gqa_mixer(p_ctx[..., s_a:s_b], p_lat[..., s_a:s_b], attn_q_g, attn_k_g, ang_row, ang_col, need_ctx)
    yc_c, yc_l = deltanet_mixer(p_ctx[..., s_b:], p_lat[..., s_b:], dn_conv_w, dn_a_log, dn_dt_bias, dn_norm_g, need_ctx)
    y_lat = jnp.concatenate([ya_l, yb_l, yc_l], -1)
    y_ctx = None
    if need_ctx:
        ya_c = chunk_token_mlp(p_ctx[..., :s_a], gmlp_ln_g, gmlp_ln_b, gmlp_w_s, gmlp_b_s)
        y_ctx = jnp.concatenate([ya_c, yb_c, yc_c], -1)
    return y_ctx, y_lat


def deepnorm_update(x, branch, gate, ln_g, ln_b):
    return layer_norm(DEEPNORM_ALPHA * x + gate * branch, ln_g, ln_b)


def sq_relu_mlp(h, w_up, w_down):
    return jnp.square(jax.nn.relu(h @ w_up)) @ w_down


def setup_inputs(seed: int = 0) -> dict:
    key = jax.random.key(seed)
    ks = jax.random.split(key, 26)
    f32 = jnp.float32
    nrm = lambda k, shape, scale: jax.random.normal(k, shape, f32) * scale
    L, D = DEPTH, D_MODEL
    dt = jnp.exp(jax.random.uniform(ks[16], (L, 2, DN_HEADS), f32, math.log(1e-3), math.log(1e-1)))
    return {
        "x": nrm(ks[0], (BATCH, SEQ, D), 1.0),
        "c": nrm(ks[1], (BATCH, D), 1.0),
        "ctx": nrm(ks[2], (BATCH, CTX_LEN, D), 1.0),
        "c_ctx": nrm(ks[3], (D,), 1.0),
        "mod_w": nrm(ks[4], (L, D, 6 * D), D ** -0.5),
        "mod_b": nrm(ks[5], (L, 6 * D), 0.01),
        "w_in": nrm(ks[6], (L, D, IN_COLS), D ** -0.5),
        "w_out": nrm(ks[7], (L, MIX_WIDTH, D), MIX_WIDTH ** -0.5 * DEEPNORM_BETA),
        "gmlp_ln_g": 1.0 + nrm(ks[8], (L, A_WIDTH), 0.02),
        "gmlp_ln_b": nrm(ks[9], (L, A_WIDTH), 0.02),
        "gmlp_w_s": nrm(ks[10], (L, A_GROUPS, MLP_CHUNK, MLP_CHUNK), MLP_CHUNK ** -0.5),
        "gmlp_b_s": 1.0 + nrm(ks[11], (L, A_GROUPS, MLP_CHUNK), 0.02),
        "attn_q_g": 1.0 + nrm(ks[12], (L, HEAD_DIM), 0.02),
        "attn_k_g": 1.0 + nrm(ks[13], (L, HEAD_DIM), 0.02),
        "dn_conv_w": nrm(ks[14], (L, DN_CONV, 3 * DN_WIDTH), DN_CONV ** -0.5),
        "dn_a_log": jnp.log(jax.random.uniform(ks[15], (L, 2, DN_HEADS), f32, 1.0, 16.0)),
        "dn_dt_bias": dt + jnp.log(-jnp.expm1(-dt)),
        "dn_norm_g": 1.0 + nrm(ks[17], (L, HEAD_DIM), 0.02),
        "ln1_g": 1.0 + nrm(ks[18], (L, D), 0.02),
        "ln1_b": nrm(ks[19], (L, D), 0.02),
        "ln2_g": 1.0 + nrm(ks[20], (L, D), 0.02),
        "ln2_b": nrm(ks[21], (L, D), 0.02),
        "w_up": nrm(ks[22], (L, D, D_FF), D ** -0.5),
        "w_down": nrm(ks[23], (L, D_FF, D), D_FF ** -0.5 * DEEPNORM_BETA),
    }


def reference(x, c, ctx, c_ctx, mod_w, mod_b, w_in, w_out, gmlp_ln_g, gmlp_ln_b, gmlp_w_s, gmlp_b_s,
              attn_q_g, attn_k_g, dn_conv_w, dn_a_log, dn_dt_bias, dn_norm_g,
              ln1_g, ln1_b, ln2_g, ln2_b, w_up, w_down):
    n = x.shape[1]
    rows = n // GRID_W
    ang_row, ang_col = axial_rope_angles(rows)
    x_lat, x_ctx = x, ctx
    for i in range(DEPTH):
        need_ctx = i < DEPTH - 1
        mod_l = (jax.nn.silu(c) @ mod_w[i] + mod_b[i])[:, None, :]
        mod_c = (jax.nn.silu(c_ctx) @ mod_w[i] + mod_b[i])[None, None, :]
        sh1_l, sc1_l, g1_l, sh2_l, sc2_l, g2_l = jnp.split(mod_l, 6, -1)
        sh1_c, sc1_c, g1_c, sh2_c, sc2_c, g2_c = jnp.split(mod_c, 6, -1)
        p_lat = modulate(x_lat, sh1_l, sc1_l) @ w_in[i]
        p_ctx = modulate(x_ctx, sh1_c, sc1_c) @ w_in[i]
        y_ctx, y_lat = token_mixers(p_ctx, p_lat, ang_row, ang_col, gmlp_ln_g[i], gmlp_ln_b[i], gmlp_w_s[i],
                                    gmlp_b_s[i], attn_q_g[i], attn_k_g[i], dn_conv_w[i], dn_a_log[i],
                                    dn_dt_bias[i], dn_norm_g[i], need_ctx)
        x_lat = deepnorm_update(x_lat, y_lat @ w_out[i], g1_l, ln1_g[i], ln1_b[i])
        m_lat = sq_relu_mlp(modulate(x_lat, sh2_l, sc2_l), w_up[i], w_down[i])
        x_lat = deepnorm_update(x_lat, m_lat, g2_l, ln2_g[i], ln2_b[i])
        if need_ctx:
            x_ctx = deepnorm_update(x_ctx, y_ctx @ w_out[i], g1_c, ln1_g[i], ln1_b[i])
            m_ctx = sq_relu_mlp(modulate(x_ctx, sh2_c, sc2_c), w_up[i], w_down[i])
            x_ctx = deepnorm_update(x_ctx, m_ctx, g2_c, ln2_g[i], ln2_b[i])
    return x_lat
```

```python
import math
import numpy as np
import ml_dtypes
import concourse.bass as bass
import concourse.mybir as mybir
from concourse.bass_utils import run_bass_kernel_spmd

F32 = mybir.dt.float32
BF16 = mybir.dt.bfloat16
AF = mybir.ActivationFunctionType
ALU = mybir.AluOpType
AX = mybir.AxisListType

D = 1024
L = 2
SEQ = 4096
CTX = 256
NTOK = SEQ + CTX
OWN_LAT = SEQ // 2
OWN_CTX = CTX // 2
OWN = OWN_LAT + OWN_CTX
DFF = 4096
ALPHA = float((2 * L) ** 0.25)
EPS = 1e-6
NCORES = 8


class Tk:
    __slots__ = ("name", "w", "r", "dsem", "dcnt")
    FENCE = {}

    def __init__(self, name=""):
        self.name = name
        self.w = {}
        self.r = dict(Tk.FENCE)
        self.dsem = None
        self.dcnt = 0


class _Eng:
    def __init__(self, name, sem):
        self.name = name
        self.sem = sem
        self.count = 0
        self.known = {}
        self.ops = []


NOCOLL = [False]
NO_SAME_ENGINE_SYNC = {"pe"}


class KB:
    ENGS = ("pe", "act", "dve", "pool", "sp")

    def __init__(self, nc, same_engine_sync=True):
        self.nc = nc
        self.same = same_engine_sync
        self._ctx = []
        self.eng = {}
        for n in self.ENGS:
            self.eng[n] = _Eng(n, self._sem("e_" + n))
        self.nwaits = 0
        self._uid = 0
        Tk.FENCE = {}
        self._dma_owners = []
        self._pool_hist = []
        self.max_pool_dma = 12
        self.coll_inc = 1
        self._free_sems = []
        self._fence_base = {}
        self._coll_owners = []
        self.arena_words = 53000
        self.arena = self._enter(self.nc.sbuf_tensor("arena", [128, self.arena_words], F32))
        self.arena_off = 0
        self.arena_peak = 0

    def mark(self):
        return (self.arena_off, len(self._dma_owners))

    def release(self, mark):
        off, nown = mark
        ev = dict(self._fence_base)
        for e in self.eng.values():
            if e.count:
                ev[e.sem] = e.count
        for t in self._dma_owners:
            ev[t.dsem] = t.dcnt
        for t in self._dma_owners[nown:]:
            self._free_sems.append((t.dsem, t.dcnt))
            self._fence_base[t.dsem] = t.dcnt
        del self._dma_owners[nown:]
        Tk.FENCE = ev
        self.arena_off = off

    def _new_dsem(self, owner, prefix="d_", queue="sp"):
        if self._free_sems and queue != "pool":
            owner.dsem, owner.dcnt = self._free_sems.pop()
        else:
            owner.dsem = self._sem(prefix + owner.name)
        self._dma_owners.append(owner)

    def _enter(self, cm):
        v = cm.__enter__()
        self._ctx.append(cm)
        return v

    def _sem(self, name):
        self._uid = getattr(self, "_uid", 0) + 1
        return self._enter(self.nc.semaphore(f"{name}_{self._uid}"))

    def sbuf(self, name, shape, dtype=F32):
        shape = list(shape)
        cnt = 1
        for s in shape[1:]:
            cnt *= s
        isz = 2 if dtype == BF16 else 4
        nwords = (cnt * isz + 3) // 4
        nwords = (nwords + 7) // 8 * 8
        off = self.arena_off
        if off + nwords > self.arena_words:
            raise AssertionError(f"arena overflow allocating {name} {shape}: off={off * 4} need={nwords * 4}")
        self.arena_off = off + nwords
        self.arena_peak = max(self.arena_peak, self.arena_off)
        ap = self.arena[0:shape[0], off:off + nwords]
        if dtype == BF16:
            ap = ap.bitcast(BF16)
        elif dtype != F32:
            raise AssertionError("arena supports f32/bf16")
        ap = ap[:, 0:cnt]
        if len(shape) > 2:
            names = " ".join(f"d{k}" for k in range(len(shape) - 1))
            kw = {f"d{k}": shape[k + 1] for k in range(len(shape) - 1)}
            ap = ap.rearrange(f"p ({names}) -> p {names}", **kw)
        return ap

    def psum(self, name, shape, dtype=F32):
        return self._enter(self.nc.psum_tensor(name, list(shape), dtype))

    def close(self):
        while self._ctx:
            self._ctx.pop().__exit__(None, None, None)

    def _deps(self, e, reads, writes, waw=True):
        need = {}
        for t in reads:
            for s, v in t.w.items():
                if need.get(s, 0) < v:
                    need[s] = v
        for t in writes:
            for d in ((t.w, t.r) if waw else (t.r,)):
                for s, v in d.items():
                    if need.get(s, 0) < v:
                        need[s] = v
        waits = []
        for s, v in need.items():
            if s is e.sem and (e.name in NO_SAME_ENGINE_SYNC or not self.same):
                continue
            if e.known.get(s, 0) < v:
                e.known[s] = v
                waits.append((s, v))
        self.nwaits += len(waits)
        return waits

    def op(self, eng, meth, *args, reads=(), writes=(), **kwargs):
        e = self.eng[eng]

        def fn(h, meth=meth, args=args, kwargs=kwargs):
            return getattr(h, meth)(*args, **kwargs)
        reads = [t for t in reads if t is not None]
        writes = [t for t in writes if t is not None]
        waits = self._deps(e, reads, writes)
        e.count += 1
        ev = (e.sem, e.count)
        e.ops.append((waits, fn, (e.sem, 1)))
        for t in writes:
            t.w = {ev[0]: ev[1]}
            t.r = {}
        for t in reads:
            if t.r.get(ev[0], 0) < ev[1]:
                t.r[ev[0]] = ev[1]
        return ev

    def dma(self, out, in_, reads=(), writes=(), owner=None, queue="sp", **kw):
        e = self.eng[queue]
        reads = [t for t in reads if t is not None]
        writes = [t for t in writes if t is not None]
        if owner is None:
            owner = (writes + reads)[0]
        if owner.dsem is None:
            self._new_dsem(owner, queue=queue)
        waits = self._deps(e, reads, writes)
        waits += self._pace(e, queue)
        owner.dcnt += 16
        ev = (owner.dsem, owner.dcnt)
        if queue == "pool":
            self._pool_hist.append(ev)

        def fn(h, out=out, in_=in_, kw=kw):
            return h.dma_start(out=out, in_=in_, **kw)
        e.ops.append((waits, fn, (owner.dsem, 16)))
        for t in writes:
            t.w = {ev[0]: ev[1]}
            t.r = {}
        for t in reads:
            if t.r.get(ev[0], 0) < ev[1]:
                t.r[ev[0]] = ev[1]
        return ev

    def dma_acc(self, out, in_, reads=(), writes=(), owner=None, queue="sp", **kw):
        e = self.eng[queue]
        reads = [t for t in reads if t is not None]
        writes = [t for t in writes if t is not None]
        if owner is None:
            owner = (writes + reads)[0]
        if owner.dsem is None:
            self._new_dsem(owner, queue=queue)
        waits = self._deps(e, reads, writes, waw=False)
        waits += self._pace(e, queue)
        owner.dcnt += 16
        ev = (owner.dsem, owner.dcnt)
        if queue == "pool":
            self._pool_hist.append(ev)

        def fn(h, out=out, in_=in_, kw=kw):
            return h.dma_start(out=out, in_=in_, **kw)
        e.ops.append((waits, fn, (owner.dsem, 16)))
        for t in writes:
            t.w[ev[0]] = max(t.w.get(ev[0], 0), ev[1])
        for t in reads:
            if t.r.get(ev[0], 0) < ev[1]:
                t.r[ev[0]] = ev[1]
        return ev

    def coll(self, kind, src, dst, groups, reads=(), writes=(), acc=False):
        if NOCOLL[0]:
            n = src.shape[0]
            (self.dma_acc if acc else self.dma)(dst[0:n, :], src, reads=reads, writes=writes, owner=writes[0])
            self.dma_acc(dst[n:2 * n, :], src, reads=reads, writes=writes, owner=writes[0])
            return
        e = self.eng["pool"]
        reads = [t for t in reads if t is not None]
        writes = [t for t in writes if t is not None]
        owner = writes[0]
        if owner.dsem is None:
            owner.dsem = self._sem("c_" + owner.name)
            self._coll_owners.append(owner)
        waits = self._deps(e, reads, writes, waw=not acc)
        owner.dcnt += self.coll_inc
        ev = (owner.dsem, owner.dcnt)

        def fn(h, kind=kind, src=src, dst=dst, groups=groups):
            return h.collective_compute(kind, ALU.bypass, replica_groups=groups, ins=[src.opt()], outs=[dst.opt()])
        e.ops.append((waits, fn, (owner.dsem, None if self.coll_inc == 1 else self.coll_inc)))
        for t in writes:
            if acc:
                t.w[ev[0]] = max(t.w.get(ev[0], 0), ev[1])
            else:
                t.w = {ev[0]: ev[1]}
                t.r = {}
        for t in reads:
            if t.r.get(ev[0], 0) < ev[1]:
                t.r[ev[0]] = ev[1]
        return ev

    def _pace(self, e, queue):
        if queue != "pool" or len(self._pool_hist) < self.max_pool_dma:
            return []
        s, v = self._pool_hist[-self.max_pool_dma]
        if e.known.get(s, 0) < v:
            e.known[s] = v
            return [(s, v)]
        return []

    def wait_all(self, eng, tks):
        e = self.eng[eng]
        waits = self._deps(e, tks, [])
        if waits:
            e.ops.append((waits, None, None))

    def emit(self):
        nc = self.nc
        handles = {"pe": "tensor", "act": "scalar", "dve": "vector", "pool": "gpsimd", "sp": "sync"}
        with nc.Block() as block:
            for n in self.ENGS:
                e = self.eng[n]
                if not e.ops:
                    continue

                def body(h, e=e):
                    for waits, fn, inc in e.ops:
                        for s, v in waits:
                            h.wait_ge(s, v)
                        if fn is not None:
                            ins = fn(h)
                            if inc is not None:
                                if inc[1] is None:
                                    ins.then_inc(inc[0])
                                else:
                                    ins.then_inc(inc[0], inc[1])
                getattr(block, handles[n])(body)


class Ring:
    def __init__(self, kb, name, shape, dtype, n, psum=False):
        alloc = kb.psum if psum else kb.sbuf
        self.t = [alloc(f"{name}{i}", shape, dtype) for i in range(n)]
        self.tk = [Tk(f"{name}{i}") for i in range(n)]
        self.n = n
        self.i = 0

    def next(self):
        i = self.i % self.n
        self.i += 1
        return self.t[i], self.tk[i]


class PsRing:
    def __init__(self, banks, idx):
        self.banks = banks
        self.idx = idx
        self.i = 0

    def next(self):
        b = self.banks[self.idx[self.i % len(self.idx)]]
        self.i += 1
        return b


def make_banks(kb):
    return [(kb.psum(f"bank{i}", [128, 512], F32), Tk(f"bank{i}")) for i in range(8)]


class Common:
    def __init__(self, kb):
        self.kb = kb
        self.ident = kb.sbuf("ident", [128, 128], F32)
        self.t_ident = Tk("ident")
        self.identb = kb.sbuf("identb", [128, 128], BF16)
        self.t_identb = Tk("identb")
        self.mhalf = kb.sbuf("mhalf", [128, 16], F32)
        self.t_mhalf = Tk("mhalf")
        kb.op("pool", "memset", self.ident[:], 0.0, writes=[self.t_ident])
        kb.op("pool", "affine_select", self.ident[:], self.ident[:], pattern=[[-1, 128]],
                                                compare_op=ALU.not_equal, fill=1.0, base=0,
                                                channel_multiplier=1,
              reads=[self.t_ident], writes=[self.t_ident])
        kb.op("pool", "tensor_copy", self.identb[:], self.ident[:],
              reads=[self.t_ident], writes=[self.t_identb])
        kb.op("pool", "memset", self.mhalf[:], -0.5, writes=[self.t_mhalf])


def emit_rstd(kb, cm, mv, t_mv, n, rstd, t_rstd):
    kb.op("dve", "tensor_scalar", rstd[:, 0:n], mv[:, 0:n, 1], EPS, None, op0=ALU.add,
          reads=[t_mv], writes=[t_rstd])
    kb.op("pool", "tensor_tensor", rstd[:, 0:n], rstd[:, 0:n], cm.mhalf[:, 0:n], op=ALU.pow,
          reads=[t_rstd, cm.t_mhalf], writes=[t_rstd])


def emit_stats(kb, x_ap, t_x, width, bnst, t_bnst, mv_ap, t_mv):
    nch = max(1, width // 512)
    cw = width // nch
    for c in range(nch):
        kb.op("dve", "bn_stats", bnst[:, c, :], x_ap[:, c * cw:(c + 1) * cw],
              reads=[t_x], writes=[t_bnst])
    kb.op("dve", "bn_aggr", mv_ap, bnst[:, 0:nch, :], reads=[t_bnst], writes=[t_mv])


def emit_P0(kb, cm, io, bank):
    c_b, c_ctx, mod_w, mod_b, modv = io["c_b"], io["c_ctx"], io["mod_w"], io["mod_b"], io["modv"]
    cT = kb.sbuf("p0_cT", [128, 2, 8], F32)
    t_cT = Tk("p0_cT")
    sT = kb.sbuf("p0_sT", [128, 8, 2], F32)
    t_sT = Tk("p0_sT")
    kb.dma(cT[:, 0, :], c_b.rearrange("(c p) -> p c", p=128), writes=[t_cT], allow_slow_non_contiguous=True)
    kb.dma_acc(cT[:, 1, :], c_ctx.rearrange("(c p) -> p c", p=128), writes=[t_cT], allow_slow_non_contiguous=True)
    kb.op("act", "activation", sT[:].rearrange("p c j -> p j c"), cT[:], AF.Silu,
          reads=[t_cT], writes=[t_sT])
    wring = Ring(kb, "p0_w", [128, 8, 512], F32, 2)
    bring = Ring(kb, "p0_b", [2, 512], F32, 2)
    rring = Ring(kb, "p0_r", [2, 512], F32, 2)
    ps, t_ps = bank
    t_out = io["t_modv"]
    for i in range(L):
        for n in range(12):
            w, t_w = wring.next()
            for kc in range(8):
                f = kb.dma if kc == 0 else kb.dma_acc
                f(w[:, kc, :], mod_w[i, kc * 128:(kc + 1) * 128, n * 512:(n + 1) * 512], writes=[t_w])
            b, t_b = bring.next()
            kb.dma(b[:], mod_b[i:i + 1, n * 512:(n + 1) * 512].to_broadcast([2, 512]), writes=[t_b])
            for kc in range(8):
                kb.op("pe", "matmul", ps[0:2, :], sT[:, kc, :], w[:, kc, :],
                                                           start=(kc == 0), stop=(kc == 7),
                      reads=[t_sT, t_w], writes=[t_ps])
            r, t_r = rring.next()
            kb.op("dve", "tensor_tensor", r[:], ps[0:2, :], b[:], op=ALU.add,
                  reads=[t_ps, t_b], writes=[t_r])
            if n in (2, 3, 8, 9):
                kb.op("dve", "tensor_scalar", r[:], r[:], 1.0, None, op0=ALU.add,
                      reads=[t_r], writes=[t_r])
            kb.dma_acc(modv[i, :, n * 512:(n + 1) * 512], r[:], reads=[t_r], writes=[t_out], owner=t_r)


class ModBufs:
    def __init__(self, kb, name, nt):
        self.nt = nt
        self.xn = [kb.sbuf(f"{name}_xn{t}", [128, D], F32) for t in range(nt)]
        self.t_xn = [Tk(f"{name}_xn{t}") for t in range(nt)]
        self.bnst = kb.sbuf(f"{name}_bnst", [128, nt, 2, 6], F32)
        self.t_bnst = [Tk(f"{name}_bnst{t}") for t in range(nt)]
        self.mv = kb.sbuf(f"{name}_mv", [128, nt, 2], F32)
        self.t_mv = Tk(f"{name}_mv")
        self.rstd = kb.sbuf(f"{name}_rstd", [128, nt], F32)
        self.t_rstd = Tk(f"{name}_rstd")


def emit_modulate(kb, cm, mb, xs, nt, cols, t_cols, sc_idx, sh_idx, out, t_out, psring, evac_flip=[0]):
    for t in range(nt):
        x_ap, t_x = xs[t]
        emit_stats(kb, x_ap, t_x, D, mb.bnst[:, t], mb.t_bnst[t], mb.mv[:, t, :], mb.t_mv)
    emit_rstd(kb, cm, mb.mv, mb.t_mv, nt, mb.rstd, mb.t_rstd)
    for t in range(nt):
        x_ap, t_x = xs[t]
        kb.op("dve", "tensor_scalar",
            mb.xn[t][:], x_ap, mb.mv[:, t, 0:1], mb.rstd[:, t:t + 1], op0=ALU.subtract, op1=ALU.mult,
            reads=[t_x, mb.t_mv, mb.t_rstd], writes=[mb.t_xn[t]])
    for fc in range(8):
        ps, t_ps = psring.next()
        for t in range(nt):
            kb.op("pe", "transpose",
                ps[:, t * 128:(t + 1) * 128], mb.xn[t][:, fc * 128:(fc + 1) * 128], cm.ident[:],
                reads=[mb.t_xn[t], cm.t_ident], writes=[t_ps])
        evac_flip[0] ^= 1
        if evac_flip[0]:
            kb.op("act", "activation",
                out[:, fc, 0:nt * 128], ps[:, 0:nt * 128], AF.Identity,
                bias=cols[:, sh_idx, fc:fc + 1], scale=cols[:, sc_idx, fc:fc + 1],
                reads=[t_ps, t_cols], writes=[t_out])
        else:
            kb.op("dve", "tensor_scalar",
                out[:, fc, 0:nt * 128], ps[:, 0:nt * 128], cols[:, sc_idx, fc:fc + 1],
                cols[:, sh_idx, fc:fc + 1], op0=ALU.mult, op1=ALU.add,
                reads=[t_ps, t_cols], writes=[t_out])


def load_cols_T(kb, cm, src_vec, cols_dst, t_cols, bank, reads=(), name="cstage", stage_buf=None):
    if stage_buf is None:
        stage, t_stage = kb.sbuf(name, [48, 128], F32), Tk(name)
    else:
        stage, t_stage = stage_buf
    kb.dma(stage[:], src_vec.rearrange("(r p) -> r p", p=128), reads=list(reads), writes=[t_stage])
    ps, t_ps = bank
    kb.op("pe", "transpose", ps[:, 0:48], stage[:], cm.ident[0:48, 0:48], reads=[t_stage, cm.t_ident], writes=[t_ps])
    kb.op("act", "copy", cols_dst.rearrange("p j c -> p (j c)"), ps[:, 0:48], reads=[t_ps], writes=[t_cols])


def load_cols(kb, modv, layer, which, cols, t_cols, first=True):
    f = kb.dma if first else kb.dma_acc
    f(cols[:, 0:6, :], modv[layer, which, :].rearrange("(j c p) -> p j c", j=6, p=128),
      writes=[t_cols], allow_slow_non_contiguous=True)


def emit_PA(kb, cm, io, psring):
    x_in, modv, xmT_out = io["x_in"], io["modv"], io["xmT_out"]
    t_modv = io["t_modv"]
    t_xmT = io["t_xmT_out"]
    colsL = kb.sbuf("pa_colsL", [128, 6, 8], F32)
    t_colsL = Tk("pa_colsL")
    colsC = kb.sbuf("pa_colsC", [128, 6, 8], F32)
    t_colsC = Tk("pa_colsC")
    load_cols_T(kb, cm, modv[0, 0, :], colsL[:, 0:6, :], t_colsL, psring.next(), reads=[t_modv], name="pa_csL")
    load_cols_T(kb, cm, modv[0, 1, :], colsC[:, 0:6, :], t_colsC, psring.next(), reads=[t_modv], name="pa_csC")
    NT = 4
    mb = ModBufs(kb, "pa", NT)
    xring = Ring(kb, "pa_x", [128, NT, D], F32, 2)
    oring = Ring(kb, "pa_o", [128, 8, NT * 128], BF16, 2)
    groups = [(g * 512, 4, colsL, t_colsL) for g in range(4)] + [(OWN_LAT, 1, colsC, t_colsC)]
    for (tok0, nt, cols, t_cols) in groups:
        xt, t_xt = xring.next()
        kb.dma(xt[:, 0:nt, :], x_in[tok0:tok0 + nt * 128, :].rearrange("(t p) d -> p t d", p=128), writes=[t_xt])
        o, t_o = oring.next()
        emit_modulate(kb, cm, mb, [(xt[:, t, :], t_xt) for t in range(nt)], nt, cols, t_cols, 1, 0,
                      o, t_o, psring)
        kb.dma_acc(xmT_out[:, tok0:tok0 + nt * 128].rearrange("(c p) t -> p c t", p=128),
                   o[:, :, 0:nt * 128], reads=[t_o], writes=[t_xmT], owner=t_o)


def load_weight_bf16(kb, dst, t_dst, src, nchunk, per=1):
    first = True
    for c0 in range(0, nchunk, per):
        c1 = min(nchunk, c0 + per)
        f = kb.dma if first else kb.dma_acc
        first = False
        f(dst[:, c0:c1, :], src[c0 * 128:c1 * 128, :].rearrange("(c p) n -> p c n", p=128),
          writes=[t_dst], queue="pool")


def load_row(kb, dst, t_dst, src_row, n, first=True):
    f = kb.dma if first else kb.dma_acc
    f(dst, src_row.to_broadcast([128, n]), writes=[t_dst])


def emit_deepnorm(kb, cm, xt_ap, t_xt, br_list, grow, t_grow, lng, lnb, t_ln, tmp, t_tmp,
                  bnst, t_bnst, mv, t_mv, rstd, t_rstd):
    for hf, (br, t_br) in enumerate(br_list):
        kb.op("dve", "tensor_tensor",
            tmp[:, hf * 512:(hf + 1) * 512], br, grow[:, hf * 512:(hf + 1) * 512], op=ALU.mult,
            reads=[t_br, t_grow], writes=[t_tmp])
    kb.op("dve", "scalar_tensor_tensor", xt_ap, xt_ap, ALPHA, tmp[:], op0=ALU.mult, op1=ALU.add,
          reads=[t_xt, t_tmp], writes=[t_xt])
    emit_stats(kb, xt_ap, t_xt, D, bnst, t_bnst, mv[:, 0, :], t_mv)
    emit_rstd(kb, cm, mv, t_mv, 1, rstd, t_rstd)
    kb.op("dve", "tensor_scalar", xt_ap, xt_ap, mv[:, 0, 0:1], rstd[:, 0:1],
                                           op0=ALU.subtract, op1=ALU.mult,
          reads=[t_xt, t_mv, t_rstd], writes=[t_xt])
    kb.op("pool", "tensor_tensor", xt_ap, xt_ap, lng, op=ALU.mult, reads=[t_xt, t_ln], writes=[t_xt])
    kb.op("pool", "tensor_tensor", xt_ap, xt_ap, lnb, op=ALU.add, reads=[t_xt, t_ln], writes=[t_xt])


def emit_PR(kb, cm, io, layer, last, banks):
    modv = io["modv"]
    t_modv = io.get("t_modv")
    x_in, xmT_own, yM = io["x_in"], io["xmT_own"], io["yM"]
    x1s, xm2Ts = io["x1s"], io["xm2Ts"]
    x_out = io["x_out"]
    xmT_next = io.get("xmT_next")
    t_x1s, t_xm2Ts, t_xout = Tk("x1s"), Tk("xm2Ts"), io["t_x_out"]
    t_xmTn = io.get("t_xmT_next")
    pre = f"r{layer}_"

    s1reg = kb.sbuf(pre + "s1reg", [128, 16384], BF16)
    wA = s1reg[:, 0:4096].rearrange("p (c n) -> p c n", c=8); t_wA = Tk(pre + "wA")
    wout = s1reg[:, 4096:12288].rearrange("p (c n) -> p c n", c=8); t_wout = Tk(pre + "wout")
    xm_slots = [s1reg[:, 12288 + 2048 * s:12288 + 2048 * (s + 1)].rearrange("p (c n) -> p c n", c=8) for s in range(2)]
    t_xm_slots = [Tk(pre + f"xm{s}") for s in range(2)]
    wdown_b = s1reg[:, :].rearrange("p (c n) -> p c n", c=16); t_wdown_b = Tk(pre + "wdown_b")
    wup = kb.sbuf(pre + "wup", [128, 8, DFF], BF16); t_wup = Tk(pre + "wup")
    wdown_a = kb.sbuf(pre + "wdown_a", [128, 16, D], BF16); t_wdown_a = Tk(pre + "wdown_a")
    load_weight_bf16(kb, wA, t_wA, io["wA"], 8, per=8)
    load_weight_bf16(kb, wout, t_wout, io["wout"], 8, per=4)
    rows = kb.sbuf(pre + "rows", [128, 3, D], F32)
    t_rows = [Tk(pre + "rowsG")]
    t_rowsLN = Tk(pre + "rowsLN")
    rowsA = kb.sbuf(pre + "rowsA", [128, 2, 256], F32)
    t_rowsA = Tk(pre + "rowsA")

    def load_gates(which, sub):
        kb.dma(rows[:, 0, :], modv[layer, which:which + 1, (2 + 3 * sub) * D:(3 + 3 * sub) * D].to_broadcast([128, D]),
               reads=[t_modv], writes=[t_rows[0]])

    def load_ln(sub):
        kb.dma(rows[:, 1, :], io[f"ln{sub + 1}_g"].to_broadcast([128, D]), writes=[t_rowsLN])
        kb.dma_acc(rows[:, 2, :], io[f"ln{sub + 1}_b"].to_broadcast([128, D]), writes=[t_rowsLN])
    load_gates(0, 0)
    load_ln(0)
    kb.dma(rowsA[:, 0, :], io["gmlp_ln_g"].to_broadcast([128, 256]), writes=[t_rowsA])
    kb.dma_acc(rowsA[:, 1, :], io["gmlp_ln_b"].to_broadcast([128, 256]), writes=[t_rowsA])
    colsL = kb.sbuf(pre + "colsL", [128, 12, 8], F32); t_colsL = Tk(pre + "colsL")
    colsC = kb.sbuf(pre + "colsC", [128, 12, 8], F32); t_colsC = Tk(pre + "colsC")
    cstage = (kb.sbuf(pre + "cstage", [48, 128], F32), Tk(pre + "cstage"))
    for which, cols, t_colsX in ((0, colsL, t_colsL), (1, colsC, t_colsC)):
        load_cols_T(kb, cm, modv[layer, which, :], cols[:, 0:6, :], t_colsX, banks[6 + which], reads=[t_modv],
                    stage_buf=cstage)
        if not last:
            load_cols_T(kb, cm, modv[layer + 1, which, :], cols[:, 6:12, :], t_colsX, banks[6 + which], reads=[t_modv],
                        stage_buf=cstage)
    wsn = kb.sbuf(pre + "wsn", [128, 4, 128], F32); t_wsn = Tk(pre + "wsn")
    wsT = kb.sbuf(pre + "wsT", [128, 4, 128], BF16); t_wsT = Tk(pre + "wsT")
    bsT = kb.sbuf(pre + "bsT", [128, 4], F32); t_bsT = Tk(pre + "bsT")
    kb.dma(wsn[:], io["gmlp_w_s"].rearrange("g t s -> t g s"), writes=[t_wsn])
    kb.dma(bsT[:], io["gmlp_b_s"].rearrange("g t -> t g"), writes=[t_bsT], allow_slow_non_contiguous=True)
    for g in range(4):
        ps, t_ps = banks[6 + (g % 2)]
        kb.op("pe", "transpose", ps[:, 0:128], wsn[:, g, :], cm.ident[:],
              reads=[t_wsn, cm.t_ident], writes=[t_ps])
        kb.op("act", "copy", wsT[:, g, :], ps[:, 0:128], reads=[t_ps], writes=[t_wsT])

    load_weight_bf16(kb, wup, t_wup, io["wup"], 8, per=1)
    load_weight_bf16(kb, wdown_a, t_wdown_a, io["wdown"][0:2048, :], 16, per=4)

    NT = 2
    T = NT * 128
    groups = [(g * T, NT, 0) for g in range(OWN_LAT // T)]
    if not last:
        groups.append((OWN_LAT, 1, 1))
    mb = ModBufs(kb, pre + "mb", NT)
    xring = Ring(kb, pre + "x", [128, NT, D], F32, 2)
    yring = Ring(kb, pre + "y", [128, 8, T], BF16, 2)
    x2ring = Ring(kb, pre + "xm2", [128, 8, T], BF16, 2)
    if "hsel" in io:
        ycring = Ring(kb, pre + "yc", [128, 2, 6, T], BF16, 1)
        hs = kb.sbuf(pre + "hs", [128, 2], F32); t_hs = Tk(pre + "hs")
        kb.dma(hs[:], io["hsel"].to_broadcast([128, 2]), writes=[t_hs])
    gel = kb.sbuf(pre + "gel", [128, 512], F32); t_gel = Tk(pre + "gel")
    vn = kb.sbuf(pre + "vn", [128, 256], F32); t_vn = Tk(pre + "vn")
    vln = kb.sbuf(pre + "vln", [128, 256], BF16); t_vln = Tk(pre + "vln")
    mixb = kb.sbuf(pre + "mixb", [128, 256], F32); t_mixb = Tk(pre + "mixb")
    ya = kb.sbuf(pre + "ya", [128, 256], BF16); t_ya = Tk(pre + "ya")
    tmp = kb.sbuf(pre + "tmp", [128, D], F32); t_tmp = Tk(pre + "tmp")
    bnst = kb.sbuf(pre + "bnst", [128, 2, 6], F32); t_bnst = Tk(pre + "bnst")
    mv = kb.sbuf(pre + "mv", [128, 1, 2], F32); t_mv = Tk(pre + "mv")
    rstd = kb.sbuf(pre + "rstd", [128, 1], F32); t_rstd = Tk(pre + "rstd")

    ringA = PsRing(banks, [0, 1])
    ringBr = PsRing(banks, [2, 3, 4, 5])
    ringT = PsRing(banks, [6, 7])

    cur_gate = [0]
    pend = None
    for gi, (tok0, nt, which) in enumerate(groups):
        xt, t_xt = xring.next()
        kb.dma(xt[:, 0:nt, :], x_in[tok0:tok0 + nt * 128, :].rearrange("(t p) d -> p t d", p=128),
               reads=[io.get("t_x_in")], writes=[t_xt])
        xm, t_xm = xm_slots[gi % 2], t_xm_slots[gi % 2]
        kb.dma(xm[:, :, 0:nt * 128], xmT_own(tok0, nt * 128).rearrange("(c p) t -> p c t", p=128),
               reads=[io.get("t_xmT_own")], writes=[t_xm])
        yT, t_yT = yring.next()
        cands = yM(tok0, nt * 128)
        if len(cands) == 1:
            kb.dma(yT[:, 2:8, 0:nt * 128], cands[0].rearrange("(c p) t -> p c t", p=128),
                   reads=[io.get("t_yM_in")], writes=[t_yT])
        else:
            yc, t_yc = ycring.next()
            first = True
            for k in range(2):
                for part in range(2):
                    (kb.dma if first else kb.dma_acc)(yc[:, k, 3 * part:3 * part + 3, 0:nt * 128],
                                                      cands[k][part].rearrange("(c p) t -> p c t", p=128),
                                                      reads=[io.get("t_yM_in")], writes=[t_yc])
                    first = False
            kb.op("dve", "tensor_scalar", yT[:, 2:8, 0:nt * 128], yc[:, 0, :, 0:nt * 128], hs[:, 0:1], None, op0=ALU.mult,
                  reads=[t_yc, t_hs], writes=[t_yT])
            kb.op("dve", "scalar_tensor_tensor", yT[:, 2:8, 0:nt * 128], yc[:, 1, :, 0:nt * 128], hs[:, 1:2],
                  yT[:, 2:8, 0:nt * 128], op0=ALU.mult, op1=ALU.add, reads=[t_yc, t_hs, t_yT], writes=[t_yT])
        if which != cur_gate[0]:
            load_gates(which, 0)
            cur_gate[0] = which
        cols, t_cols = (colsL, t_colsL) if which == 0 else (colsC, t_colsC)
        for t in range(nt):
            pA, t_pA = ringA.next()
            for kc in range(8):
                kb.op("pe", "matmul",
                    pA[:, :], xm[:, kc, t * 128:(t + 1) * 128], wA[:, kc, :], start=(kc == 0), stop=(kc == 7),
                    reads=[t_xm, t_wA], writes=[t_pA])
            kb.op("act", "activation", gel[:], pA[:, :], AF.Gelu, reads=[t_pA], writes=[t_gel])
            emit_stats(kb, gel[:, 256:512], t_gel, 256, bnst, t_bnst, mv[:, 0, :], t_mv)
            emit_rstd(kb, cm, mv, t_mv, 1, rstd, t_rstd)
            kb.op("dve", "tensor_scalar", vn[:], gel[:, 256:512], mv[:, 0, 0:1], rstd[:, 0:1],
                                                   op0=ALU.subtract, op1=ALU.mult,
                  reads=[t_gel, t_mv, t_rstd], writes=[t_vn])
            kb.op("dve", "tensor_tensor", vn[:], vn[:], rowsA[:, 0, :], op=ALU.mult,
                  reads=[t_vn, t_rowsA], writes=[t_vn])
            kb.op("dve", "tensor_tensor", vln[:], vn[:], rowsA[:, 1, :], op=ALU.add,
                  reads=[t_vn, t_rowsA], writes=[t_vln])
            mx, t_mx = ringA.next()
            for g in range(4):
                kb.op("pe", "matmul",
                    mx[:, g * 64:(g + 1) * 64], wsT[:, g, :], vln[:, g * 64:(g + 1) * 64], start=True, stop=True,
                    reads=[t_wsT, t_vln], writes=[t_mx])
            kb.op("dve", "tensor_tensor",
                mixb[:].rearrange("p (g d) -> p g d", g=4), mx[:, 0:256].rearrange("p (g d) -> p g d", g=4),
                bsT[:].unsqueeze(2).to_broadcast([128, 4, 64]), op=ALU.add,
                reads=[t_mx, t_bsT], writes=[t_mixb])
            kb.op("dve", "tensor_tensor", ya[:], mixb[:], gel[:, 0:256], op=ALU.mult,
                  reads=[t_mixb, t_gel], writes=[t_ya])
            pT, t_pT = ringT.next()
            pT16 = pT[:].bitcast(BF16)
            for c in range(2):
                kb.op("pe", "transpose",
                    pT16[:, c * 128:(c + 1) * 128], ya[:, c * 128:(c + 1) * 128], cm.identb[:],
                    reads=[t_ya, cm.t_identb], writes=[t_pT])
            kb.op("act", "copy",
                yT[:, 0:2, t * 128:(t + 1) * 128], pT16[:, 0:256].rearrange("p (c t) -> p c t", c=2),
                reads=[t_pT], writes=[t_yT])
            brs = []
            for hf in range(2):
                br, t_br = ringBr.next()
                for kc in range(8):
                    kb.op("pe", "matmul",
                        br[:, :], yT[:, kc, t * 128:(t + 1) * 128], wout[:, kc, hf * 512:(hf + 1) * 512],
                        start=(kc == 0), stop=(kc == 7),
                        reads=[t_yT, t_wout], writes=[t_br])
                brs.append((br[:, :], t_br))
            emit_deepnorm(kb, cm, xt[:, t, :], t_xt, brs, rows[:, 0, :], t_rows[0], rows[:, 1, :], rows[:, 2, :],
                          t_rowsLN, tmp, t_tmp, bnst, t_bnst, mv, t_mv, rstd, t_rstd)
        kb.dma_acc(x1s[tok0:tok0 + nt * 128, :].rearrange("(t p) d -> p t d", p=128), xt[:, 0:nt, :],
                   reads=[t_xt], writes=[t_x1s], owner=t_xt)
        x2, t_x2 = x2ring.next()
        emit_modulate(kb, cm, mb, [(xt[:, t, :], t_xt) for t in range(nt)], nt, cols, t_cols, 4, 3,
                      x2, t_x2, ringT)
        kb.dma_acc(xm2Ts[:, tok0:tok0 + nt * 128].rearrange("(c p) t -> p c t", p=128), x2[:, :, 0:nt * 128],
                   reads=[t_x2], writes=[t_xm2Ts], owner=t_x2)

    hring = Ring(kb, pre + "h", [128, T], BF16, 4)
    sqring = Ring(kb, pre + "sq", [128, T], F32, 2)
    oring = yring
    ringUp = PsRing(banks, [4, 5, 6])
    ringT2 = PsRing(banks, [7])
    cur_gate[0] = -1
    load_ln(1)
    first = True
    for c0 in range(0, 16, 4):
        f = kb.dma if first else kb.dma_acc
        f(wdown_b[:, c0:c0 + 4, :], io["wdown"][(16 + c0) * 128:(20 + c0) * 128, :].rearrange("(c p) n -> p c n", p=128),
          writes=[t_wdown_b] + ([t_wA, t_wout] + t_xm_slots if first else []), queue="pool")
        first = False
    for gi, (tok0, nt, which) in enumerate(groups):
        n = nt * 128
        xt, t_xt = xring.next()
        kb.dma(xt[:, 0:nt, :], x1s[tok0:tok0 + n, :].rearrange("(t p) d -> p t d", p=128),
               reads=[t_x1s], writes=[t_xt])
        x2, t_x2 = x2ring.next()
        kb.dma(x2[:, :, 0:n], xm2Ts[:, tok0:tok0 + n].rearrange("(c p) t -> p c t", p=128),
               reads=[t_xm2Ts], writes=[t_x2])
        if which != cur_gate[0]:
            load_gates(which, 1)
            cur_gate[0] = which
        cols, t_cols = (colsL, t_colsL) if which == 0 else (colsC, t_colsC)
        acc = [[banks[2 * t + hf] for hf in range(2)] for t in range(nt)]

        def up(j):
            ps, t_ps = ringUp.next()
            for kc in range(8):
                kb.op("pe", "matmul",
                    ps[:, 0:n], wup[:, kc, j * 128:(j + 1) * 128], x2[:, kc, 0:n], start=(kc == 0), stop=(kc == 7),
                    reads=[t_wup, t_x2], writes=[t_ps])
            sq, t_sq = sqring.next()
            kb.op("act", "activation", sq[:, 0:n], ps[:, 0:n], AF.Square,
                  reads=[t_ps], writes=[t_sq])
            hT, t_hT = hring.next()
            kb.op("dve", "scalar_tensor_tensor",
                hT[:, 0:n], ps[:, 0:n], 0.0, sq[:, 0:n], op0=ALU.is_gt, op1=ALU.mult,
                reads=[t_ps, t_sq], writes=[t_hT])
            return hT, t_hT

        def down(j, hT, t_hT):
            for t in range(nt):
                for hf in range(2):
                    a, t_a = acc[t][hf]
                    wd, t_wd, jj = (wdown_a, t_wdown_a, j) if j < 16 else (wdown_b, t_wdown_b, j - 16)
                    kb.op("pe", "matmul",
                        a[:, :], hT[:, t * 128:(t + 1) * 128], wd[:, jj, hf * 512:(hf + 1) * 512],
                        start=(j == 0), stop=(j == 31),
                        reads=[t_hT, t_wd], writes=[t_a])
        q = [up(0), up(1)]
        for j in range(32):
            if j + 2 < 32:
                q.append(up(j + 2))
            down(j, *q.pop(0))
        for t in range(nt):
            brs = [(acc[t][hf][0][:, :], acc[t][hf][1]) for hf in range(2)]
            emit_deepnorm(kb, cm, xt[:, t, :], t_xt, brs, rows[:, 0, :], t_rows[0], rows[:, 1, :], rows[:, 2, :],
                          t_rowsLN, tmp, t_tmp, bnst, t_bnst, mv, t_mv, rstd, t_rstd)
        kb.dma_acc(x_out[tok0:tok0 + n, :].rearrange("(t p) d -> p t d", p=128), xt[:, 0:nt, :],
                   reads=[t_xt], writes=[t_xout], owner=t_xt)
        if not last:
            o, t_o = oring.next()
            emit_modulate(kb, cm, mb, [(xt[:, t, :], t_xt) for t in range(nt)], nt, cols, t_cols, 7, 6,
                          o, t_o, ringT2)
            kb.dma_acc(xmT_next[:, tok0:tok0 + n].rearrange("(c p) t -> p c t", p=128), o[:, :, 0:n],
                       reads=[t_o], writes=[t_xmTn], owner=t_o)


HD = 64
NTILE = NTOK // 128
LAT_TILES = SEQ // 128


def rope_tables():
    half = 32
    inv = (1.0 / (10000.0 ** (np.arange(0, half, 2, dtype=np.float32) / np.float32(half)))).astype(np.float32)
    t = np.arange(SEQ)
    ar = (t // 64).astype(np.float32)[:, None] * inv[None, :]
    ac = (t % 64).astype(np.float32)[:, None] * inv[None, :]
    C = np.concatenate([np.cos(ar), np.cos(ar), np.cos(ac), np.cos(ac)], 1)
    S = np.concatenate([-np.sin(ar), np.sin(ar), -np.sin(ac), np.sin(ac)], 1)
    return np.ascontiguousarray(C, np.float32), np.ascontiguousarray(S, np.float32)


class MState:
    pass


def load_xm_block(kb, io, xm, t_xm, tok0, n):
    pieces = io["xmT_pieces"](tok0, n)
    for k, (c0, ncn, o0, nn, ap) in enumerate(pieces):
        (kb.dma if k == 0 else kb.dma_acc)(xm[:, c0:c0 + ncn, o0:o0 + nn], ap.rearrange("(c p) t -> p c t", p=128),
                                           reads=[io.get("t_xmT_full")], writes=[t_xm])


def emit_PM_setup(kb, cm, io, layer):
    st = MState()
    pre = f"m{layer}_"
    st.pre = pre
    st.qT = kb.sbuf(pre + "qT", [64, 3, NTOK], BF16)
    st.t_qT = [Tk(pre + f"qT{b}") for b in range(9)]
    st.kT = kb.sbuf(pre + "kT", [64, NTOK], BF16)
    st.t_kT = [Tk(pre + f"kT{b}") for b in range(9)]
    st.vaug = kb.sbuf(pre + "vaug", [128, NTILE, 65], BF16)
    st.t_vaug = [Tk(pre + f"va{m}") for m in range(NTILE)]
    st.t_vones = Tk(pre + "vones")
    kb.op("pool", "memset", st.vaug[:, :, 64:65], 1.0, writes=[st.t_vones])
    st.szb = kb.sbuf(pre + "szb", [128, NTILE, 192], BF16)
    st.t_szb = [Tk(pre + f"szb{m}") for m in range(NTILE)]
    st.gb = kb.sbuf(pre + "gb", [128, NTILE, 12], F32)
    st.t_gb = [Tk(pre + f"gb{m}") for m in range(NTILE)]
    st.qkvT = kb.sbuf(pre + "qkvT", [128, 5, NTOK], BF16)
    st.t_qkvT = [Tk(pre + f"qkvT{b}") for b in range(9)]
    st.O = kb.sbuf(pre + "O", [128, NTILE, 192], F32)
    st.t_O = [Tk(pre + f"O{m}") for m in range(NTILE)]
    st.k1z = kb.sbuf(pre + "k1z", [128, NTOK], BF16)
    st.t_k1z0 = Tk(pre + "k1z0")
    kb.op("pool", "memset", st.k1z[0:64, :], 0.0, writes=[st.t_k1z0])
    emit_PM_masks(kb, cm, st)
    return st


def emit_PM_a(kb, cm, io, st, layer, banks):
    pre = st.pre + "a_"
    wB = kb.sbuf(pre + "wB", [128, 8, 320], BF16); t_wB = Tk(pre + "wB")
    wCt = kb.sbuf(pre + "wCt", [128, 8, 204], BF16); t_wCt = Tk(pre + "wCt")
    load_weight_bf16(kb, wB, t_wB, io["wB"], 8, per=8)
    load_weight_bf16(kb, wCt, t_wCt, io["wCt"], 8, per=8)
    grow = kb.sbuf(pre + "grow", [128, 4, 64], F32); t_grow = Tk(pre + "grow")
    for hh in range(4):
        (kb.dma if hh == 0 else kb.dma_acc)(grow[:, hh, :], io["attn_q_g" if hh < 3 else "attn_k_g"].to_broadcast([128, 64]),
                                             writes=[t_grow])
    brow = kb.sbuf(pre + "brow", [128, 12], F32); t_brow = Tk(pre + "brow")
    kb.op("pool", "memset", brow[:, 6:12], 0.0, writes=[t_brow])
    kb.dma_acc(brow[:, 0:6], io["dn_dt_bias"].to_broadcast([128, 6]), writes=[t_brow])
    nea = kb.sbuf(pre + "nea", [128, 6], F32); t_nea = Tk(pre + "nea")
    kb.dma(nea[:], io["dn_a_log"].to_broadcast([128, 6]), writes=[t_nea])
    kb.op("act", "activation", nea[:], nea[:], AF.Exp, reads=[t_nea], writes=[t_nea])
    kb.op("dve", "tensor_scalar", nea[:], nea[:], -1.0, None, op0=ALU.mult, reads=[t_nea], writes=[t_nea])

    xring = Ring(kb, pre + "xm", [128, 8, 512], BF16, 2)
    cring = Ring(kb, pre + "ctab", [128, 4, 64], F32, 2)
    sring = Ring(kb, pre + "stab", [128, 4, 64], F32, 2)
    sq = kb.sbuf(pre + "sq", [128, 256], F32); t_sq = Tk(pre + "sq")
    ssq = kb.sbuf(pre + "ssq", [128, 4], F32); t_ssq = Tk(pre + "ssq")
    qn = kb.sbuf(pre + "qn", [128, 256], F32); t_qn = Tk(pre + "qn")
    t1 = kb.sbuf(pre + "t1", [128, 256], F32); t_t1 = Tk(pre + "t1")
    t2 = kb.sbuf(pre + "t2", [128, 256], F32); t_t2 = Tk(pre + "t2")
    qrr = Ring(kb, pre + "qr", [128, 256], F32, 2)
    ez = kb.sbuf(pre + "ez", [128, 192], F32); t_ez = Tk(pre + "ez")
    ab = kb.sbuf(pre + "ab", [128, 12], F32); t_ab = Tk(pre + "ab")
    e1 = kb.sbuf(pre + "e1", [128, 12], F32); t_e1 = Tk(pre + "e1")
    ringB = PsRing(banks, [0, 1])
    ringC = PsRing(banks, [2, 3])
    tp = [banks[4 + hh] for hh in range(4)]
    flip = 0
    blocks = [(b * 512, 4) for b in range(8)] + [(SEQ, 2)]
    for bi, (tok0, nt) in enumerate(blocks):
        n = nt * 128
        xm, t_xm = xring.next()
        load_xm_block(kb, io, xm, t_xm, tok0, n)
        lat = tok0 < SEQ
        if lat:
            ct, t_ct = cring.next()
            stb, t_stb = sring.next()
            kb.dma(ct[:], io["ropeC"][tok0:tok0 + 512, :].rearrange("(t p) f -> p t f", p=128), writes=[t_ct])
            kb.dma(stb[:], io["ropeS"][tok0:tok0 + 512, :].rearrange("(t p) f -> p t f", p=128), writes=[t_stb])
        for t in range(nt):
            m = tok0 // 128 + t
            pB, t_pB = ringB.next()
            pC, t_pC = ringC.next()
            for kc in range(8):
                kb.op("pe", "matmul", pB[:, 0:320], xm[:, kc, t * 128:(t + 1) * 128], wB[:, kc, :],
                      start=(kc == 0), stop=(kc == 7), reads=[t_xm, t_wB], writes=[t_pB])
            for kc in range(8):
                kb.op("pe", "matmul", pC[:, 0:204], xm[:, kc, t * 128:(t + 1) * 128], wCt[:, kc, :],
                      start=(kc == 0), stop=(kc == 7), reads=[t_xm, t_wCt], writes=[t_pC])
            kb.op("act", "activation", sq[:], pB[:, 0:256], AF.Square, reads=[t_pB], writes=[t_sq])
            kb.op("dve", "tensor_reduce", ssq[:], sq[:].rearrange("p (h d) -> p h d", h=4), axis=AX.X, op=ALU.add,
                  reads=[t_sq], writes=[t_ssq])
            kb.op("dve", "tensor_scalar", ssq[:], ssq[:], 1.0 / 64.0, EPS, op0=ALU.mult, op1=ALU.add,
                  reads=[t_ssq], writes=[t_ssq])
            kb.op("pool", "tensor_tensor", ssq[:], ssq[:], cm.mhalf[:, 0:4], op=ALU.pow,
                  reads=[t_ssq, cm.t_mhalf], writes=[t_ssq])
            qn3 = qn[:].rearrange("p (h d) -> p h d", h=4)
            kb.op("dve", "tensor_tensor", qn3, pB[:, 0:256].rearrange("p (h d) -> p h d", h=4),
                  ssq[:].unsqueeze(2).to_broadcast([128, 4, 64]), op=ALU.mult, reads=[t_pB, t_ssq], writes=[t_qn])
            qr, t_qr = qrr.next()
            if lat:
                kb.op("dve", "tensor_tensor", qn3, qn3, grow[:], op=ALU.mult, reads=[t_qn, t_grow], writes=[t_qn])
                kb.op("dve", "tensor_tensor", t1[:].rearrange("p (h d) -> p h d", h=4), qn3,
                      ct[:, t, :].unsqueeze(1).to_broadcast([128, 4, 64]), op=ALU.mult,
                      reads=[t_qn, t_ct], writes=[t_t1])
                qn5 = qn[:].rearrange("p (h r a f) -> p h r a f", h=4, r=2, a=2)
                t25 = t2[:].rearrange("p (h r a f) -> p h r a f", h=4, r=2, a=2)
                st4 = stb[:, t, :].rearrange("p (r a f) -> p r a f", r=2, a=2)
                for a in range(2):
                    kb.op("pool", "tensor_tensor", t25[:, :, :, a, :], qn5[:, :, :, 1 - a, :],
                          st4[:, :, a, :].unsqueeze(1).to_broadcast([128, 4, 2, 16]), op=ALU.mult,
                          reads=[t_qn, t_stb], writes=[t_t2])
                kb.op("dve", "tensor_tensor", qr[:], t1[:], t2[:], op=ALU.add, reads=[t_t1, t_t2], writes=[t_qr])
            else:
                kb.op("dve", "tensor_tensor", qr[:].rearrange("p (h d) -> p h d", h=4), qn3, grow[:], op=ALU.mult,
                      reads=[t_qn, t_grow], writes=[t_qr])
            for hh in range(4):
                kb.op("pe", "transpose", tp[hh][0][0:64, t * 128:(t + 1) * 128], qr[:, hh * 64:(hh + 1) * 64],
                      cm.ident[:], reads=[t_qr, cm.t_ident], writes=[tp[hh][1]])
            kb.op("act", "copy", st.vaug[:, m, 0:64], pB[:, 256:320], reads=[t_pB, st.t_vones], writes=[st.t_vaug[m]])
            kb.op("act", "activation", ez[:], pC[:, 0:192], AF.Exp, scale=-1.0, reads=[t_pC], writes=[t_ez])
            kb.op("dve", "tensor_scalar", ez[:], ez[:], 1.0, None, op0=ALU.add, reads=[t_ez], writes=[t_ez])
            kb.op("dve", "reciprocal", ez[:], ez[:], reads=[t_ez], writes=[t_ez])
            kb.op("dve", "tensor_tensor", st.szb[:, m, :], pC[:, 0:192], ez[:], op=ALU.mult,
                  reads=[t_pC, t_ez], writes=[st.t_szb[m]])
            kb.op("dve", "tensor_tensor", ab[:], pC[:, 192:204], brow[:], op=ALU.add, reads=[t_pC, t_brow], writes=[t_ab])
            kb.op("act", "activation", e1[:, 0:6], ab[:, 0:6], AF.Exp, reads=[t_ab], writes=[t_e1])
            kb.op("act", "activation", e1[:, 0:6], e1[:, 0:6], AF.Ln, bias=1.0, reads=[t_e1], writes=[t_e1])
            kb.op("act", "activation", e1[:, 6:12], ab[:, 6:12], AF.Exp, scale=-1.0, reads=[t_ab, t_e1], writes=[t_e1])
            kb.op("dve", "tensor_tensor", st.gb[:, m, 0:6], e1[:, 0:6], nea[:], op=ALU.mult,
                  reads=[t_e1, t_nea], writes=[st.t_gb[m]])
            kb.op("dve", "tensor_scalar", e1[:, 6:12], e1[:, 6:12], 1.0, None, op0=ALU.add, reads=[t_e1], writes=[t_e1])
            kb.op("dve", "reciprocal", st.gb[:, m, 6:12], e1[:, 6:12], reads=[t_e1], writes=[st.t_gb[m]])
        for hh in range(4):
            dst = st.qT[:, hh, tok0:tok0 + n] if hh < 3 else st.kT[:, tok0:tok0 + n]
            t_dst = st.t_qT[bi] if hh < 3 else st.t_kT[bi]
            flip ^= 1
            if flip:
                kb.op("act", "copy", dst, tp[hh][0][0:64, 0:n], reads=[tp[hh][1]], writes=[t_dst])
            else:
                kb.op("dve", "tensor_copy", dst, tp[hh][0][0:64, 0:n], reads=[tp[hh][1]], writes=[t_dst])


def emit_PM_attn(kb, cm, io, st, layer, last, banks):
    pre = st.pre + "e_"
    yM = io["yM"]
    t_yM = io["t_yM"]
    pring = Ring(kb, pre + "pT", [128, 512], BF16, 4)
    rec = kb.sbuf(pre + "rec", [65, 512], F32); t_rec = Tk(pre + "rec")
    ones = kb.sbuf(pre + "ones", [65, 64], F32); t_ones = Tk(pre + "ones")
    kb.op("pool", "memset", ones[:], 1.0, writes=[t_ones])
    bcs = kb.sbuf(pre + "bcs", [64, 512], F32); t_bcs = Tk(pre + "bcs")
    oring = Ring(kb, pre + "o", [64, 512], BF16, 2)
    ringS = PsRing(banks, [0, 1, 2])
    ringO = PsRing(banks, [3, 4])
    bc_ps, t_bc = banks[5]
    jobs = []
    for j in range(3):
        for qb in range(8):
            jobs.append((j, qb * 512, 512, list(range(NTILE))))
        if not last:
            jobs.append((j, SEQ, 256, [32, 33]))
    for (j, q0, nq, kcs) in jobs:
        o_ps, t_o = ringO.next()
        qblk = q0 // 512
        t_q = st.t_qT[qblk]

        def score(kc):
            s_ps, t_s = ringS.next()
            kb.op("pe", "matmul", s_ps[:, 0:nq], st.kT[:, kc * 128:(kc + 1) * 128], st.qT[:, j, q0:q0 + nq],
                  start=True, stop=True, reads=[st.t_kT[kc // 4], t_q], writes=[t_s])
            pT, t_pT = pring.next()
            kb.op("act", "activation", pT[:, 0:nq], s_ps[:, 0:nq], AF.Exp, scale=0.125, reads=[t_s], writes=[t_pT])
            return pT, t_pT
        cur = score(kcs[0])
        for i, kc in enumerate(kcs):
            nxt = score(kcs[i + 1]) if i + 1 < len(kcs) else None
            pT, t_pT = cur
            kb.op("pe", "matmul", o_ps[0:65, 0:nq], st.vaug[:, kc, :], pT[:, 0:nq],
                  start=(i == 0), stop=(i == len(kcs) - 1), reads=[st.t_vaug[kc], t_pT], writes=[t_o])
            cur = nxt
        kb.op("dve", "reciprocal", rec[64:65, 0:nq], o_ps[64:65, 0:nq], reads=[t_o], writes=[t_rec])
        kb.op("pe", "matmul", bc_ps[0:64, 0:nq], ones[64:65, :], rec[64:65, 0:nq], start=True, stop=True,
              reads=[t_ones, t_rec], writes=[t_bc])
        kb.op("act", "copy", bcs[:, 0:nq], bc_ps[0:64, 0:nq], reads=[t_bc], writes=[t_bcs])
        ob, t_ob = oring.next()
        kb.op("dve", "tensor_tensor", ob[:, 0:nq], o_ps[0:64, 0:nq], bcs[:, 0:nq], op=ALU.mult,
              reads=[t_o, t_bcs], writes=[t_ob])
        kb.dma_acc(yM[j * 64:(j + 1) * 64, q0:q0 + nq], ob[:, 0:nq], reads=[t_ob], writes=[t_yM], owner=t_ob)


BIG = 30000.0
NSTAGES = [100]
GRAM = [3]
GJ = [0, 1, 2]


def fence(dst_tks, src_tks):
    ev = {}
    for t in src_tks:
        for d in (t.w, t.r):
            for s, v in d.items():
                if ev.get(s, 0) < v:
                    ev[s] = v
    for t in dst_tks:
        for s, v in ev.items():
            if t.r.get(s, 0) < v:
                t.r[s] = v


def emit_PM_masks(kb, cm, st):
    pre = st.pre + "k_"
    names = ["Blk", "U", "UT", "N1_0", "N1_1", "M2_0", "M2_1", "onesf"]
    st.mk = kb.sbuf(pre + "masks", [128, len(names), 128], F32)
    st.t_mk = Tk(pre + "masks")
    mk = st.mk
    ix = {n: k for k, n in enumerate(names)}
    st.mix = ix
    t = st.t_mk
    kb.op("pool", "memset", mk[:, ix["Blk"], :], 0.0, writes=[t])
    kb.op("pool", "memset", mk[0:64, ix["Blk"], 0:64], 1.0, reads=[t], writes=[t])
    kb.op("pool", "memset", mk[64:128, ix["Blk"], 64:128], 1.0, reads=[t], writes=[t])
    kb.op("pool", "memset", mk[:, ix["onesf"], :], 1.0, reads=[t], writes=[t])
    blk = mk[:, ix["Blk"], :]
    kb.op("pool", "affine_select", mk[:, ix["U"], :], blk, pattern=[[1, 128]], compare_op=ALU.is_ge, fill=0.0,
          base=0, channel_multiplier=-1, reads=[t], writes=[t])
    kb.op("pool", "affine_select", mk[:, ix["UT"], :], blk, pattern=[[-1, 128]], compare_op=ALU.is_ge, fill=0.0,
          base=0, channel_multiplier=1, reads=[t], writes=[t])
    kb.op("pool", "affine_select", mk[:, ix["N1_0"], :], blk, pattern=[[-1, 128]], compare_op=ALU.is_gt, fill=0.0,
          base=0, channel_multiplier=1, reads=[t], writes=[t])
    kb.op("pool", "affine_select", mk[:, ix["N1_1"], :], blk, pattern=[[1, 128]], compare_op=ALU.is_gt, fill=0.0,
          base=0, channel_multiplier=-1, reads=[t], writes=[t])
    for n in ("N1_0", "N1_1"):
        kb.op("dve", "tensor_scalar", mk[:, ix[n], :], mk[:, ix[n], :], -1.0, -BIG, op0=ALU.add, op1=ALU.mult,
              reads=[t], writes=[t])
    kb.op("dve", "tensor_scalar", mk[:, ix["M2_0"], :], mk[:, ix["U"], :], -1.0, BIG, op0=ALU.add, op1=ALU.mult,
          reads=[t], writes=[t])
    kb.op("dve", "tensor_scalar", mk[:, ix["M2_1"], :], mk[:, ix["UT"], :], -1.0, BIG, op0=ALU.add, op1=ALU.mult,
          reads=[t], writes=[t])


def emit_PM_b(kb, cm, io, st, layer, banks, xring, raw_slots, t_raw):
    pre = st.pre + "b_"
    wCf = kb.sbuf(pre + "wCf", [128, 8, 640], BF16); t_wCf = Tk(pre + "wCf")
    load_weight_bf16(kb, wCf, t_wCf, io["wCf"], 8, per=8)
    cw = kb.sbuf(pre + "cw", [128, 5, 5], F32); t_cw = Tk(pre + "cw")
    kb.dma(cw[:], io["convw"], writes=[t_cw])
    kb.op("pool", "memset", st.qkvT[0:64, 4, :], 0.0, writes=st.t_qkvT)
    acc = kb.sbuf(pre + "acc", [128, 512], F32); t_acc = Tk(pre + "acc")
    cs = kb.sbuf(pre + "cs", [128, 512], F32); t_cs = Tk(pre + "cs")
    sqv = kb.sbuf(pre + "sqv", [128, 512], F32); t_sqv = Tk(pre + "sqv")
    rn = kb.sbuf(pre + "rn", [128, 512], F32); t_rn = Tk(pre + "rn")
    ex = kb.sbuf(pre + "ex", [128, 512], F32); t_ex = Tk(pre + "ex")
    ringP = PsRing(banks, [0, 1, 2, 3])
    ringN = PsRing(banks, [4, 5])
    blk = st.mk[:, st.mix["Blk"], :]
    blocks = [(b * 512, 512) for b in range(8)] + [(SEQ, 256)]
    flip = [0]

    def inproj(bi):
        tok0, n = blocks[bi]
        raw, t_r = raw_slots[bi % 3], t_raw[bi % 3]
        xm, t_xm = xring.next()
        load_xm_block(kb, io, xm, t_xm, tok0, n)
        for i in range(5):
            ps, t_ps = ringP.next()
            for kc in range(8):
                kb.op("pe", "matmul", ps[:, 0:n], wCf[:, kc, i * 128:(i + 1) * 128], xm[:, kc, 0:n],
                      start=(kc == 0), stop=(kc == 7), reads=[t_wCf, t_xm], writes=[t_ps])
            flip[0] ^= 1
            if flip[0]:
                kb.op("act", "copy", raw[:, i, 2:2 + n], ps[:, 0:n], reads=[t_ps], writes=[t_r])
            else:
                kb.op("dve", "tensor_copy", raw[:, i, 2:2 + n], ps[:, 0:n], reads=[t_ps], writes=[t_r])
        seg_start = tok0 in (0, SEQ)
        if seg_start:
            kb.op("pool", "memset", raw[:, :, 0:2], 0.0, reads=[t_r], writes=[t_r])
        else:
            pr, t_pr = raw_slots[(bi - 1) % 3], t_raw[(bi - 1) % 3]
            npv = blocks[bi - 1][1]
            kb.op("pool", "tensor_copy", raw[:, :, 0:2], pr[:, :, npv:npv + 2], reads=[t_pr, t_r], writes=[t_r])
            kb.op("pool", "tensor_copy", pr[:, :, 2 + npv:4 + npv], raw[:, :, 2:4], reads=[t_r, t_pr], writes=[t_pr])
        seg_end = (tok0 + n) in (SEQ, NTOK)
        if seg_end:
            kb.op("pool", "memset", raw[:, :, 2 + n:4 + n], 0.0, reads=[t_r], writes=[t_r])

    def conv(bi):
        tok0, n = blocks[bi]
        raw, t_r = raw_slots[bi % 3], t_raw[bi % 3]
        t_out = st.t_qkvT[bi]
        for i in range(5):
            kb.op("dve", "tensor_scalar", acc[:, 0:n], raw[:, i, 0:n], cw[:, i, 0:1], None, op0=ALU.mult,
                  reads=[t_r, t_cw], writes=[t_acc])
            for j in range(1, 5):
                kb.op("dve", "scalar_tensor_tensor", acc[:, 0:n], raw[:, i, j:j + n], cw[:, i, j:j + 1], acc[:, 0:n],
                      op0=ALU.mult, op1=ALU.add, reads=[t_r, t_cw, t_acc], writes=[t_acc])
            kb.op("act", "activation", ex[:, 0:n], acc[:, 0:n], AF.Exp, scale=-1.0, reads=[t_acc], writes=[t_ex])
            kb.op("dve", "tensor_scalar", ex[:, 0:n], ex[:, 0:n], 1.0, None, op0=ALU.add, reads=[t_ex], writes=[t_ex])
            kb.op("dve", "reciprocal", ex[:, 0:n], ex[:, 0:n], reads=[t_ex], writes=[t_ex])
            kb.op("dve", "tensor_tensor", cs[:, 0:n], acc[:, 0:n], ex[:, 0:n], op=ALU.mult, reads=[t_acc, t_ex], writes=[t_cs])
            nr = 128 if i < 2 else (64 if i < 4 else 0)
            if nr:
                kb.op("act", "activation", sqv[0:nr, 0:n], cs[0:nr, 0:n], AF.Square, reads=[t_cs], writes=[t_sqv])
                ps, t_ps = ringN.next()
                kb.op("pe", "matmul", ps[0:nr, 0:n], blk[0:nr, 0:nr], sqv[0:nr, 0:n], start=True, stop=True,
                      reads=[st.t_mk, t_sqv], writes=[t_ps])
                kb.op("act", "activation", rn[0:nr, 0:n], ps[0:nr, 0:n], AF.Ln, bias=EPS, reads=[t_ps], writes=[t_rn])
                kb.op("act", "activation", rn[0:nr, 0:n], rn[0:nr, 0:n], AF.Exp, scale=-0.5, reads=[t_rn], writes=[t_rn])
                if i in (0, 2):
                    kb.op("dve", "scalar_tensor_tensor", st.qkvT[0:nr, i, tok0:tok0 + n], cs[0:nr, 0:n], 0.125,
                          rn[0:nr, 0:n], op0=ALU.mult, op1=ALU.mult, reads=[t_cs, t_rn], writes=[t_out])
                else:
                    kb.op("dve", "tensor_tensor", st.qkvT[0:nr, i, tok0:tok0 + n], cs[0:nr, 0:n], rn[0:nr, 0:n],
                          op=ALU.mult, reads=[t_cs, t_rn], writes=[t_out])
            if nr < 128:
                kb.op("pool", "tensor_copy", st.qkvT[64:128, i, tok0:tok0 + n], cs[64:128, 0:n],
                      reads=[t_cs], writes=[t_out])
            if i == 1:
                kb.op("pool", "tensor_copy", st.k1z[64:128, tok0:tok0 + n], st.qkvT[64:128, 1, tok0:tok0 + n],
                      reads=[t_out, st.t_k1z0], writes=[t_out])
    inproj(0)
    for bi in range(len(blocks)):
        if bi + 1 < len(blocks):
            inproj(bi + 1)
        conv(bi)


def emit_PM_c(kb, cm, io, st, layer, banks, noscan=False):
    pre = st.pre + "c_"
    mk, ix = st.mk, st.mix
    t_mk = st.t_mk
    M = lambda n: mk[:, ix[n], :]
    S2 = [kb.sbuf(pre + f"S{r}", [128, 3, 64], F32) for r in range(2)]
    t_S2 = [Tk(pre + f"S{r}") for r in range(2)]
    W = 384

    def buf(name, shape=(128, 3, 128), n=1, dt=F32):
        return Ring(kb, pre + name, list(shape), dt, n)
    r_sm = buf("sm", (128, 8, 3), 2)
    r_gsel = buf("gsel", (128, 2, 3), 2)
    r_egl = buf("egl", (128, 2, 3), 4)
    r_gbc = buf("gbc", n=2)
    r_dm = buf("dm", n=2)
    r_dmT = buf("dmT", n=2)
    r_egr = buf("egr", n=2)
    r_X = buf("X", n=4)
    r_XT = buf("XT", n=4)
    r_TT = buf("TT", n=2)
    r_vb = buf("vb", (128, 3, 64), 2, BF16)
    r_kbgA = buf("kbgA", (128, 2, 64), 2, BF16)
    r_kbgB = buf("kbgB", (128, 64), 2, BF16)
    r_TTb = buf("TTb", n=2, dt=BF16)
    r_kdA = buf("kdA", (128, 2, 2, 64), 4, BF16)
    r_kdB = buf("kdB", (128, 2, 64), 4, BF16)
    r_ekdc = buf("ekdc", (128, 2, 3), 2)
    r_u = buf("u", (128, 3, 64), 4)
    r_wT = buf("wT", n=4, dt=BF16)
    r_qdT = buf("qdT", n=4, dt=BF16)
    r_iT = buf("iT", n=4, dt=BF16)
    vnew2 = [kb.sbuf(pre + f"vnew{r}", [128, 3, 64], BF16) for r in range(2)]
    t_vnew2 = [Tk(pre + f"vnew{r}") for r in range(2)]
    Sb2 = [kb.sbuf(pre + f"Sb{r}", [128, 3, 64], BF16) for r in range(2)]
    t_Sb2 = [Tk(pre + f"Sb{r}") for r in range(2)]
    for r in range(2):
        kb.op("pool", "memset", vnew2[r][:], 0.0, writes=[t_vnew2[r]])
        kb.op("pool", "memset", S2[r][:], 0.0, writes=[t_S2[r]])
        kb.op("pool", "memset", Sb2[r][:], 0.0, writes=[t_Sb2[r]])
    for k in range(4):
        kb.op("pool", "memset", r_wT.t[k][:], 0.0, writes=[r_wT.tk[k]])
        kb.op("pool", "memset", r_qdT.t[k][:], 0.0, writes=[r_qdT.tk[k]])
    ringG = PsRing(banks, [4, 5, 6, 7])
    b_ws, t_ws = banks[0]
    b_o, t_o = banks[1]
    b_sn, t_sn = banks[2]
    b_misc, t_misc = banks[3]
    t_gcg = t_eglp = t_tk = t_ups = t_misc
    tk16 = b_misc[:].bitcast(BF16)
    flip = [0]

    def evac(dst, src, reads, writes):
        flip[0] ^= 1
        if flip[0]:
            kb.op("act", "copy", dst, src, reads=reads, writes=writes)
        else:
            kb.op("dve", "tensor_copy", dst, src, reads=reads, writes=writes)

    def qk_views(m):
        c0 = m * 128
        blk_i = min(m // 4, 8)
        tq = st.t_qkvT[blk_i]
        kTj = [st.qkvT[0:64, 1, c0:c0 + 128], st.qkvT[64:128, 1, c0:c0 + 128], st.qkvT[0:64, 3, c0:c0 + 128]]
        qTj = [st.qkvT[0:64, 0, c0:c0 + 128], st.qkvT[64:128, 0, c0:c0 + 128], st.qkvT[0:64, 2, c0:c0 + 128]]
        st_kL = [st.qkvT[0:64, 1, c0:c0 + 128], st.k1z[:, c0:c0 + 128], st.qkvT[0:64, 3, c0:c0 + 128]]
        st_kR = [st.qkvT[0:64, 1, c0:c0 + 128], st.qkvT[:, 1, c0:c0 + 128], st.qkvT[0:64, 3, c0:c0 + 128]]
        st_qR = [st.qkvT[0:64, 0, c0:c0 + 128], st.qkvT[:, 0, c0:c0 + 128], st.qkvT[0:64, 2, c0:c0 + 128]]
        qk_ops[0] = (st_kL, st_kR, st_qR)
        return tq, kTj, qTj

    rows = [(0, 64), (64, 128), (0, 64)]
    qk_ops = [None]

    def prep(m, r):
        tq, kTj, qTj = qk_views(m)
        kL, kR, qR = qk_ops[0]
        c0 = m * 128
        gr = st.gb[:, m, 3 * r:3 * r + 3]
        br = st.gb[:, m, 6 + 3 * r:9 + 3 * r]
        t_gb = st.t_gb[m]
        Tri = M("U") if r == 0 else M("UT")
        N1 = M("N1_0") if r == 0 else M("N1_1")
        M2 = M("M2_0") if r == 0 else M("M2_1")
        sm, t_sm = r_sm.next()
        gcs, ngc, ekd, egc, bgc, nbeta = (sm[:, k, :] for k in range(6))
        gsel, t_gsel = r_gsel.next()
        egl, t_egl = r_egl.next()
        gbc, t_gbc = r_gbc.next()
        dm, t_dm = r_dm.next()
        dmT, t_dmT = r_dmT.next()
        egr, t_egr = r_egr.next()
        TT, t_TT = r_TT.next()
        vb, t_vb = r_vb.next()
        kbgA, t_kbgA = r_kbgA.next()
        kbgB, t_kbgB = r_kbgB.next()
        kdA, t_kdA = r_kdA.next()
        kdB, t_kdB = r_kdB.next()
        ekdc, t_ekdc = r_ekdc.next()
        u, t_u = r_u.next()
        wT, t_wT = r_wT.next()
        qdT, t_qdT = r_qdT.next()
        iT, t_iT = r_iT.next()
        stages = []

        def s_small():
            kb.op("pe", "matmul", b_misc[:, 448:451], Tri, gr, start=True, stop=True, reads=[t_mk, t_gb], writes=[t_gcg])
            kb.op("pe", "matmul", b_misc[:, 451:454], M("Blk"), gr, start=True, stop=True, reads=[t_mk, t_gb], writes=[t_gcg])
            kb.op("dve", "tensor_copy", gcs, b_misc[:, 448:451], reads=[t_gcg], writes=[t_sm])
            kb.op("dve", "tensor_scalar", ngc, gcs, -1.0, None, op0=ALU.mult, reads=[t_sm], writes=[t_sm])
            kb.op("dve", "tensor_tensor", ekd, b_misc[:, 451:454], gcs, op=ALU.subtract, reads=[t_gcg, t_sm], writes=[t_sm])
            kb.op("act", "activation", ekd, ekd, AF.Exp, reads=[t_sm], writes=[t_sm])
            kb.op("act", "activation", egc, gcs, AF.Exp, reads=[t_sm], writes=[t_sm])
            kb.op("dve", "tensor_tensor", bgc, br, egc, op=ALU.mult, reads=[t_gb, t_sm], writes=[t_sm])
            kb.op("dve", "tensor_scalar", nbeta, br, -1.0, None, op0=ALU.mult, reads=[t_gb, t_sm], writes=[t_sm])
            cmask = M("Blk")[:, 0:128:64]
            kb.op("dve", "tensor_tensor", ekdc[:], ekd.unsqueeze(1).to_broadcast([128, 2, 3]),
                  cmask.unsqueeze(2).to_broadcast([128, 2, 3]), op=ALU.mult, reads=[t_sm, t_mk], writes=[t_ekdc])
            kb.op("dve", "tensor_tensor", gsel[:], gr.unsqueeze(1).to_broadcast([128, 2, 3]),
                  cmask.unsqueeze(2).to_broadcast([128, 2, 3]), op=ALU.mult, reads=[t_gb, t_mk], writes=[t_gsel])
            kb.op("pe", "matmul", b_misc[:, 456:462], M("onesf"), gsel[:].rearrange("p c j -> p (c j)"),
                  start=True, stop=True, reads=[t_mk, t_gsel], writes=[t_eglp])
            kb.op("act", "activation", egl[:].rearrange("p c j -> p (c j)"), b_misc[:, 456:462], AF.Exp,
                  reads=[t_eglp], writes=[t_egl])
            kb.op("pool", "tensor_copy", gbc[:], gr.unsqueeze(2).to_broadcast([128, 3, 128]), reads=[t_gb], writes=[t_gbc])
        stages.append(s_small)

        def s_tok():
            for bpos, ti in enumerate((1, 3, 2, 4)):
                kb.op("pe", "transpose", tk16[:, bpos * 128:(bpos + 1) * 128], st.qkvT[:, ti, c0:c0 + 128], cm.identb[:],
                      reads=[tq, cm.t_identb], writes=[t_tk])
            kA = tk16[:, 0:128].rearrange("p (j d) -> p j d", j=2)
            kB = tk16[:, 128:192]
            vv = tk16[:, 128:512].rearrange("p (j x) -> p j x", j=3)[:, :, 64:128]
            kb.op("dve", "tensor_tensor", vb[:], vv, br.unsqueeze(2).to_broadcast([128, 3, 64]), op=ALU.mult,
                  reads=[t_tk, t_gb], writes=[t_vb])
            kb.op("dve", "tensor_tensor", kbgA[:], kA, bgc[:, 0:2].unsqueeze(2).to_broadcast([128, 2, 64]), op=ALU.mult,
                  reads=[t_tk, t_sm], writes=[t_kbgA])
            kb.op("dve", "tensor_scalar", kbgB[:], kB, bgc[:, 2:3], None, op0=ALU.mult, reads=[t_tk, t_sm], writes=[t_kbgB])
            for c in range(2):
                kb.op("dve", "tensor_tensor", kdA[:, c, :, :], kA, ekdc[:, c, 0:2].unsqueeze(2).to_broadcast([128, 2, 64]),
                      op=ALU.mult, reads=[t_tk, t_ekdc], writes=[t_kdA])
                kb.op("dve", "tensor_scalar", kdB[:, c, :], kB, ekdc[:, c, 2:3], None, op0=ALU.mult,
                      reads=[t_tk, t_ekdc], writes=[t_kdB])
        stages.append(s_tok)

        def s_decay():
            P1, t_P1 = ringG.next()
            for j in range(3):
                kb.op("pe", "matmul", P1[:, j * 128:(j + 1) * 128], gbc[:, j, :], Tri, start=True, stop=False,
                      reads=[t_gbc, t_mk], writes=[t_P1])
                kb.op("pe", "matmul", P1[:, j * 128:(j + 1) * 128], cm.ident[:], N1, start=False, stop=True,
                      reads=[cm.t_ident, t_mk], writes=[t_P1])
            for j in range(3):
                kb.op("act", "activation", dm[:, j, :], P1[:, j * 128:(j + 1) * 128], AF.Exp, scale=-1.0,
                      bias=gcs[:, j:j + 1], reads=[t_P1, t_sm], writes=[t_dm])
            P2, t_P2 = ringG.next()
            for j in range(3):
                kb.op("pe", "matmul", P2[:, j * 128:(j + 1) * 128], gbc[:, j, :], Tri, start=True, stop=False,
                      reads=[t_gbc, t_mk], writes=[t_P2])
                kb.op("pe", "matmul", P2[:, j * 128:(j + 1) * 128], cm.ident[:], M2, start=False, stop=True,
                      reads=[cm.t_ident, t_mk], writes=[t_P2])
            for j in range(3):
                kb.op("act", "activation", dmT[:, j, :], P2[:, j * 128:(j + 1) * 128], AF.Exp, scale=1.0,
                      bias=ngc[:, j:j + 1], reads=[t_P2, t_sm], writes=[t_dmT])
            P3, t_P3 = ringG.next()
            for j in range(3):
                kb.op("pe", "matmul", P3[:, j * 128:(j + 1) * 128], gbc[:, j, :], Tri, start=True, stop=True,
                      reads=[t_gbc, t_mk], writes=[t_P3])
            kb.op("act", "activation", egr[:].rearrange("p j i -> p (j i)"), P3[:, 0:W], AF.Exp, reads=[t_P3], writes=[t_egr])
            for j in range(3):
                lo, hi = rows[j]
                kb.op("dve", "tensor_tensor", qdT[lo:hi, j, :], qTj[j], egr[lo:hi, j, :], op=ALU.mult,
                      reads=[tq, t_egr], writes=[t_qdT])
        stages.append(s_decay)

        Xc = [None]
        XTc = [None]

        def s_gram():
            KK, t_KK = ringG.next()
            for j in GJ:
                kb.op("pe", "matmul", KK[:, j * 128:(j + 1) * 128], kL[j], kR[j], start=True, stop=True,
                      reads=[tq], writes=[t_KK])
            X, t_X = r_X.next()
            if GRAM[0] < 1:
                Xc[0] = (X, t_X); XTc[0] = (X, t_X)
                return
            for j in range(3):
                kb.op("dve", "scalar_tensor_tensor", X[:, j, :], KK[:, j * 128:(j + 1) * 128], nbeta[:, j:j + 1], dm[:, j, :],
                      op0=ALU.mult, op1=ALU.mult, reads=[t_KK, t_sm, t_dm], writes=[t_X])
            if GRAM[0] < 2:
                Xc[0] = (X, t_X); XTc[0] = (X, t_X)
                return
            KQ, t_KQ = ringG.next()
            for j in range(3):
                kb.op("pe", "matmul", KQ[:, j * 128:(j + 1) * 128], kL[j], qR[j], start=True, stop=True,
                      reads=[tq], writes=[t_KQ])
            kb.op("dve", "tensor_tensor", iT[:].rearrange("p j i -> p (j i)"), KQ[:, 0:W], dmT[:].rearrange("p j i -> p (j i)"),
                  op=ALU.mult, reads=[t_KQ, t_dmT], writes=[t_iT])
            if GRAM[0] < 3:
                Xc[0] = (X, t_X); XTc[0] = (X, t_X)
                return
            P, t_P = ringG.next()
            for j in range(3):
                kb.op("pe", "transpose", P[:, j * 128:(j + 1) * 128], X[:, j, :], cm.ident[:], reads=[t_X, cm.t_ident], writes=[t_P])
            XT, t_XT = r_XT.next()
            evac(XT[:].rearrange("p j i -> p (j i)"), P[:, 0:W], [t_P], [t_XT])
            kb.op("dve", "tensor_tensor", TT[:], XT[:], cm.ident[:].unsqueeze(1).to_broadcast([128, 3, 128]), op=ALU.add,
                  reads=[t_XT, cm.t_ident], writes=[t_TT])
            Xc[0] = (X, t_X)
            XTc[0] = (XT, t_XT)
        stages.append(s_gram)

        def mk_level(k):
            def s_level():
                X, t_X = Xc[0]
                XT, t_XT = XTc[0]
                P, t_P = ringG.next()
                for j in range(3):
                    kb.op("pe", "matmul", P[:, j * 128:(j + 1) * 128], XT[:, j, :], X[:, j, :], start=True, stop=True,
                          reads=[t_X, t_XT], writes=[t_P])
                Xn, t_Xn = r_X.next()
                evac(Xn[:].rearrange("p j i -> p (j i)"), P[:, 0:W], [t_P], [t_Xn])
                if k < 5:
                    Q, t_Q = ringG.next()
                    for j in range(3):
                        kb.op("pe", "matmul", Q[:, j * 128:(j + 1) * 128], X[:, j, :], XT[:, j, :], start=True, stop=True,
                              reads=[t_X, t_XT], writes=[t_Q])
                    XTn, t_XTn = r_XT.next()
                    evac(XTn[:].rearrange("p j i -> p (j i)"), Q[:, 0:W], [t_Q], [t_XTn])
                    XTc[0] = (XTn, t_XTn)
                R, t_R = ringG.next()
                for j in range(3):
                    kb.op("pe", "matmul", R[:, j * 128:(j + 1) * 128], Xn[:, j, :], TT[:, j, :], start=True, stop=True,
                          reads=[t_Xn, t_TT], writes=[t_R])
                kb.op("dve", "tensor_tensor", TT[:].rearrange("p j i -> p (j i)"), TT[:].rearrange("p j i -> p (j i)"),
                      R[:, 0:W], op=ALU.add, reads=[t_TT, t_R], writes=[t_TT])
                Xc[0] = (Xn, t_Xn)
            return s_level
        for k in range(1, 6):
            stages.append(mk_level(k))

        def s_final():
            TTf, t_TTf = TT, t_TT
            TTb, t_TTb = r_TTb.next()
            kb.op("dve", "tensor_copy", TTb[:], TTf[:], reads=[t_TTf], writes=[t_TTb])
            for j in range(3):
                kb.op("pe", "matmul", b_misc[:, 256 + j * 64:256 + (j + 1) * 64], TTb[:, j, :], vb[:, j, :], start=True, stop=True,
                      reads=[t_TTb, t_vb], writes=[t_ups])
            kb.op("act", "copy", u[:].rearrange("p j d -> p (j d)"), b_misc[:, 256:448], reads=[t_ups], writes=[t_u])
            P, t_P = ringG.next()
            for j in range(2):
                kb.op("pe", "matmul", P[:, j * 128:(j + 1) * 128], kbgA[:].rearrange("p j d -> p (j d)"), TTb[:, j, :],
                      start=True, stop=True, reads=[t_kbgA, t_TTb], writes=[t_P])
            kb.op("pe", "matmul", P[0:64, 256:384], kbgB[:], TTb[:, 2, :], start=True, stop=True,
                  reads=[t_kbgB, t_TTb], writes=[t_P])
            kb.op("dve", "tensor_copy", wT[0:64, 0:3:2, :], P[0:64, 0:W].rearrange("p (j i) -> p j i", j=3)[:, 0:3:2, :],
                  reads=[t_P], writes=[t_wT])
            kb.op("act", "copy", wT[64:128, 1, :], P[64:128, 128:256], reads=[t_P], writes=[t_wT])
        stages.append(s_final)
        sc = dict(m=m, r=r, u=(u, t_u), wT=(wT, t_wT), qdT=(qdT, t_qdT), iT=(iT, t_iT), kdA=(kdA, t_kdA),
                  kdB=(kdB, t_kdB), egl=(egl, t_egl))
        return stages, sc

    def scan_stages(sc):
        m, r = sc["m"], sc["r"]
        u, t_u = sc["u"]; wT, t_wT = sc["wT"]; qdT, t_qdT = sc["qdT"]; iT, t_iT = sc["iT"]
        kdA, t_kdA = sc["kdA"]; kdB, t_kdB = sc["kdB"]; egl, t_egl = sc["egl"]
        t_Om = st.t_O[m]
        S, t_S = S2[r], t_S2[r]
        Sb, t_Sb = Sb2[r], t_Sb2[r]
        vnew, t_vnew = vnew2[r], t_vnew2[r]
        o_first = first_dir[m] == r
        stages = []
        for c in ((0, 1) if r == 0 else (1, 0)):
            p0, p1 = 64 * c, 64 * c + 64

            def s1(c=c, p0=p0, p1=p1):
                for j in range(3):
                    lo, hi = rows[j]
                    kb.op("pe", "matmul", b_ws[:, j * 64:(j + 1) * 64], wT[:, j, :], Sb[:, j, :], start=True, stop=True,
                          reads=[t_wT, t_Sb], writes=[t_ws])
                kb.op("dve", "tensor_tensor", vnew[p0:p1].rearrange("p j d -> p (j d)"), u[p0:p1].rearrange("p j d -> p (j d)"),
                      b_ws[p0:p1, 0:192], op=ALU.subtract, reads=[t_u, t_ws], writes=[t_vnew])

            def s2(c=c, p0=p0, p1=p1):
                for j in range(3):
                    lo, hi = rows[j]
                    kb.op("pe", "matmul", b_o[:, j * 64:(j + 1) * 64], qdT[:, j, :], Sb[:, j, :],
                          start=True, stop=False, reads=[t_qdT, t_Sb], writes=[t_o])
                    kb.op("pe", "matmul", b_o[:, j * 64:(j + 1) * 64], iT[:, j, :], vnew[:, j, :],
                          start=False, stop=True, reads=[t_iT, t_vnew], writes=[t_o])
                for j in range(2):
                    kb.op("pe", "matmul", b_sn[:, j * 64:(j + 1) * 64], kdA[:, c, :, :].rearrange("p j d -> p (j d)"),
                          vnew[:, j, :], start=True, stop=True, reads=[t_kdA, t_vnew], writes=[t_sn])
                kb.op("pe", "matmul", b_sn[0:64, 128:192], kdB[:, c, :], vnew[:, 2, :], start=True, stop=True,
                      reads=[t_kdB, t_vnew], writes=[t_sn])
                for j in range(3):
                    lo, hi = rows[j]
                    kb.op("dve", "scalar_tensor_tensor", S[lo:hi, j, :], S[lo:hi, j, :], egl[lo:hi, c, j:j + 1],
                          b_sn[lo:hi, j * 64:(j + 1) * 64], op0=ALU.mult, op1=ALU.add,
                          reads=[t_S, t_egl, t_sn], writes=[t_S])
                    kb.op("pool", "tensor_copy", Sb[lo:hi, j, :], S[lo:hi, j, :], reads=[t_S], writes=[t_Sb])
                if o_first:
                    kb.op("act", "copy", st.O[p0:p1, m, :], b_o[p0:p1, 0:192], reads=[t_o], writes=[t_Om])
                else:
                    kb.op("dve", "tensor_tensor", st.O[p0:p1, m, :], st.O[p0:p1, m, :], b_o[p0:p1, 0:192], op=ALU.add,
                          reads=[t_o, t_Om], writes=[t_Om])
            stages.append(s1)
            stages.append(s2)
        return stages

    orders = [[32, 33] + list(range(32)), [33, 32] + list(range(31, -1, -1))]
    first_dir = {}
    for m in range(NTILE):
        first_dir[m] = 0 if orders[0].index(m) < orders[1].index(m) else 1
    pending = [[], []]
    for k in range(NTILE):
        preps = [prep(orders[r][k], r) for r in range(2)]
        nst = len(preps[0][0])
        pk = [0, 0]
        for sidx in range(nst):
            if sidx >= NSTAGES[0]:
                break
            for r in range(2):
                preps[r][0][sidx]()
            if sidx >= 1 and (sidx - 1) % 2 == 0:
                for r in range(2):
                    if pk[r] < len(pending[r]):
                        pending[r][pk[r]]()
                        pk[r] += 1
        for r in range(2):
            while pk[r] < len(pending[r]):
                pending[r][pk[r]]()
                pk[r] += 1
            pending[r] = scan_stages(preps[r][1]) if not noscan else []
    for r in range(2):
        for f in pending[r]:
            f()


def emit_PM_d(kb, cm, io, st, layer, last, banks):
    pre = st.pre + "d_"
    yM, t_yM = io["yM"], io["t_yM"]
    grow = kb.sbuf(pre + "grow", [128, 64], F32); t_grow = Tk(pre + "grow")
    kb.dma(grow[:], io["dn_norm_g"].to_broadcast([128, 64]), writes=[t_grow])
    sq = kb.sbuf(pre + "sq", [128, 192], F32); t_sq = Tk(pre + "sq")
    ssq = kb.sbuf(pre + "ssq", [128, 3], F32); t_ssq = Tk(pre + "ssq")
    on = kb.sbuf(pre + "on", [128, 192], F32); t_on = Tk(pre + "on")
    yring = Ring(kb, pre + "y", [128, 256], BF16, 2)
    oring = Ring(kb, pre + "yo", [128, 2, 512], BF16, 2)
    ringT = PsRing(banks, [6, 7])
    ntiles = LAT_TILES if last else NTILE
    for b0 in range(0, ntiles, 4):
        nt = min(4, ntiles - b0)
        pT, t_pT = ringT.next()
        pT16 = pT[:].bitcast(BF16)
        for t in range(nt):
            m = b0 + t
            kb.op("act", "activation", sq[:], st.O[:, m, :], AF.Square, reads=[st.t_O[m]], writes=[t_sq])
            kb.op("dve", "tensor_reduce", ssq[:], sq[:].rearrange("p (h d) -> p h d", h=3), axis=AX.X, op=ALU.add,
                  reads=[t_sq], writes=[t_ssq])
            kb.op("dve", "tensor_scalar", ssq[:], ssq[:], 1.0 / 64.0, EPS, op0=ALU.mult, op1=ALU.add, reads=[t_ssq], writes=[t_ssq])
            kb.op("pool", "tensor_tensor", ssq[:], ssq[:], cm.mhalf[:, 0:3], op=ALU.pow, reads=[t_ssq, cm.t_mhalf], writes=[t_ssq])
            on3 = on[:].rearrange("p (h d) -> p h d", h=3)
            kb.op("dve", "tensor_tensor", on3, st.O[:, m, :].rearrange("p (h d) -> p h d", h=3),
                  ssq[:].unsqueeze(2).to_broadcast([128, 3, 64]), op=ALU.mult, reads=[st.t_O[m], t_ssq], writes=[t_on])
            kb.op("pool", "tensor_tensor", on3, on3, grow[:].unsqueeze(1).to_broadcast([128, 3, 64]), op=ALU.mult,
                  reads=[t_on, t_grow], writes=[t_on])
            y, t_y = yring.next()
            kb.op("pool", "memset", y[:, 192:256], 0.0, writes=[t_y])
            kb.op("dve", "tensor_tensor", y[:, 0:192], on[:], st.szb[:, m, :], op=ALU.mult, reads=[t_on, st.t_szb[m]], writes=[t_y])
            for c in range(2):
                kb.op("pe", "transpose", pT16[:, c * 512 + t * 128:c * 512 + (t + 1) * 128], y[:, c * 128:(c + 1) * 128],
                      cm.identb[:], reads=[t_y, cm.t_identb], writes=[t_pT])
        ob, t_ob = oring.next()
        n = nt * 128
        kb.op("act", "copy", ob[:, 0, 0:n], pT16[:, 0:n], reads=[t_pT], writes=[t_ob])
        kb.op("dve", "tensor_copy", ob[0:64, 1, 0:n], pT16[0:64, 512:512 + n], reads=[t_pT], writes=[t_ob])
        kb.dma_acc(yM[192:320, b0 * 128:b0 * 128 + n], ob[:, 0, 0:n], reads=[t_ob], writes=[t_yM], owner=t_ob)
        kb.dma_acc(yM[320:384, b0 * 128:b0 * 128 + n], ob[0:64, 1, 0:n], reads=[t_ob], writes=[t_yM], owner=t_ob)


def _new_nc():
    Tk.FENCE = {}
    return bass.Bass("TRN2", target_bir_lowering=False)


def _din(nc, name, shape, dt=F32):
    return nc.dram_tensor(name, list(shape), dt, kind="ExternalInput").ap()


def _dout(nc, name, shape, dt=F32):
    return nc.dram_tensor(name, list(shape), dt, kind="ExternalOutput").ap()


def _dint(nc, name, shape, dt=F32):
    return nc.dram_tensor(name, list(shape), dt, kind="Internal").ap()


def build_L1():
    nc = _new_nc()
    io = dict(c_b=_din(nc, "c_b", [D]), c_ctx=_din(nc, "c_ctx", [D]), mod_w=_din(nc, "mod_w", [L, D, 6 * D]),
              mod_b=_din(nc, "mod_b", [L, 6 * D]), x_in=_din(nc, "x_in", [OWN, D]),
              modv=_dout(nc, "modv", [L, 2, 6 * D]), xmT_out=_dout(nc, "xmT_out", [D, OWN], BF16))
    kb = KB(nc)
    cm = Common(kb)
    banks = make_banks(kb)
    io["t_modv"] = Tk("modv")
    io["t_xmT_out"] = Tk("xmT_out")
    emit_P0(kb, cm, io, banks[0])
    emit_PA(kb, cm, io, PsRing(banks, [6, 7]))
    kb.wait_all("sp", [io["t_modv"], io["t_xmT_out"]])
    kb.emit()
    kb.close()
    return nc


def build_LR(layer, last, debug=False):
    nc = _new_nc()
    n_out = OWN_LAT if last else OWN
    xmT = _din(nc, "xmT_own", [D, OWN], BF16)
    yM = _din(nc, "yM_own", [768, OWN], BF16)
    io = dict(x_in=_din(nc, "x_in", [OWN, D]), modv=_din(nc, "modv", [L, 2, 6 * D]),
              xmT_own=lambda t0, n: xmT[:, t0:t0 + n], yM=lambda t0, n: [yM[:, t0:t0 + n]],
              wA=_din(nc, "wA", [D, 512]), wout=_din(nc, "wout", [D, D]), wup=_din(nc, "wup", [D, DFF]),
              wdown=_din(nc, "wdown", [DFF, D]),
              ln1_g=_din(nc, "ln1_g", [1, D]), ln1_b=_din(nc, "ln1_b", [1, D]),
              ln2_g=_din(nc, "ln2_g", [1, D]), ln2_b=_din(nc, "ln2_b", [1, D]),
              gmlp_ln_g=_din(nc, "gmlp_ln_g", [1, 256]), gmlp_ln_b=_din(nc, "gmlp_ln_b", [1, 256]),
              gmlp_w_s=_din(nc, "gmlp_w_s", [4, 128, 128]), gmlp_b_s=_din(nc, "gmlp_b_s", [4, 128]),
              x1s=(_dout if debug else _dint)(nc, "x1s", [OWN, D]),
              xm2Ts=(_dout if debug else _dint)(nc, "xm2Ts", [D, OWN], BF16),
              x_out=_dout(nc, "x_out", [n_out, D]))
    outs = [Tk("x_out")]
    io["t_x_out"] = outs[0]
    if not last:
        io["xmT_next"] = _dout(nc, "xmT_next", [D, OWN], BF16)
        io["t_xmT_next"] = Tk("xmT_next")
        outs.append(io["t_xmT_next"])
    kb = KB(nc)
    cm = Common(kb)
    banks = make_banks(kb)
    emit_PR(kb, cm, io, layer, last, banks)
    kb.wait_all("sp", outs)
    kb.emit()
    kb.close()
    return nc


def m_host_weights(w_in, conv_w, h):
    qB = w_in[:, 512 + 192 * h:512 + 192 * h + 192]
    kB = w_in[:, 896 + 64 * h:896 + 64 * h + 64]
    vB = w_in[:, 1024 + 64 * h:1024 + 64 * h + 64]
    wB = np.ascontiguousarray(np.concatenate([qB, kB, vB], 1))
    base = 1152
    z = w_in[:, base + 1152 + 192 * h:base + 1152 + 192 * h + 192]
    a = [w_in[:, base + 1536 + dd * 6 + 3 * h:base + 1536 + dd * 6 + 3 * h + 3] for dd in range(2)]
    bt = [w_in[:, base + 1548 + dd * 6 + 3 * h:base + 1548 + dd * 6 + 3 * h + 3] for dd in range(2)]
    wCt = np.ascontiguousarray(np.concatenate([z] + a + bt, 1))

    def ch(typ, j):
        c0 = typ * 384 + (3 * h + j) * 64
        return c0, c0 + 64
    tiles = [((0, 0), (0, 1)), ((1, 0), (1, 1)), ((0, 2), (2, 1)), ((1, 2), (2, 0)), (None, (2, 2))]
    cols = []
    cw = np.zeros((128, 5, 5), np.float32)
    for i, (lo, hi) in enumerate(tiles):
        for half, sel in enumerate((lo, hi)):
            if sel is None:
                cols.append(np.zeros((w_in.shape[0], 64), np.float32))
            else:
                c0, c1 = ch(*sel)
                cols.append(w_in[:, base + c0:base + c1])
                cw[half * 64:(half + 1) * 64, i, :] = conv_w[:, c0:c1].T
    wCf = np.ascontiguousarray(np.concatenate(cols, 1))
    return wB, wCt, wCf, cw


def build_LM(layer, last, parts=("attn", "dn")):
    nc = _new_nc()
    io = dict(xmT_full=_din(nc, "xmT_full", [D, NTOK], BF16),
              wB=_din(nc, "wB", [D, 320]), wCt=_din(nc, "wCt", [D, 204]), wCf=_din(nc, "wCf", [D, 640]),
              convw=_din(nc, "convw", [128, 5, 5]),
              attn_q_g=_din(nc, "attn_q_g", [1, 64]), attn_k_g=_din(nc, "attn_k_g", [1, 64]),
              dn_dt_bias=_din(nc, "dn_dt_bias", [1, 6]), dn_a_log=_din(nc, "dn_a_log", [1, 6]),
              dn_norm_g=_din(nc, "dn_norm_g", [1, 64]),
              ropeC=_din(nc, "ropeC", [SEQ, 64]), ropeS=_din(nc, "ropeS", [SEQ, 64]),
              yM=_dout(nc, "yM", [384, NTOK], BF16))
    io["t_yM"] = Tk("yM")
    xfull = io["xmT_full"]
    io["xmT_pieces"] = lambda tok0, n: [(0, 8, 0, n, xfull[:, tok0:tok0 + n])]
    kb = KB(nc)
    cm = Common(kb)
    banks = make_banks(kb)
    emit_PM(kb, cm, io, layer, last, banks, parts)
    kb.wait_all("sp", [io["t_yM"]])
    kb.emit()
    kb.close()
    return nc


def emit_PM(kb, cm, io, layer, last, banks, parts=("attn", "dn")):
    st = emit_PM_setup(kb, cm, io, layer)
    mk = kb.mark()
    emit_PM_a(kb, cm, io, st, layer, banks)
    kb.release(mk)
    if "attn" in parts:
        emit_PM_attn(kb, cm, io, st, layer, last, banks)
        kb.release(mk)
    if "dn" in parts:
        pre = st.pre
        raw = [kb.sbuf(pre + f"raw{s}", [128, 5, 516], F32) for s in range(3)]
        t_raw = [Tk(pre + f"raw{s}") for s in range(3)]
        xring = Ring(kb, pre + "bxm", [128, 8, 512], BF16, 2)
        emit_PM_b(kb, cm, io, st, layer, banks, xring, raw, t_raw)
        kb.release(mk)
        if "nob" in parts:
            return st
        emit_PM_c(kb, cm, io, st, layer, banks, noscan=("noscan" in parts))
        kb.release(mk)
        if "noc" in parts:
            return st
        emit_PM_d(kb, cm, io, st, layer, last, banks)
        kb.release(mk)
    return st


PAIR_GROUPS = [[0, 1], [2, 3], [4, 5], [6, 7]]
_LAYER_IN = dict(wA=[D, 512], wout=[D, D], wup=[D, DFF], wdown=[DFF, D], ln1_g=[1, D], ln1_b=[1, D], ln2_g=[1, D],
                 ln2_b=[1, D], gmlp_ln_g=[1, 256], gmlp_ln_b=[1, 256], gmlp_w_s=[4, 128, 128], gmlp_b_s=[4, 128],
                 wB=[D, 320], wCt=[D, 204], wCf=[D, 640], convw=[128, 5, 5], attn_q_g=[1, 64], attn_k_g=[1, 64],
                 dn_dt_bias=[1, 6], dn_a_log=[1, 6], dn_norm_g=[1, 64])


def build_fused():
    nc = _new_nc()
    dscr = lambda name, shape, dt=F32: nc.dram_tensor(name, list(shape), dt).ap()
    g = dict(c_b=_din(nc, "c_b", [D]), c_ctx=_din(nc, "c_ctx", [D]), mod_w=_din(nc, "mod_w", [L, D, 6 * D]),
             mod_b=_din(nc, "mod_b", [L, 6 * D]), x_in=_din(nc, "x_in", [OWN, D]), hsel=_din(nc, "hsel", [1, 2]),
             ropeC=_din(nc, "ropeC", [SEQ, 64]), ropeS=_din(nc, "ropeS", [SEQ, 64]))
    lay = [{k: _din(nc, f"{k}{i}", shp) for k, shp in _LAYER_IN.items()} for i in range(L)]
    x_out = _dout(nc, "x_out", [OWN_LAT, D])
    modv = dscr("modv", [L, 2, 6 * D]); t_modv = Tk("modv")
    xmT_own = [dscr(f"xmT_own{i}", [D, OWN], BF16) for i in range(L)]
    t_xmT_own = [Tk(f"xmT_own{i}") for i in range(L)]
    NPX = 4
    RX = D // NPX
    xmT_g = [[dscr(f"xmT_g{i}_{pc}", [2 * RX, OWN], BF16) for pc in range(NPX)] for i in range(L)]
    t_xmT_g = [Tk(f"xmT_g{i}") for i in range(L)]
    yM = [dscr(f"yM{i}", [384, NTOK], BF16) for i in range(L)]
    t_yM = [Tk(f"yM{i}") for i in range(L)]
    yG = [[dscr(f"yG{i}_{pc}", [384, NTOK], BF16) for pc in range(2)] for i in range(L)]
    t_yG = [Tk(f"yG{i}") for i in range(L)]
    x1s = [dscr(f"x1s{i}", [OWN, D]) for i in range(L)]
    xm2Ts = [dscr(f"xm2Ts{i}", [D, OWN], BF16) for i in range(L)]
    xres = [dscr(f"xres{i}", [OWN, D]) for i in range(L - 1)]
    t_xres = [Tk(f"xres{i}") for i in range(L - 1)]
    t_xout = Tk("x_out")

    kb = KB(nc)
    cm = Common(kb)
    banks = make_banks(kb)
    mk0 = kb.mark()
    emit_P0(kb, cm, dict(c_b=g["c_b"], c_ctx=g["c_ctx"], mod_w=g["mod_w"], mod_b=g["mod_b"], modv=modv, t_modv=t_modv), banks[0])
    kb.release(mk0)
    emit_PA(kb, cm, dict(x_in=g["x_in"], modv=modv, t_modv=t_modv, xmT_out=xmT_own[0], t_xmT_out=t_xmT_own[0]),
            PsRing(banks, [6, 7]))
    kb.release(mk0)
    for i in range(L):
        last = i == L - 1
        for pc in range(NPX):
            kb.coll("AllGather", xmT_own[i][pc * RX:(pc + 1) * RX, :], xmT_g[i][pc], PAIR_GROUPS,
                    reads=[t_xmT_own[i]], writes=[t_xmT_g[i]], acc=(pc > 0))
        xg = xmT_g[i]
        cpp = RX // 128

        def pieces(tok0, n, xg=xg):
            out = []
            for pc in range(NPX):
                if tok0 < SEQ:
                    r, loc = divmod(tok0, OWN_LAT)
                    out.append((pc * cpp, cpp, 0, n, xg[pc][r * RX:(r + 1) * RX, loc:loc + n]))
                else:
                    for r in range(2):
                        out.append((pc * cpp, cpp, 128 * r, 128, xg[pc][r * RX:(r + 1) * RX, OWN_LAT:OWN]))
            return out
        ioM = dict(lay[i])
        ioM.update(xmT_pieces=pieces, t_xmT_full=t_xmT_g[i], ropeC=g["ropeC"], ropeS=g["ropeS"], yM=yM[i], t_yM=t_yM[i])
        emit_PM(kb, cm, ioM, i, last, banks)
        kb.release(mk0)
        for pc in range(2):
            kb.coll("AllGather", yM[i][pc * 192:(pc + 1) * 192, :], yG[i][pc], PAIR_GROUPS,
                    reads=[t_yM[i]], writes=[t_yG[i]], acc=(pc > 0))
        yg = yG[i]

        def ycands(tok0, n, yg=yg):
            out = []
            for h in range(2):
                col = OWN_LAT * h + tok0 if tok0 < OWN_LAT else SEQ + OWN_CTX * h + (tok0 - OWN_LAT)
                out.append([yg[0][:, col:col + n], yg[1][:, col:col + n]])
            return out
        xo = xmT_own[i]
        ioR = dict(lay[i])
        ioR.update(modv=modv, t_modv=t_modv, x_in=g["x_in"] if i == 0 else xres[i - 1],
                   t_x_in=None if i == 0 else t_xres[i - 1],
                   xmT_own=lambda t0, n, xo=xo: xo[:, t0:t0 + n], t_xmT_own=t_xmT_own[i],
                   yM=ycands, t_yM_in=t_yG[i], hsel=g["hsel"], x1s=x1s[i], xm2Ts=xm2Ts[i],
                   x_out=x_out if last else xres[i], t_x_out=t_xout if last else t_xres[i])
        if not last:
            ioR.update(xmT_next=xmT_own[i + 1], t_xmT_next=t_xmT_own[i + 1])
        emit_PR(kb, cm, ioR, i, last, banks)
        kb.release(mk0)
    outs = [t_xout]
    kb.wait_all("sp", outs)
    kb.emit()
    kb.close()
    return nc


WOUT_PERM = np.arange(D)


def kernel_fused(x, c, ctx, c_ctx, mod_w, mod_b, w_in, w_out, gmlp_ln_g, gmlp_ln_b, gmlp_w_s, gmlp_b_s,
                 attn_q_g, attn_k_g, dn_conv_w, dn_a_log, dn_dt_bias, dn_norm_g,
                 ln1_g, ln1_b, ln2_g, ln2_b, w_up, w_down):
    f32 = lambda a: np.ascontiguousarray(np.asarray(a), dtype=np.float32)
    (x, c, ctx, c_ctx, mod_w, mod_b, w_in, w_out, gmlp_ln_g, gmlp_ln_b, gmlp_w_s, gmlp_b_s, attn_q_g, attn_k_g,
     dn_conv_w, dn_a_log, dn_dt_bias, dn_norm_g, ln1_g, ln1_b, ln2_g, ln2_b, w_up, w_down) = map(
        f32, (x, c, ctx, c_ctx, mod_w, mod_b, w_in, w_out, gmlp_ln_g, gmlp_ln_b, gmlp_w_s, gmlp_b_s, attn_q_g, attn_k_g,
              dn_conv_w, dn_a_log, dn_dt_bias, dn_norm_g, ln1_g, ln1_b, ln2_g, ln2_b, w_up, w_down))
    cores = [(b, h) for b in range(4) for h in range(2)]
    ropeC, ropeS = rope_tables()
    shared = []
    for i in range(L):
        shared.append({f"wA{i}": np.ascontiguousarray(w_in[i][:, 0:512]), f"wout{i}": np.ascontiguousarray(w_out[i][WOUT_PERM]),
                       f"wup{i}": w_up[i], f"wdown{i}": w_down[i], f"ln1_g{i}": ln1_g[i][None], f"ln1_b{i}": ln1_b[i][None],
                       f"ln2_g{i}": ln2_g[i][None], f"ln2_b{i}": ln2_b[i][None], f"gmlp_ln_g{i}": gmlp_ln_g[i][None],
                       f"gmlp_ln_b{i}": gmlp_ln_b[i][None], f"gmlp_w_s{i}": gmlp_w_s[i], f"gmlp_b_s{i}": gmlp_b_s[i],
                       f"attn_q_g{i}": attn_q_g[i][None], f"attn_k_g{i}": attn_k_g[i][None], f"dn_norm_g{i}": dn_norm_g[i][None]})
    perh = []
    for h in range(2):
        dct = {}
        for i in range(L):
            wB, wCt, wCf, cw = m_host_weights(w_in[i], dn_conv_w[i], h)
            dct.update({f"wB{i}": wB, f"wCt{i}": wCt, f"wCf{i}": wCf, f"convw{i}": cw,
                        f"dn_dt_bias{i}": np.ascontiguousarray(dn_dt_bias[i][:, 3 * h:3 * h + 3].reshape(1, 6)),
                        f"dn_a_log{i}": np.ascontiguousarray(dn_a_log[i][:, 3 * h:3 * h + 3].reshape(1, 6))})
        perh.append(dct)
    in_maps = []
    for b, h in cores:
        m = dict(c_b=c[b], c_ctx=c_ctx, mod_w=mod_w, mod_b=mod_b, ropeC=ropeC, ropeS=ropeS,
                 hsel=np.array([[1.0 - h, float(h)]], np.float32),
                 x_in=np.ascontiguousarray(np.concatenate([x[b, OWN_LAT * h:OWN_LAT * (h + 1)],
                                                           ctx[b, OWN_CTX * h:OWN_CTX * (h + 1)]], 0)))
        for i in range(L):
            m.update(shared[i])
        m.update(perh[h])
        in_maps.append(m)
    res = _run(build_fused(), in_maps, "fused")
    out = np.stack([np.concatenate([res[2 * b]["x_out"], res[2 * b + 1]["x_out"]], 0) for b in range(4)], 0)
    return np.ascontiguousarray(out, dtype=np.float32)


_BF = ml_dtypes.bfloat16


def _run(nc, in_maps, tag=""):
    import sys, time
    t0 = time.time()
    res = run_bass_kernel_spmd(nc, in_maps, core_ids=list(range(len(in_maps))))
    print(f"[kernel] launch {tag} done in {time.time() - t0:.1f}s", file=sys.stderr, flush=True)
    return res.results


def kernel_unfused(x, c, ctx, c_ctx, mod_w, mod_b, w_in, w_out, gmlp_ln_g, gmlp_ln_b, gmlp_w_s, gmlp_b_s,
           attn_q_g, attn_k_g, dn_conv_w, dn_a_log, dn_dt_bias, dn_norm_g,
           ln1_g, ln1_b, ln2_g, ln2_b, w_up, w_down):
    f32 = lambda a: np.ascontiguousarray(np.asarray(a), dtype=np.float32)
    x, c, ctx, c_ctx, mod_w, mod_b, w_in, w_out = map(f32, (x, c, ctx, c_ctx, mod_w, mod_b, w_in, w_out))
    gmlp_ln_g, gmlp_ln_b, gmlp_w_s, gmlp_b_s = map(f32, (gmlp_ln_g, gmlp_ln_b, gmlp_w_s, gmlp_b_s))
    attn_q_g, attn_k_g, dn_conv_w, dn_a_log, dn_dt_bias, dn_norm_g = map(
        f32, (attn_q_g, attn_k_g, dn_conv_w, dn_a_log, dn_dt_bias, dn_norm_g))
    ln1_g, ln1_b, ln2_g, ln2_b, w_up, w_down = map(f32, (ln1_g, ln1_b, ln2_g, ln2_b, w_up, w_down))
    cores = [(b, h) for b in range(4) for h in range(2)]
    ropeC, ropeS = rope_tables()

    x_cur = [np.ascontiguousarray(np.concatenate([x[b, OWN_LAT * h:OWN_LAT * (h + 1)],
                                                  ctx[b, OWN_CTX * h:OWN_CTX * (h + 1)]], 0)) for b, h in cores]
    r1 = _run(build_L1(), tag="L1", in_maps=[dict(c_b=c[b], c_ctx=c_ctx, mod_w=mod_w, mod_b=mod_b, x_in=x_cur[k])
                           for k, (b, h) in enumerate(cores)])
    modv = [r["modv"] for r in r1]
    xmT = [r["xmT_out"] for r in r1]
    for i in range(L):
        last = i == L - 1
        mw = [m_host_weights(w_in[i], dn_conv_w[i], h) for h in range(2)]
        inM = []
        for k, (b, h) in enumerate(cores):
            a0, a1 = xmT[2 * b], xmT[2 * b + 1]
            full = np.ascontiguousarray(np.concatenate([a0[:, :OWN_LAT], a1[:, :OWN_LAT], a0[:, OWN_LAT:], a1[:, OWN_LAT:]], 1))
            wB, wCt, wCf, cw = mw[h]
            inM.append(dict(xmT_full=full, wB=wB, wCt=wCt, wCf=wCf, convw=cw,
                            attn_q_g=attn_q_g[i][None], attn_k_g=attn_k_g[i][None],
                            dn_dt_bias=np.ascontiguousarray(dn_dt_bias[i][:, 3 * h:3 * h + 3].reshape(1, 6)),
                            dn_a_log=np.ascontiguousarray(dn_a_log[i][:, 3 * h:3 * h + 3].reshape(1, 6)),
                            dn_norm_g=dn_norm_g[i][None], ropeC=ropeC, ropeS=ropeS))
        rM = _run(build_LM(i, last), inM, f"LM{i}")
        inR = []
        for k, (b, h) in enumerate(cores):
            y0, y1 = rM[2 * b]["yM"], rM[2 * b + 1]["yM"]
            cols = np.r_[OWN_LAT * h:OWN_LAT * (h + 1), SEQ + OWN_CTX * h:SEQ + OWN_CTX * (h + 1)]
            yown = np.ascontiguousarray(np.concatenate([y0[0:192], y1[0:192], y0[192:384], y1[192:384]], 0)[:, cols])
            inR.append(dict(x_in=x_cur[k], modv=modv[k], xmT_own=xmT[k], yM_own=yown,
                            wA=np.ascontiguousarray(w_in[i][:, 0:512]), wout=w_out[i], wup=w_up[i], wdown=w_down[i],
                            ln1_g=ln1_g[i][None], ln1_b=ln1_b[i][None], ln2_g=ln2_g[i][None], ln2_b=ln2_b[i][None],
                            gmlp_ln_g=gmlp_ln_g[i][None], gmlp_ln_b=gmlp_ln_b[i][None],
                            gmlp_w_s=gmlp_w_s[i], gmlp_b_s=gmlp_b_s[i]))
        rR = _run(build_LR(i, last), inR, f"LR{i}")
        x_cur = [r["x_out"] for r in rR]
        if not last:
            xmT = [r["xmT_next"] for r in rR]
    out = np.stack([np.concatenate([x_cur[2 * b][:OWN_LAT], x_cur[2 * b + 1][:OWN_LAT]], 0) for b in range(4)], 0)
    return np.ascontiguousarray(out, dtype=np.float32)


kernel = kernel_fused
```

```python
import math
import numpy as np
import ml_dtypes
import concourse.bass as bass
import concourse.mybir as mybir
from concourse.bass_utils import run_bass_kernel_spmd

F32 = mybir.dt.float32
BF16 = mybir.dt.bfloat16
AF = mybir.ActivationFunctionType
ALU = mybir.AluOpType
AX = mybir.AxisListType

D = 1024
L = 2
SEQ = 4096
CTX = 256
NTOK = SEQ + CTX
OWN_LAT = SEQ // 2
OWN_CTX = CTX // 2
OWN = OWN_LAT + OWN_CTX
DFF = 4096
ALPHA = float((2 * L) ** 0.25)
EPS = 1e-6
NCORES = 8


class Tk:
    __slots__ = ("name", "w", "r", "dsem", "dcnt")
    FENCE = {}

    def __init__(self, name=""):
        self.name = name
        self.w = {}
        self.r = dict(Tk.FENCE)
        self.dsem = None
        self.dcnt = 0


class _Eng:
    def __init__(self, name, sem):
        self.name = name
        self.sem = sem
        self.count = 0
        self.known = {}
        self.ops = []


NOCOLL = [False]
NO_SAME_ENGINE_SYNC = {"pe"}


class KB:
    ENGS = ("pe", "act", "dve", "pool", "sp")

    def __init__(self, nc, same_engine_sync=True):
        self.nc = nc
        self.same = same_engine_sync
        self._ctx = []
        self.eng = {}
        for n in self.ENGS:
            self.eng[n] = _Eng(n, self._sem("e_" + n))
        self.nwaits = 0
        self._uid = 0
        Tk.FENCE = {}
        self._dma_owners = []
        self._pool_hist = []
        self.max_pool_dma = 2
        self.coll_inc = 1
        self._free_sems = []
        self._fence_base = {}
        self._coll_owners = []
        self.arena_words = 53000
        self.arena = self._enter(self.nc.sbuf_tensor("arena", [128, self.arena_words], F32))
        self.arena_off = 0
        self.arena_peak = 0

    def mark(self):
        return (self.arena_off, len(self._dma_owners))

    def release(self, mark):
        off, nown = mark
        ev = dict(self._fence_base)
        for e in self.eng.values():
            if e.count:
                ev[e.sem] = e.count
        for t in self._dma_owners:
            ev[t.dsem] = t.dcnt
        for t in self._dma_owners[nown:]:
            self._free_sems.append((t.dsem, t.dcnt))
            self._fence_base[t.dsem] = t.dcnt
        del self._dma_owners[nown:]
        Tk.FENCE = ev
        self.arena_off = off

    def _new_dsem(self, owner, prefix="d_", queue="sp"):
        if self._free_sems and queue != "pool":
            owner.dsem, owner.dcnt = self._free_sems.pop()
        else:
            owner.dsem = self._sem(prefix + owner.name)
        self._dma_owners.append(owner)

    def _enter(self, cm):
        v = cm.__enter__()
        self._ctx.append(cm)
        return v

    def _sem(self, name):
        self._uid = getattr(self, "_uid", 0) + 1
        return self._enter(self.nc.semaphore(f"{name}_{self._uid}"))

    def sbuf(self, name, shape, dtype=F32):
        shape = list(shape)
        cnt = 1
        for s in shape[1:]:
            cnt *= s
        isz = 2 if dtype == BF16 else 4
        nwords = (cnt * isz + 3) // 4
        nwords = (nwords + 7) // 8 * 8
        off = self.arena_off
        if off + nwords > self.arena_words:
            raise AssertionError(f"arena overflow allocating {name} {shape}: off={off * 4} need={nwords * 4}")
        self.arena_off = off + nwords
        self.arena_peak = max(self.arena_peak, self.arena_off)
        ap = self.arena[0:shape[0], off:off + nwords]
        if dtype == BF16:
            ap = ap.bitcast(BF16)
        elif dtype != F32:
            raise AssertionError("arena supports f32/bf16")
        ap = ap[:, 0:cnt]
        if len(shape) > 2:
            names = " ".join(f"d{k}" for k in range(len(shape) - 1))
            kw = {f"d{k}": shape[k + 1] for k in range(len(shape) - 1)}
            ap = ap.rearrange(f"p ({names}) -> p {names}", **kw)
        return ap

    def psum(self, name, shape, dtype=F32):
        return self._enter(self.nc.psum_tensor(name, list(shape), dtype))

    def close(self):
        while self._ctx:
            self._ctx.pop().__exit__(None, None, None)

    def _deps(self, e, reads, writes, waw=True):
        need = {}
        for t in reads:
            for s, v in t.w.items():
                if need.get(s, 0) < v:
                    need[s] = v
        for t in writes:
            for d in ((t.w, t.r) if waw else (t.r,)):
                for s, v in d.items():
                    if need.get(s, 0) < v:
                        need[s] = v
        waits = []
        for s, v in need.items():
            if s is e.sem and (e.name in NO_SAME_ENGINE_SYNC or not self.same):
                continue
            if e.known.get(s, 0) < v:
                e.known[s] = v
                waits.append((s, v))
        self.nwaits += len(waits)
        return waits

    def op(self, eng, meth, *args, reads=(), writes=(), **kwargs):
        e = self.eng[eng]

        def fn(h, meth=meth, args=args, kwargs=kwargs):
            return getattr(h, meth)(*args, **kwargs)
        reads = [t for t in reads if t is not None]
        writes = [t for t in writes if t is not None]
        waits = self._deps(e, reads, writes)
        e.count += 1
        ev = (e.sem, e.count)
        e.ops.append((waits, fn, (e.sem, 1)))
        for t in writes:
            t.w = {ev[0]: ev[1]}
            t.r = {}
        for t in reads:
            if t.r.get(ev[0], 0) < ev[1]:
                t.r[ev[0]] = ev[1]
        return ev

    def dma(self, out, in_, reads=(), writes=(), owner=None, queue="sp", **kw):
        e = self.eng[queue]
        reads = [t for t in reads if t is not None]
        writes = [t for t in writes if t is not None]
        if owner is None:
            owner = (writes + reads)[0]
        if owner.dsem is None:
            self._new_dsem(owner, queue=queue)
        waits = self._deps(e, reads, writes)
        waits += self._pace(e, queue)
        owner.dcnt += 16
        ev = (owner.dsem, owner.dcnt)
        if queue == "pool":
            self._pool_hist.append(ev)

        def fn(h, out=out, in_=in_, kw=kw):
            return h.dma_start(out=out, in_=in_, **kw)
        e.ops.append((waits, fn, (owner.dsem, 16)))
        for t in writes:
            t.w = {ev[0]: ev[1]}
            t.r = {}
        for t in reads:
            if t.r.get(ev[0], 0) < ev[1]:
                t.r[ev[0]] = ev[1]
        return ev

    def dma_acc(self, out, in_, reads=(), writes=(), owner=None, queue="sp", **kw):
        e = self.eng[queue]
        reads = [t for t in reads if t is not None]
        writes = [t for t in writes if t is not None]
        if owner is None:
            owner = (writes + reads)[0]
        if owner.dsem is None:
            self._new_dsem(owner, queue=queue)
        waits = self._deps(e, reads, writes, waw=False)
        waits += self._pace(e, queue)
        owner.dcnt += 16
        ev = (owner.dsem, owner.dcnt)
        if queue == "pool":
            self._pool_hist.append(ev)

        def fn(h, out=out, in_=in_, kw=kw):
            return h.dma_start(out=out, in_=in_, **kw)
        e.ops.append((waits, fn, (owner.dsem, 16)))
        for t in writes:
            t.w[ev[0]] = max(t.w.get(ev[0], 0), ev[1])
        for t in reads:
            if t.r.get(ev[0], 0) < ev[1]:
                t.r[ev[0]] = ev[1]
        return ev

    def coll(self, kind, src, dst, groups, reads=(), writes=(), acc=False):
        if NOCOLL[0]:
            n = src.shape[0]
            (self.dma_acc if acc else self.dma)(dst[0:n, :], src, reads=reads, writes=writes, owner=writes[0])
            self.dma_acc(dst[n:2 * n, :], src, reads=reads, writes=writes, owner=writes[0])
            return
        e = self.eng["pool"]
        reads = [t for t in reads if t is not None]
        writes = [t for t in writes if t is not None]
        owner = writes[0]
        if owner.dsem is None:
            owner.dsem = self._sem("c_" + owner.name)
            self._coll_owners.append(owner)
        waits = self._deps(e, reads, writes, waw=not acc)
        owner.dcnt += self.coll_inc
        ev = (owner.dsem, owner.dcnt)

        def fn(h, kind=kind, src=src, dst=dst, groups=groups):
            return h.collective_compute(kind, ALU.bypass, replica_groups=groups, ins=[src.opt()], outs=[dst.opt()])
        e.ops.append((waits, fn, (owner.dsem, None if self.coll_inc == 1 else self.coll_inc)))
        for t in writes:
            if acc:
                t.w[ev[0]] = max(t.w.get(ev[0], 0), ev[1])
            else:
                t.w = {ev[0]: ev[1]}
                t.r = {}
        for t in reads:
            if t.r.get(ev[0], 0) < ev[1]:
                t.r[ev[0]] = ev[1]
        return ev

    def _pace(self, e, queue):
        if queue != "pool" or len(self._pool_hist) < self.max_pool_dma:
            return []
        s, v = self._pool_hist[-self.max_pool_dma]
        if e.known.get(s, 0) < v:
            e.known[s] = v
            return [(s, v)]
        return []

    def wait_all(self, eng, tks):
        e = self.eng[eng]
        waits = self._deps(e, tks, [])
        if waits:
            e.ops.append((waits, None, None))

    def emit(self):
        nc = self.nc
        handles = {"pe": "tensor", "act": "scalar", "dve": "vector", "pool": "gpsimd", "sp": "sync"}
        with nc.Block() as block:
            for n in self.ENGS:
                e = self.eng[n]
                if not e.ops:
                    continue

                def body(h, e=e):
                    for waits, fn, inc in e.ops:
                        for s, v in waits:
                            h.wait_ge(s, v)
                        if fn is not None:
                            ins = fn(h)
                            if inc is not None:
                                if inc[1] is None:
                                    ins.then_inc(inc[0])
                                else:
                                    ins.then_inc(inc[0], inc[1])
                getattr(block, handles[n])(body)


class Ring:
    def __init__(self, kb, name, shape, dtype, n, psum=False):
        alloc = kb.psum if psum else kb.sbuf
        self.t = [alloc(f"{name}{i}", shape, dtype) for i in range(n)]
        self.tk = [Tk(f"{name}{i}") for i in range(n)]
        self.n = n
        self.i = 0

    def next(self):
        i = self.i % self.n
        self.i += 1
        return self.t[i], self.tk[i]


class PsRing:
    def __init__(self, banks, idx):
        self.banks = banks
        self.idx = idx
        self.i = 0

    def next(self):
        b = self.banks[self.idx[self.i % len(self.idx)]]
        self.i += 1
        return b


def make_banks(kb):
    return [(kb.psum(f"bank{i}", [128, 512], F32), Tk(f"bank{i}")) for i in range(8)]


class Common:
    def __init__(self, kb):
        self.kb = kb
        self.ident = kb.sbuf("ident", [128, 128], F32)
        self.t_ident = Tk("ident")
        self.identb = kb.sbuf("identb", [128, 128], BF16)
        self.t_identb = Tk("identb")
        self.mhalf = kb.sbuf("mhalf", [128, 16], F32)
        self.t_mhalf = Tk("mhalf")
        kb.op("pool", "memset", self.ident[:], 0.0, writes=[self.t_ident])
        kb.op("pool", "affine_select", self.ident[:], self.ident[:], pattern=[[-1, 128]],
                                                compare_op=ALU.not_equal, fill=1.0, base=0,
                                                channel_multiplier=1,
              reads=[self.t_ident], writes=[self.t_ident])
        kb.op("pool", "tensor_copy", self.identb[:], self.ident[:],
              reads=[self.t_ident], writes=[self.t_identb])
        kb.op("pool", "memset", self.mhalf[:], -0.5, writes=[self.t_mhalf])


def emit_rstd(kb, cm, mv, t_mv, n, rstd, t_rstd):
    kb.op("dve", "tensor_scalar", rstd[:, 0:n], mv[:, 0:n, 1], EPS, None, op0=ALU.add,
          reads=[t_mv], writes=[t_rstd])
    kb.op("pool", "tensor_tensor", rstd[:, 0:n], rstd[:, 0:n], cm.mhalf[:, 0:n], op=ALU.pow,
          reads=[t_rstd, cm.t_mhalf], writes=[t_rstd])


def emit_stats(kb, x_ap, t_x, width, bnst, t_bnst, mv_ap, t_mv):
    nch = max(1, width // 512)
    cw = width // nch
    for c in range(nch):
        kb.op("dve", "bn_stats", bnst[:, c, :], x_ap[:, c * cw:(c + 1) * cw],
              reads=[t_x], writes=[t_bnst])
    kb.op("dve", "bn_aggr", mv_ap, bnst[:, 0:nch, :], reads=[t_bnst], writes=[t_mv])


def emit_P0(kb, cm, io, bank):
    c_b, c_ctx, mod_w, mod_b, modv = io["c_b"], io["c_ctx"], io["mod_w"], io["mod_b"], io["modv"]
    cT = kb.sbuf("p0_cT", [128, 2, 8], F32)
    t_cT = Tk("p0_cT")
    sT = kb.sbuf("p0_sT", [128, 8, 2], F32)
    t_sT = Tk("p0_sT")
    kb.dma(cT[:, 0, :], c_b.rearrange("(c p) -> p c", p=128), writes=[t_cT], allow_slow_non_contiguous=True)
    kb.dma_acc(cT[:, 1, :], c_ctx.rearrange("(c p) -> p c", p=128), writes=[t_cT], allow_slow_non_contiguous=True)
    kb.op("act", "activation", sT[:].rearrange("p c j -> p j c"), cT[:], AF.Silu,
          reads=[t_cT], writes=[t_sT])
    wring = Ring(kb, "p0_w", [128, 8, 512], F32, 2)
    bring = Ring(kb, "p0_b", [2, 512], F32, 2)
    rring = Ring(kb, "p0_r", [2, 512], F32, 2)
    ps, t_ps = bank
    t_out = io["t_modv"]
    for i in range(L):
        for n in range(12):
            w, t_w = wring.next()
            for kc in range(8):
                f = kb.dma if kc == 0 else kb.dma_acc
                f(w[:, kc, :], mod_w[i, kc * 128:(kc + 1) * 128, n * 512:(n + 1) * 512], writes=[t_w])
            b, t_b = bring.next()
            kb.dma(b[:], mod_b[i:i + 1, n * 512:(n + 1) * 512].to_broadcast([2, 512]), writes=[t_b])
            for kc in range(8):
                kb.op("pe", "matmul", ps[0:2, :], sT[:, kc, :], w[:, kc, :],
                                                           start=(kc == 0), stop=(kc == 7),
                      reads=[t_sT, t_w], writes=[t_ps])
            r, t_r = rring.next()
            kb.op("dve", "tensor_tensor", r[:], ps[0:2, :], b[:], op=ALU.add,
                  reads=[t_ps, t_b], writes=[t_r])
            if n in (2, 3, 8, 9):
                kb.op("dve", "tensor_scalar", r[:], r[:], 1.0, None, op0=ALU.add,
                      reads=[t_r], writes=[t_r])
            kb.dma_acc(modv[i, :, n * 512:(n + 1) * 512], r[:], reads=[t_r], writes=[t_out], owner=t_r)


class ModBufs:
    def __init__(self, kb, name, nt):
        self.nt = nt
        self.xn = [kb.sbuf(f"{name}_xn{t}", [128, D], F32) for t in range(nt)]
        self.t_xn = [Tk(f"{name}_xn{t}") for t in range(nt)]
        self.bnst = kb.sbuf(f"{name}_bnst", [128, nt, 2, 6], F32)
        self.t_bnst = [Tk(f"{name}_bnst{t}") for t in range(nt)]
        self.mv = kb.sbuf(f"{name}_mv", [128, nt, 2], F32)
        self.t_mv = Tk(f"{name}_mv")
        self.rstd = kb.sbuf(f"{name}_rstd", [128, nt], F32)
        self.t_rstd = Tk(f"{name}_rstd")


def emit_modulate(kb, cm, mb, xs, nt, cols, t_cols, sc_idx, sh_idx, out, t_out, psring, evac_flip=[0]):
    for t in range(nt):
        x_ap, t_x = xs[t]
        emit_stats(kb, x_ap, t_x, D, mb.bnst[:, t], mb.t_bnst[t], mb.mv[:, t, :], mb.t_mv)
    emit_rstd(kb, cm, mb.mv, mb.t_mv, nt, mb.rstd, mb.t_rstd)
    for t in range(nt):
        x_ap, t_x = xs[t]
        kb.op("dve", "tensor_scalar",
            mb.xn[t][:], x_ap, mb.mv[:, t, 0:1], mb.rstd[:, t:t + 1], op0=ALU.subtract, op1=ALU.mult,
            reads=[t_x, mb.t_mv, mb.t_rstd], writes=[mb.t_xn[t]])
    for fc in range(8):
        ps, t_ps = psring.next()
        for t in range(nt):
            kb.op("pe", "transpose",
                ps[:, t * 128:(t + 1) * 128], mb.xn[t][:, fc * 128:(fc + 1) * 128], cm.ident[:],
                reads=[mb.t_xn[t], cm.t_ident], writes=[t_ps])
        evac_flip[0] ^= 1
        if evac_flip[0]:
            kb.op("act", "activation",
                out[:, fc, 0:nt * 128], ps[:, 0:nt * 128], AF.Identity,
                bias=cols[:, sh_idx, fc:fc + 1], scale=cols[:, sc_idx, fc:fc + 1],
                reads=[t_ps, t_cols], writes=[t_out])
        else:
            kb.op("dve", "tensor_scalar",
                out[:, fc, 0:nt * 128], ps[:, 0:nt * 128], cols[:, sc_idx, fc:fc + 1],
                cols[:, sh_idx, fc:fc + 1], op0=ALU.mult, op1=ALU.add,
                reads=[t_ps, t_cols], writes=[t_out])


def load_cols_T(kb, cm, src_vec, cols_dst, t_cols, bank, reads=(), name="cstage", stage_buf=None):
    if stage_buf is None:
        stage, t_stage = kb.sbuf(name, [48, 128], F32), Tk(name)
    else:
        stage, t_stage = stage_buf
    kb.dma(stage[:], src_vec.rearrange("(r p) -> r p", p=128), reads=list(reads), writes=[t_stage])
    ps, t_ps = bank
    kb.op("pe", "transpose", ps[:, 0:48], stage[:], cm.ident[0:48, 0:48], reads=[t_stage, cm.t_ident], writes=[t_ps])
    kb.op("act", "copy", cols_dst.rearrange("p j c -> p (j c)"), ps[:, 0:48], reads=[t_ps], writes=[t_cols])


def load_cols(kb, modv, layer, which, cols, t_cols, first=True):
    f = kb.dma if first else kb.dma_acc
    f(cols[:, 0:6, :], modv[layer, which, :].rearrange("(j c p) -> p j c", j=6, p=128),
      writes=[t_cols], allow_slow_non_contiguous=True)


def emit_PA(kb, cm, io, psring):
    x_in, modv, xmT_out = io["x_in"], io["modv"], io["xmT_out"]
    t_modv = io["t_modv"]
    t_xmT = io["t_xmT_out"]
    colsL = kb.sbuf("pa_colsL", [128, 6, 8], F32)
    t_colsL = Tk("pa_colsL")
    colsC = kb.sbuf("pa_colsC", [128, 6, 8], F32)
    t_colsC = Tk("pa_colsC")
    load_cols_T(kb, cm, modv[0, 0, :], colsL[:, 0:6, :], t_colsL, psring.next(), reads=[t_modv], name="pa_csL")
    load_cols_T(kb, cm, modv[0, 1, :], colsC[:, 0:6, :], t_colsC, psring.next(), reads=[t_modv], name="pa_csC")
    NT = 4
    mb = ModBufs(kb, "pa", NT)
    xring = Ring(kb, "pa_x", [128, NT, D], F32, 2)
    oring = Ring(kb, "pa_o", [128, 8, NT * 128], BF16, 2)
    groups = [(g * 512, 4, colsL, t_colsL) for g in range(4)] + [(OWN_LAT, 1, colsC, t_colsC)]
    for (tok0, nt, cols, t_cols) in groups:
        xt, t_xt = xring.next()
        kb.dma(xt[:, 0:nt, :], x_in[tok0:tok0 + nt * 128, :].rearrange("(t p) d -> p t d", p=128), writes=[t_xt])
        o, t_o = oring.next()
        emit_modulate(kb, cm, mb, [(xt[:, t, :], t_xt) for t in range(nt)], nt, cols, t_cols, 1, 0,
                      o, t_o, psring)
        kb.dma_acc(xmT_out[:, tok0:tok0 + nt * 128].rearrange("(c p) t -> p c t", p=128),
                   o[:, :, 0:nt * 128], reads=[t_o], writes=[t_xmT], owner=t_o)


def load_weight_bf16(kb, dst, t_dst, src, nchunk, per=1):
    first = True
    for c0 in range(0, nchunk, per):
        c1 = min(nchunk, c0 + per)
        f = kb.dma if first else kb.dma_acc
        first = False
        f(dst[:, c0:c1, :], src[c0 * 128:c1 * 128, :].rearrange("(c p) n -> p c n", p=128),
          writes=[t_dst], queue="pool")


def load_row(kb, dst, t_dst, src_row, n, first=True):
    f = kb.dma if first else kb.dma_acc
    f(dst, src_row.to_broadcast([128, n]), writes=[t_dst])


def emit_deepnorm(kb, cm, xt_ap, t_xt, br_list, grow, t_grow, lng, lnb, t_ln, tmp, t_tmp,
                  bnst, t_bnst, mv, t_mv, rstd, t_rstd):
    for hf, (br, t_br) in enumerate(br_list):
        kb.op("dve", "tensor_tensor",
            tmp[:, hf * 512:(hf + 1) * 512], br, grow[:, hf * 512:(hf + 1) * 512], op=ALU.mult,
            reads=[t_br, t_grow], writes=[t_tmp])
    kb.op("dve", "scalar_tensor_tensor", xt_ap, xt_ap, ALPHA, tmp[:], op0=ALU.mult, op1=ALU.add,
          reads=[t_xt, t_tmp], writes=[t_xt])
    emit_stats(kb, xt_ap, t_xt, D, bnst, t_bnst, mv[:, 0, :], t_mv)
    emit_rstd(kb, cm, mv, t_mv, 1, rstd, t_rstd)
    kb.op("dve", "tensor_scalar", xt_ap, xt_ap, mv[:, 0, 0:1], rstd[:, 0:1],
                                           op0=ALU.subtract, op1=ALU.mult,
          reads=[t_xt, t_mv, t_rstd], writes=[t_xt])
    kb.op("pool", "tensor_tensor", xt_ap, xt_ap, lng, op=ALU.mult, reads=[t_xt, t_ln], writes=[t_xt])
    kb.op("pool", "tensor_tensor", xt_ap, xt_ap, lnb, op=ALU.add, reads=[t_xt, t_ln], writes=[t_xt])


def emit_PR(kb, cm, io, layer, last, banks):
    modv = io["modv"]
    t_modv = io.get("t_modv")
    x_in, xmT_own, yM = io["x_in"], io["xmT_own"], io["yM"]
    x1s, xm2Ts = io["x1s"], io["xm2Ts"]
    x_out = io["x_out"]
    xmT_next = io.get("xmT_next")
    t_x1s, t_xm2Ts, t_xout = Tk("x1s"), Tk("xm2Ts"), io["t_x_out"]
    t_xmTn = io.get("t_xmT_next")
    pre = f"r{layer}_"

    s1reg = kb.sbuf(pre + "s1reg", [128, 16384], BF16)
    wA = s1reg[:, 0:4096].rearrange("p (c n) -> p c n", c=8); t_wA = Tk(pre + "wA")
    wout = s1reg[:, 4096:12288].rearrange("p (c n) -> p c n", c=8); t_wout = Tk(pre + "wout")
    xm_slots = [s1reg[:, 12288 + 2048 * s:12288 + 2048 * (s + 1)].rearrange("p (c n) -> p c n", c=8) for s in range(2)]
    t_xm_slots = [Tk(pre + f"xm{s}") for s in range(2)]
    wdown_b = s1reg[:, :].rearrange("p (c n) -> p c n", c=16); t_wdown_b = Tk(pre + "wdown_b")
    wup = kb.sbuf(pre + "wup", [128, 8, DFF], BF16); t_wup = Tk(pre + "wup")
    wdown_a = kb.sbuf(pre + "wdown_a", [128, 16, D], BF16); t_wdown_a = Tk(pre + "wdown_a")
    load_weight_bf16(kb, wA, t_wA, io["wA"], 8, per=8)
    load_weight_bf16(kb, wout, t_wout, io["wout"], 8, per=4)
    rows = kb.sbuf(pre + "rows", [128, 3, D], F32)
    t_rows = [Tk(pre + "rowsG")]
    t_rowsLN = Tk(pre + "rowsLN")
    rowsA = kb.sbuf(pre + "rowsA", [128, 2, 256], F32)
    t_rowsA = Tk(pre + "rowsA")

    def load_gates(which, sub):
        kb.dma(rows[:, 0, :], modv[layer, which:which + 1, (2 + 3 * sub) * D:(3 + 3 * sub) * D].to_broadcast([128, D]),
               reads=[t_modv], writes=[t_rows[0]])

    def load_ln(sub):
        kb.dma(rows[:, 1, :], io[f"ln{sub + 1}_g"].to_broadcast([128, D]), writes=[t_rowsLN])
        kb.dma_acc(rows[:, 2, :], io[f"ln{sub + 1}_b"].to_broadcast([128, D]), writes=[t_rowsLN])
    load_gates(0, 0)
    load_ln(0)
    kb.dma(rowsA[:, 0, :], io["gmlp_ln_g"].to_broadcast([128, 256]), writes=[t_rowsA])
    kb.dma_acc(rowsA[:, 1, :], io["gmlp_ln_b"].to_broadcast([128, 256]), writes=[t_rowsA])
    colsL = kb.sbuf(pre + "colsL", [128, 12, 8], F32); t_colsL = Tk(pre + "colsL")
    colsC = kb.sbuf(pre + "colsC", [128, 12, 8], F32); t_colsC = Tk(pre + "colsC")
    cstage = (kb.sbuf(pre + "cstage", [48, 128], F32), Tk(pre + "cstage"))
    for which, cols, t_colsX in ((0, colsL, t_colsL), (1, colsC, t_colsC)):
        load_cols_T(kb, cm, modv[layer, which, :], cols[:, 0:6, :], t_colsX, banks[6 + which], reads=[t_modv],
                    stage_buf=cstage)
        if not last:
            load_cols_T(kb, cm, modv[layer + 1, which, :], cols[:, 6:12, :], t_colsX, banks[6 + which], reads=[t_modv],
                        stage_buf=cstage)
    wsn = kb.sbuf(pre + "wsn", [128, 4, 128], F32); t_wsn = Tk(pre + "wsn")
    wsT = kb.sbuf(pre + "wsT", [128, 4, 128], BF16); t_wsT = Tk(pre + "wsT")
    bsT = kb.sbuf(pre + "bsT", [128, 4], F32); t_bsT = Tk(pre + "bsT")
    kb.dma(wsn[:], io["gmlp_w_s"].rearrange("g t s -> t g s"), writes=[t_wsn])
    kb.dma(bsT[:], io["gmlp_b_s"].rearrange("g t -> t g"), writes=[t_bsT], allow_slow_non_contiguous=True)
    for g in range(4):
        ps, t_ps = banks[6 + (g % 2)]
        kb.op("pe", "transpose", ps[:, 0:128], wsn[:, g, :], cm.ident[:],
              reads=[t_wsn, cm.t_ident], writes=[t_ps])
        kb.op("act", "copy", wsT[:, g, :], ps[:, 0:128], reads=[t_ps], writes=[t_wsT])

    load_weight_bf16(kb, wup, t_wup, io["wup"], 8, per=1)
    load_weight_bf16(kb, wdown_a, t_wdown_a, io["wdown"][0:2048, :], 16, per=4)

    NT = 2
    T = NT * 128
    groups = [(g * T, NT, 0) for g in range(OWN_LAT // T)]
    if not last:
        groups.append((OWN_LAT, 1, 1))
    mb = ModBufs(kb, pre + "mb", NT)
    xring = Ring(kb, pre + "x", [128, NT, D], F32, 2)
    yring = Ring(kb, pre + "y", [128, 8, T], BF16, 2)
    x2ring = Ring(kb, pre + "xm2", [128, 8, T], BF16, 2)
    if "hsel" in io:
        ycring = Ring(kb, pre + "yc", [128, 2, 6, T], BF16, 1)
        hs = kb.sbuf(pre + "hs", [128, 2], F32); t_hs = Tk(pre + "hs")
        kb.dma(hs[:], io["hsel"].to_broadcast([128, 2]), writes=[t_hs])
    gel = kb.sbuf(pre + "gel", [128, 512], F32); t_gel = Tk(pre + "gel")
    vn = kb.sbuf(pre + "vn", [128, 256], F32); t_vn = Tk(pre + "vn")
    vln = kb.sbuf(pre + "vln", [128, 256], BF16); t_vln = Tk(pre + "vln")
    mixb = kb.sbuf(pre + "mixb", [128, 256], F32); t_mixb = Tk(pre + "mixb")
    ya = kb.sbuf(pre + "ya", [128, 256], BF16); t_ya = Tk(pre + "ya")
    tmp = kb.sbuf(pre + "tmp", [128, D], F32); t_tmp = Tk(pre + "tmp")
    bnst = kb.sbuf(pre + "bnst", [128, 2, 6], F32); t_bnst = Tk(pre + "bnst")
    mv = kb.sbuf(pre + "mv", [128, 1, 2], F32); t_mv = Tk(pre + "mv")
    rstd = kb.sbuf(pre + "rstd", [128, 1], F32); t_rstd = Tk(pre + "rstd")

    ringA = PsRing(banks, [0, 1])
    ringBr = PsRing(banks, [2, 3, 4, 5])
    ringT = PsRing(banks, [6, 7])

    cur_gate = [0]
    pend = None
    for gi, (tok0, nt, which) in enumerate(groups):
        xt, t_xt = xring.next()
        kb.dma(xt[:, 0:nt, :], x_in[tok0:tok0 + nt * 128, :].rearrange("(t p) d -> p t d", p=128),
               reads=[io.get("t_x_in")], writes=[t_xt])
        xm, t_xm = xm_slots[gi % 2], t_xm_slots[gi % 2]
        kb.dma(xm[:, :, 0:nt * 128], xmT_own(tok0, nt * 128).rearrange("(c p) t -> p c t", p=128),
               reads=[io.get("t_xmT_own")], writes=[t_xm])
        yT, t_yT = yring.next()
        cands = yM(tok0, nt * 128)
        if len(cands) == 1:
            kb.dma(yT[:, 2:8, 0:nt * 128], cands[0].rearrange("(c p) t -> p c t", p=128),
                   reads=[io.get("t_yM_in")], writes=[t_yT])
        else:
            yc, t_yc = ycring.next()
            first = True
            for k in range(2):
                for part in range(2):
                    (kb.dma if first else kb.dma_acc)(yc[:, k, 3 * part:3 * part + 3, 0:nt * 128],
                                                      cands[k][part].rearrange("(c p) t -> p c t", p=128),
                                                      reads=[io.get("t_yM_in")], writes=[t_yc])
                    first = False
            kb.op("dve", "tensor_scalar", yT[:, 2:8, 0:nt * 128], yc[:, 0, :, 0:nt * 128], hs[:, 0:1], None, op0=ALU.mult,
                  reads=[t_yc, t_hs], writes=[t_yT])
            kb.op("dve", "scalar_tensor_tensor", yT[:, 2:8, 0:nt * 128], yc[:, 1, :, 0:nt * 128], hs[:, 1:2],
                  yT[:, 2:8, 0:nt * 128], op0=ALU.mult, op1=ALU.add, reads=[t_yc, t_hs, t_yT], writes=[t_yT])
        if which != cur_gate[0]:
            load_gates(which, 0)
            cur_gate[0] = which
        cols, t_cols = (colsL, t_colsL) if which == 0 else (colsC, t_colsC)
        for t in range(nt):
            pA, t_pA = ringA.next()
            for kc in range(8):
                kb.op("pe", "matmul",
                    pA[:, :], xm[:, kc, t * 128:(t + 1) * 128], wA[:, kc, :], start=(kc == 0), stop=(kc == 7),
                    reads=[t_xm, t_wA], writes=[t_pA])
            kb.op("act", "activation", gel[:], pA[:, :], AF.Gelu, reads=[t_pA], writes=[t_gel])
            emit_stats(kb, gel[:, 256:512], t_gel, 256, bnst, t_bnst, mv[:, 0, :], t_mv)
            emit_rstd(kb, cm, mv, t_mv, 1, rstd, t_rstd)
            kb.op("dve", "tensor_scalar", vn[:], gel[:, 256:512], mv[:, 0, 0:1], rstd[:, 0:1],
                                                   op0=ALU.subtract, op1=ALU.mult,
                  reads=[t_gel, t_mv, t_rstd], writes=[t_vn])
            kb.op("dve", "tensor_tensor", vn[:], vn[:], rowsA[:, 0, :], op=ALU.mult,
                  reads=[t_vn, t_rowsA], writes=[t_vn])
            kb.op("dve", "tensor_tensor", vln[:], vn[:], rowsA[:, 1, :], op=ALU.add,
                  reads=[t_vn, t_rowsA], writes=[t_vln])
            mx, t_mx = ringA.next()
            for g in range(4):
                kb.op("pe", "matmul",
                    mx[:, g * 64:(g + 1) * 64], wsT[:, g, :], vln[:, g * 64:(g + 1) * 64], start=True, stop=True,
                    reads=[t_wsT, t_vln], writes=[t_mx])
            kb.op("dve", "tensor_tensor",
                mixb[:].rearrange("p (g d) -> p g d", g=4), mx[:, 0:256].rearrange("p (g d) -> p g d", g=4),
                bsT[:].unsqueeze(2).to_broadcast([128, 4, 64]), op=ALU.add,
                reads=[t_mx, t_bsT], writes=[t_mixb])
            kb.op("dve", "tensor_tensor", ya[:], mixb[:], gel[:, 0:256], op=ALU.mult,
                  reads=[t_mixb, t_gel], writes=[t_ya])
            pT, t_pT = ringT.next()
            pT16 = pT[:].bitcast(BF16)
            for c in range(2):
                kb.op("pe", "transpose",
                    pT16[:, c * 128:(c + 1) * 128], ya[:, c * 128:(c + 1) * 128], cm.identb[:],
                    reads=[t_ya, cm.t_identb], writes=[t_pT])
            kb.op("act", "copy",
                yT[:, 0:2, t * 128:(t + 1) * 128], pT16[:, 0:256].rearrange("p (c t) -> p c t", c=2),
                reads=[t_pT], writes=[t_yT])
            brs = []
            for hf in range(2):
                br, t_br = ringBr.next()
                for kc in range(8):
                    kb.op("pe", "matmul",
                        br[:, :], yT[:, kc, t * 128:(t + 1) * 128], wout[:, kc, hf * 512:(hf + 1) * 512],
                        start=(kc == 0), stop=(kc == 7),
                        reads=[t_yT, t_wout], writes=[t_br])
                brs.append((br[:, :], t_br))
            emit_deepnorm(kb, cm, xt[:, t, :], t_xt, brs, rows[:, 0, :], t_rows[0], rows[:, 1, :], rows[:, 2, :],
                          t_rowsLN, tmp, t_tmp, bnst, t_bnst, mv, t_mv, rstd, t_rstd)
        kb.dma_acc(x1s[tok0:tok0 + nt * 128, :].rearrange("(t p) d -> p t d", p=128), xt[:, 0:nt, :],
                   reads=[t_xt], writes=[t_x1s], owner=t_xt)
        x2, t_x2 = x2ring.next()
        emit_modulate(kb, cm, mb, [(xt[:, t, :], t_xt) for t in range(nt)], nt, cols, t_cols, 4, 3,
                      x2, t_x2, ringT)
        kb.dma_acc(xm2Ts[:, tok0:tok0 + nt * 128].rearrange("(c p) t -> p c t", p=128), x2[:, :, 0:nt * 128],
                   reads=[t_x2], writes=[t_xm2Ts], owner=t_x2)

    hring = Ring(kb, pre + "h", [128, T], BF16, 4)
    sqring = Ring(kb, pre + "sq", [128, T], F32, 2)
    oring = yring
    ringUp = PsRing(banks, [4, 5, 6])
    ringT2 = PsRing(banks, [7])
    cur_gate[0] = -1
    load_ln(1)
    first = True
    for c0 in range(0, 16, 4):
        f = kb.dma if first else kb.dma_acc
        f(wdown_b[:, c0:c0 + 4, :], io["wdown"][(16 + c0) * 128:(20 + c0) * 128, :].rearrange("(c p) n -> p c n", p=128),
          writes=[t_wdown_b] + ([t_wA, t_wout] + t_xm_slots if first else []), queue="pool")
        first = False
    for gi, (tok0, nt, which) in enumerate(groups):
        n = nt * 128
        xt, t_xt = xring.next()
        kb.dma(xt[:, 0:nt, :], x1s[tok0:tok0 + n, :].rearrange("(t p) d -> p t d", p=128),
               reads=[t_x1s], writes=[t_xt])
        x2, t_x2 = x2ring.next()
        kb.dma(x2[:, :, 0:n], xm2Ts[:, tok0:tok0 + n].rearrange("(c p) t -> p c t", p=128),
               reads=[t_xm2Ts], writes=[t_x2])
        if which != cur_gate[0]:
            load_gates(which, 1)
            cur_gate[0] = which
        cols, t_cols = (colsL, t_colsL) if which == 0 else (colsC, t_colsC)
        acc = [[banks[2 * t + hf] for hf in range(2)] for t in range(nt)]

        def up(j):
            ps, t_ps = ringUp.next()
            for kc in range(8):
                kb.op("pe", "matmul",
                    ps[:, 0:n], wup[:, kc, j * 128:(j + 1) * 128], x2[:, kc, 0:n], start=(kc == 0), stop=(kc == 7),
                    reads=[t_wup, t_x2], writes=[t_ps])
            sq, t_sq = sqring.next()
            kb.op("act", "activation", sq[:, 0:n], ps[:, 0:n], AF.Square,
                  reads=[t_ps], writes=[t_sq])
            hT, t_hT = hring.next()
            kb.op("dve", "scalar_tensor_tensor",
                hT[:, 0:n], ps[:, 0:n], 0.0, sq[:, 0:n], op0=ALU.is_gt, op1=ALU.mult,
                reads=[t_ps, t_sq], writes=[t_hT])
            return hT, t_hT

        def down(j, hT, t_hT):
            for t in range(nt):
                for hf in range(2):
                    a, t_a = acc[t][hf]
                    wd, t_wd, jj = (wdown_a, t_wdown_a, j) if j < 16 else (wdown_b, t_wdown_b, j - 16)
                    kb.op("pe", "matmul",
                        a[:, :], hT[:, t * 128:(t + 1) * 128], wd[:, jj, hf * 512:(hf + 1) * 512],
                        start=(j == 0), stop=(j == 31),
                        reads=[t_hT, t_wd], writes=[t_a])
        q = [up(0), up(1)]
        for j in range(32):
            if j + 2 < 32:
                q.append(up(j + 2))
            down(j, *q.pop(0))
        for t in range(nt):
            brs = [(acc[t][hf][0][:, :], acc[t][hf][1]) for hf in range(2)]
            emit_deepnorm(kb, cm, xt[:, t, :], t_xt, brs, rows[:, 0, :], t_rows[0], rows[:, 1, :], rows[:, 2, :],
                          t_rowsLN, tmp, t_tmp, bnst, t_bnst, mv, t_mv, rstd, t_rstd)
        kb.dma_acc(x_out[tok0:tok0 + n, :].rearrange("(t p) d -> p t d", p=128), xt[:, 0:nt, :],
                   reads=[t_xt], writes=[t_xout], owner=t_xt)
        if not last:
            o, t_o = oring.next()
            emit_modulate(kb, cm, mb, [(xt[:, t, :], t_xt) for t in range(nt)], nt, cols, t_cols, 7, 6,
                          o, t_o, ringT2)
            kb.dma_acc(xmT_next[:, tok0:tok0 + n].rearrange("(c p) t -> p c t", p=128), o[:, :, 0:n],
                       reads=[t_o], writes=[t_xmTn], owner=t_o)


HD = 64
NTILE = NTOK // 128
LAT_TILES = SEQ // 128


def rope_tables():
    half = 32
    inv = (1.0 / (10000.0 ** (np.arange(0, half, 2, dtype=np.float32) / np.float32(half)))).astype(np.float32)
    t = np.arange(SEQ)
    ar = (t // 64).astype(np.float32)[:, None] * inv[None, :]
    ac = (t % 64).astype(np.float32)[:, None] * inv[None, :]
    C = np.concatenate([np.cos(ar), np.cos(ar), np.cos(ac), np.cos(ac)], 1)
    S = np.concatenate([-np.sin(ar), np.sin(ar), -np.sin(ac), np.sin(ac)], 1)
    return np.ascontiguousarray(C, np.float32), np.ascontiguousarray(S, np.float32)


class MState:
    pass


def load_xm_block(kb, io, xm, t_xm, tok0, n):
    pieces = io["xmT_pieces"](tok0, n)
    for k, (c0, ncn, o0, nn, ap) in enumerate(pieces):
        (kb.dma if k == 0 else kb.dma_acc)(xm[:, c0:c0 + ncn, o0:o0 + nn], ap.rearrange("(c p) t -> p c t", p=128),
                                           reads=[io.get("t_xmT_full")], writes=[t_xm])


def emit_PM_setup(kb, cm, io, layer):
    st = MState()
    pre = f"m{layer}_"
    st.pre = pre
    st.qT = kb.sbuf(pre + "qT", [64, 3, NTOK], BF16)
    st.t_qT = [Tk(pre + f"qT{b}") for b in range(9)]
    st.kT = kb.sbuf(pre + "kT", [64, NTOK], BF16)
    st.t_kT = [Tk(pre + f"kT{b}") for b in range(9)]
    st.vaug = kb.sbuf(pre + "vaug", [128, NTILE, 65], BF16)
    st.t_vaug = [Tk(pre + f"va{m}") for m in range(NTILE)]
    st.t_vones = Tk(pre + "vones")
    kb.op("pool", "memset", st.vaug[:, :, 64:65], 1.0, writes=[st.t_vones])
    st.szb = kb.sbuf(pre + "szb", [128, NTILE, 192], BF16)
    st.t_szb = [Tk(pre + f"szb{m}") for m in range(NTILE)]
    st.gb = kb.sbuf(pre + "gb", [128, NTILE, 12], F32)
    st.t_gb = [Tk(pre + f"gb{m}") for m in range(NTILE)]
    st.qkvT = kb.sbuf(pre + "qkvT", [128, 5, NTOK], BF16)
    st.t_qkvT = [Tk(pre + f"qkvT{b}") for b in range(9)]
    st.O = kb.sbuf(pre + "O", [128, NTILE, 192], F32)
    st.t_O = [Tk(pre + f"O{m}") for m in range(NTILE)]
    st.k1z = kb.sbuf(pre + "k1z", [128, NTOK], BF16)
    st.t_k1z0 = Tk(pre + "k1z0")
    kb.op("pool", "memset", st.k1z[0:64, :], 0.0, writes=[st.t_k1z0])
    emit_PM_masks(kb, cm, st)
    return st


def emit_PM_a(kb, cm, io, st, layer, banks):
    pre = st.pre + "a_"
    wB = kb.sbuf(pre + "wB", [128, 8, 320], BF16); t_wB = Tk(pre + "wB")
    wCt = kb.sbuf(pre + "wCt", [128, 8, 204], BF16); t_wCt = Tk(pre + "wCt")
    load_weight_bf16(kb, wB, t_wB, io["wB"], 8, per=8)
    load_weight_bf16(kb, wCt, t_wCt, io["wCt"], 8, per=8)
    grow = kb.sbuf(pre + "grow", [128, 4, 64], F32); t_grow = Tk(pre + "grow")
    for hh in range(4):
        (kb.dma if hh == 0 else kb.dma_acc)(grow[:, hh, :], io["attn_q_g" if hh < 3 else "attn_k_g"].to_broadcast([128, 64]),
                                             writes=[t_grow])
    brow = kb.sbuf(pre + "brow", [128, 12], F32); t_brow = Tk(pre + "brow")
    kb.op("pool", "memset", brow[:, 6:12], 0.0, writes=[t_brow])
    kb.dma_acc(brow[:, 0:6], io["dn_dt_bias"].to_broadcast([128, 6]), writes=[t_brow])
    nea = kb.sbuf(pre + "nea", [128, 6], F32); t_nea = Tk(pre + "nea")
    kb.dma(nea[:], io["dn_a_log"].to_broadcast([128, 6]), writes=[t_nea])
    kb.op("act", "activation", nea[:], nea[:], AF.Exp, reads=[t_nea], writes=[t_nea])
    kb.op("dve", "tensor_scalar", nea[:], nea[:], -1.0, None, op0=ALU.mult, reads=[t_nea], writes=[t_nea])

    xring = Ring(kb, pre + "xm", [128, 8, 512], BF16, 2)
    cring = Ring(kb, pre + "ctab", [128, 4, 64], F32, 2)
    sring = Ring(kb, pre + "stab", [128, 4, 64], F32, 2)
    sq = kb.sbuf(pre + "sq", [128, 256], F32); t_sq = Tk(pre + "sq")
    ssq = kb.sbuf(pre + "ssq", [128, 4], F32); t_ssq = Tk(pre + "ssq")
    qn = kb.sbuf(pre + "qn", [128, 256], F32); t_qn = Tk(pre + "qn")
    t1 = kb.sbuf(pre + "t1", [128, 256], F32); t_t1 = Tk(pre + "t1")
    t2 = kb.sbuf(pre + "t2", [128, 256], F32); t_t2 = Tk(pre + "t2")
    qrr = Ring(kb, pre + "qr", [128, 256], F32, 2)
    ez = kb.sbuf(pre + "ez", [128, 192], F32); t_ez = Tk(pre + "ez")
    ab = kb.sbuf(pre + "ab", [128, 12], F32); t_ab = Tk(pre + "ab")
    e1 = kb.sbuf(pre + "e1", [128, 12], F32); t_e1 = Tk(pre + "e1")
    ringB = PsRing(banks, [0, 1])
    ringC = PsRing(banks, [2, 3])
    tp = [banks[4 + hh] for hh in range(4)]
    flip = 0
    blocks = [(b * 512, 4) for b in range(8)] + [(SEQ, 2)]
    for bi, (tok0, nt) in enumerate(blocks):
        n = nt * 128
        xm, t_xm = xring.next()
        load_xm_block(kb, io, xm, t_xm, tok0, n)
        lat = tok0 < SEQ
        if lat:
            ct, t_ct = cring.next()
            stb, t_stb = sring.next()
            kb.dma(ct[:], io["ropeC"][tok0:tok0 + 512, :].rearrange("(t p) f -> p t f", p=128), writes=[t_ct])
            kb.dma(stb[:], io["ropeS"][tok0:tok0 + 512, :].rearrange("(t p) f -> p t f", p=128), writes=[t_stb])
        for t in range(nt):
            m = tok0 // 128 + t
            pB, t_pB = ringB.next()
            pC, t_pC = ringC.next()
            for kc in range(8):
                kb.op("pe", "matmul", pB[:, 0:320], xm[:, kc, t * 128:(t + 1) * 128], wB[:, kc, :],
                      start=(kc == 0), stop=(kc == 7), reads=[t_xm, t_wB], writes=[t_pB])
            for kc in range(8):
                kb.op("pe", "matmul", pC[:, 0:204], xm[:, kc, t * 128:(t + 1) * 128], wCt[:, kc, :],
                      start=(kc == 0), stop=(kc == 7), reads=[t_xm, t_wCt], writes=[t_pC])
            kb.op("act", "activation", sq[:], pB[:, 0:256], AF.Square, reads=[t_pB], writes=[t_sq])
            kb.op("dve", "tensor_reduce", ssq[:], sq[:].rearrange("p (h d) -> p h d", h=4), axis=AX.X, op=ALU.add,
                  reads=[t_sq], writes=[t_ssq])
            kb.op("dve", "tensor_scalar", ssq[:], ssq[:], 1.0 / 64.0, EPS, op0=ALU.mult, op1=ALU.add,
                  reads=[t_ssq], writes=[t_ssq])
            kb.op("pool", "tensor_tensor", ssq[:], ssq[:], cm.mhalf[:, 0:4], op=ALU.pow,
                  reads=[t_ssq, cm.t_mhalf], writes=[t_ssq])
            qn3 = qn[:].rearrange("p (h d) -> p h d", h=4)
            kb.op("dve", "tensor_tensor", qn3, pB[:, 0:256].rearrange("p (h d) -> p h d", h=4),
                  ssq[:].unsqueeze(2).to_broadcast([128, 4, 64]), op=ALU.mult, reads=[t_pB, t_ssq], writes=[t_qn])
            qr, t_qr = qrr.next()
            if lat:
                kb.op("dve", "tensor_tensor", qn3, qn3, grow[:], op=ALU.mult, reads=[t_qn, t_grow], writes=[t_qn])
                kb.op("dve", "tensor_tensor", t1[:].rearrange("p (h d) -> p h d", h=4), qn3,
                      ct[:, t, :].unsqueeze(1).to_broadcast([128, 4, 64]), op=ALU.mult,
                      reads=[t_qn, t_ct], writes=[t_t1])
                qn5 = qn[:].rearrange("p (h r a f) -> p h r a f", h=4, r=2, a=2)
                t25 = t2[:].rearrange("p (h r a f) -> p h r a f", h=4, r=2, a=2)
                st4 = stb[:, t, :].rearrange("p (r a f) -> p r a f", r=2, a=2)
                for a in range(2):
                    kb.op("pool", "tensor_tensor", t25[:, :, :, a, :], qn5[:, :, :, 1 - a, :],
                          st4[:, :, a, :].unsqueeze(1).to_broadcast([128, 4, 2, 16]), op=ALU.mult,
                          reads=[t_qn, t_stb], writes=[t_t2])
                kb.op("dve", "tensor_tensor", qr[:], t1[:], t2[:], op=ALU.add, reads=[t_t1, t_t2], writes=[t_qr])
            else:
                kb.op("dve", "tensor_tensor", qr[:].rearrange("p (h d) -> p h d", h=4), qn3, grow[:], op=ALU.mult,
                      reads=[t_qn, t_grow], writes=[t_qr])
            for hh in range(4):
                kb.op("pe", "transpose", tp[hh][0][0:64, t * 128:(t + 1) * 128], qr[:, hh * 64:(hh + 1) * 64],
                      cm.ident[:], reads=[t_qr, cm.t_ident], writes=[tp[hh][1]])
            kb.op("act", "copy", st.vaug[:, m, 0:64], pB[:, 256:320], reads=[t_pB, st.t_vones], writes=[st.t_vaug[m]])
            kb.op("act", "activation", ez[:], pC[:, 0:192], AF.Exp, scale=-1.0, reads=[t_pC], writes=[t_ez])
            kb.op("dve", "tensor_scalar", ez[:], ez[:], 1.0, None, op0=ALU.add, reads=[t_ez], writes=[t_ez])
            kb.op("dve", "reciprocal", ez[:], ez[:], reads=[t_ez], writes=[t_ez])
            kb.op("dve", "tensor_tensor", st.szb[:, m, :], pC[:, 0:192], ez[:], op=ALU.mult,
                  reads=[t_pC, t_ez], writes=[st.t_szb[m]])
            kb.op("dve", "tensor_tensor", ab[:], pC[:, 192:204], brow[:], op=ALU.add, reads=[t_pC, t_brow], writes=[t_ab])
            kb.op("act", "activation", e1[:, 0:6], ab[:, 0:6], AF.Exp, reads=[t_ab], writes=[t_e1])
            kb.op("act", "activation", e1[:, 0:6], e1[:, 0:6], AF.Ln, bias=1.0, reads=[t_e1], writes=[t_e1])
            kb.op("act", "activation", e1[:, 6:12], ab[:, 6:12], AF.Exp, scale=-1.0, reads=[t_ab, t_e1], writes=[t_e1])
            kb.op("dve", "tensor_tensor", st.gb[:, m, 0:6], e1[:, 0:6], nea[:], op=ALU.mult,
                  reads=[t_e1, t_nea], writes=[st.t_gb[m]])
            kb.op("dve", "tensor_scalar", e1[:, 6:12], e1[:, 6:12], 1.0, None, op0=ALU.add, reads=[t_e1], writes=[t_e1])
            kb.op("dve", "reciprocal", st.gb[:, m, 6:12], e1[:, 6:12], reads=[t_e1], writes=[st.t_gb[m]])
        for hh in range(4):
            dst = st.qT[:, hh, tok0:tok0 + n] if hh < 3 else st.kT[:, tok0:tok0 + n]
            t_dst = st.t_qT[bi] if hh < 3 else st.t_kT[bi]
            flip ^= 1
            if flip:
                kb.op("act", "copy", dst, tp[hh][0][0:64, 0:n], reads=[tp[hh][1]], writes=[t_dst])
            else:
                kb.op("dve", "tensor_copy", dst, tp[hh][0][0:64, 0:n], reads=[tp[hh][1]], writes=[t_dst])


def emit_PM_attn(kb, cm, io, st, layer, last, banks):
    pre = st.pre + "e_"
    yM = io["yM"]
    t_yM = io["t_yM"]
    pring = Ring(kb, pre + "pT", [128, 512], BF16, 4)
    rec = kb.sbuf(pre + "rec", [65, 512], F32); t_rec = Tk(pre + "rec")
    ones = kb.sbuf(pre + "ones", [65, 64], F32); t_ones = Tk(pre + "ones")
    kb.op("pool", "memset", ones[:], 1.0, writes=[t_ones])
    bcs = kb.sbuf(pre + "bcs", [64, 512], F32); t_bcs = Tk(pre + "bcs")
    oring = Ring(kb, pre + "o", [64, 512], BF16, 2)
    ringS = PsRing(banks, [0, 1, 2])
    ringO = PsRing(banks, [3, 4])
    bc_ps, t_bc = banks[5]
    jobs = []
    for j in range(3):
        for qb in range(8):
            jobs.append((j, qb * 512, 512, list(range(NTILE))))
        if not last:
            jobs.append((j, SEQ, 256, [32, 33]))
    for (j, q0, nq, kcs) in jobs:
        o_ps, t_o = ringO.next()
        qblk = q0 // 512
        t_q = st.t_qT[qblk]

        def score(kc):
            s_ps, t_s = ringS.next()
            kb.op("pe", "matmul", s_ps[:, 0:nq], st.kT[:, kc * 128:(kc + 1) * 128], st.qT[:, j, q0:q0 + nq],
                  start=True, stop=True, reads=[st.t_kT[kc // 4], t_q], writes=[t_s])
            pT, t_pT = pring.next()
            kb.op("act", "activation", pT[:, 0:nq], s_ps[:, 0:nq], AF.Exp, scale=0.125, reads=[t_s], writes=[t_pT])
            return pT, t_pT
        cur = score(kcs[0])
        for i, kc in enumerate(kcs):
            nxt = score(kcs[i + 1]) if i + 1 < len(kcs) else None
            pT, t_pT = cur
            kb.op("pe", "matmul", o_ps[0:65, 0:nq], st.vaug[:, kc, :], pT[:, 0:nq],
                  start=(i == 0), stop=(i == len(kcs) - 1), reads=[st.t_vaug[kc], t_pT], writes=[t_o])
            cur = nxt
        kb.op("dve", "reciprocal", rec[64:65, 0:nq], o_ps[64:65, 0:nq], reads=[t_o], writes=[t_rec])
        kb.op("pe", "matmul", bc_ps[0:64, 0:nq], ones[64:65, :], rec[64:65, 0:nq], start=True, stop=True,
              reads=[t_ones, t_rec], writes=[t_bc])
        kb.op("act", "copy", bcs[:, 0:nq], bc_ps[0:64, 0:nq], reads=[t_bc], writes=[t_bcs])
        ob, t_ob = oring.next()
        kb.op("dve", "tensor_tensor", ob[:, 0:nq], o_ps[0:64, 0:nq], bcs[:, 0:nq], op=ALU.mult,
              reads=[t_o, t_bcs], writes=[t_ob])
        kb.dma_acc(yM[j * 64:(j + 1) * 64, q0:q0 + nq], ob[:, 0:nq], reads=[t_ob], writes=[t_yM], owner=t_ob)


BIG = 30000.0
NSTAGES = [100]
GRAM = [3]
GJ = [0, 1, 2]


def fence(dst_tks, src_tks):
    ev = {}
    for t in src_tks:
        for d in (t.w, t.r):
            for s, v in d.items():
                if ev.get(s, 0) < v:
                    ev[s] = v
    for t in dst_tks:
        for s, v in ev.items():
            if t.r.get(s, 0) < v:
                t.r[s] = v


def emit_PM_masks(kb, cm, st):
    pre = st.pre + "k_"
    names = ["Blk", "U", "UT", "N1_0", "N1_1", "M2_0", "M2_1", "onesf"]
    st.mk = kb.sbuf(pre + "masks", [128, len(names), 128], F32)
    st.t_mk = Tk(pre + "masks")
    mk = st.mk
    ix = {n: k for k, n in enumerate(names)}
    st.mix = ix
    t = st.t_mk
    kb.op("pool", "memset", mk[:, ix["Blk"], :], 0.0, writes=[t])
    kb.op("pool", "memset", mk[0:64, ix["Blk"], 0:64], 1.0, reads=[t], writes=[t])
    kb.op("pool", "memset", mk[64:128, ix["Blk"], 64:128], 1.0, reads=[t], writes=[t])
    kb.op("pool", "memset", mk[:, ix["onesf"], :], 1.0, reads=[t], writes=[t])
    blk = mk[:, ix["Blk"], :]
    kb.op("pool", "affine_select", mk[:, ix["U"], :], blk, pattern=[[1, 128]], compare_op=ALU.is_ge, fill=0.0,
          base=0, channel_multiplier=-1, reads=[t], writes=[t])
    kb.op("pool", "affine_select", mk[:, ix["UT"], :], blk, pattern=[[-1, 128]], compare_op=ALU.is_ge, fill=0.0,
          base=0, channel_multiplier=1, reads=[t], writes=[t])
    kb.op("pool", "affine_select", mk[:, ix["N1_0"], :], blk, pattern=[[-1, 128]], compare_op=ALU.is_gt, fill=0.0,
          base=0, channel_multiplier=1, reads=[t], writes=[t])
    kb.op("pool", "affine_select", mk[:, ix["N1_1"], :], blk, pattern=[[1, 128]], compare_op=ALU.is_gt, fill=0.0,
          base=0, channel_multiplier=-1, reads=[t], writes=[t])
    for n in ("N1_0", "N1_1"):
        kb.op("dve", "tensor_scalar", mk[:, ix[n], :], mk[:, ix[n], :], -1.0, -BIG, op0=ALU.add, op1=ALU.mult,
              reads=[t], writes=[t])
    kb.op("dve", "tensor_scalar", mk[:, ix["M2_0"], :], mk[:, ix["U"], :], -1.0, BIG, op0=ALU.add, op1=ALU.mult,
          reads=[t], writes=[t])
    kb.op("dve", "tensor_scalar", mk[:, ix["M2_1"], :], mk[:, ix["UT"], :], -1.0, BIG, op0=ALU.add, op1=ALU.mult,
          reads=[t], writes=[t])


def emit_PM_b(kb, cm, io, st, layer, banks, xring, raw_slots, t_raw):
    pre = st.pre + "b_"
    wCf = kb.sbuf(pre + "wCf", [128, 8, 640], BF16); t_wCf = Tk(pre + "wCf")
    load_weight_bf16(kb, wCf, t_wCf, io["wCf"], 8, per=8)
    cw = kb.sbuf(pre + "cw", [128, 5, 5], F32); t_cw = Tk(pre + "cw")
    kb.dma(cw[:], io["convw"], writes=[t_cw])
    kb.op("pool", "memset", st.qkvT[0:64, 4, :], 0.0, writes=st.t_qkvT)
    acc = kb.sbuf(pre + "acc", [128, 512], F32); t_acc = Tk(pre + "acc")
    cs = kb.sbuf(pre + "cs", [128, 512], F32); t_cs = Tk(pre + "cs")
    sqv = kb.sbuf(pre + "sqv", [128, 512], F32); t_sqv = Tk(pre + "sqv")
    rn = kb.sbuf(pre + "rn", [128, 512], F32); t_rn = Tk(pre + "rn")
    ex = kb.sbuf(pre + "ex", [128, 512], F32); t_ex = Tk(pre + "ex")
    ringP = PsRing(banks, [0, 1, 2, 3])
    ringN = PsRing(banks, [4, 5])
    blk = st.mk[:, st.mix["Blk"], :]
    blocks = [(b * 512, 512) for b in range(8)] + [(SEQ, 256)]
    flip = [0]

    def inproj(bi):
        tok0, n = blocks[bi]
        raw, t_r = raw_slots[bi % 3], t_raw[bi % 3]
        xm, t_xm = xring.next()
        load_xm_block(kb, io, xm, t_xm, tok0, n)
        for i in range(5):
            ps, t_ps = ringP.next()
            for kc in range(8):
                kb.op("pe", "matmul", ps[:, 0:n], wCf[:, kc, i * 128:(i + 1) * 128], xm[:, kc, 0:n],
                      start=(kc == 0), stop=(kc == 7), reads=[t_wCf, t_xm], writes=[t_ps])
            flip[0] ^= 1
            if flip[0]:
                kb.op("act", "copy", raw[:, i, 2:2 + n], ps[:, 0:n], reads=[t_ps], writes=[t_r])
            else:
                kb.op("dve", "tensor_copy", raw[:, i, 2:2 + n], ps[:, 0:n], reads=[t_ps], writes=[t_r])
        seg_start = tok0 in (0, SEQ)
        if seg_start:
            kb.op("pool", "memset", raw[:, :, 0:2], 0.0, reads=[t_r], writes=[t_r])
        else:
            pr, t_pr = raw_slots[(bi - 1) % 3], t_raw[(bi - 1) % 3]
            npv = blocks[bi - 1][1]
            kb.op("pool", "tensor_copy", raw[:, :, 0:2], pr[:, :, npv:npv + 2], reads=[t_pr, t_r], writes=[t_r])
            kb.op("pool", "tensor_copy", pr[:, :, 2 + npv:4 + npv], raw[:, :, 2:4], reads=[t_r, t_pr], writes=[t_pr])
        seg_end = (tok0 + n) in (SEQ, NTOK)
        if seg_end:
            kb.op("pool", "memset", raw[:, :, 2 + n:4 + n], 0.0, reads=[t_r], writes=[t_r])

    def conv(bi):
        tok0, n = blocks[bi]
        raw, t_r = raw_slots[bi % 3], t_raw[bi % 3]
        t_out = st.t_qkvT[bi]
        for i in range(5):
            kb.op("dve", "tensor_scalar", acc[:, 0:n], raw[:, i, 0:n], cw[:, i, 0:1], None, op0=ALU.mult,
                  reads=[t_r, t_cw], writes=[t_acc])
            for j in range(1, 5):
                kb.op("dve", "scalar_tensor_tensor", acc[:, 0:n], raw[:, i, j:j + n], cw[:, i, j:j + 1], acc[:, 0:n],
                      op0=ALU.mult, op1=ALU.add, reads=[t_r, t_cw, t_acc], writes=[t_acc])
            kb.op("act", "activation", ex[:, 0:n], acc[:, 0:n], AF.Exp, scale=-1.0, reads=[t_acc], writes=[t_ex])
            kb.op("dve", "tensor_scalar", ex[:, 0:n], ex[:, 0:n], 1.0, None, op0=ALU.add, reads=[t_ex], writes=[t_ex])
            kb.op("dve", "reciprocal", ex[:, 0:n], ex[:, 0:n], reads=[t_ex], writes=[t_ex])
            kb.op("dve", "tensor_tensor", cs[:, 0:n], acc[:, 0:n], ex[:, 0:n], op=ALU.mult, reads=[t_acc, t_ex], writes=[t_cs])
            nr = 128 if i < 2 else (64 if i < 4 else 0)
            if nr:
                kb.op("act", "activation", sqv[0:nr, 0:n], cs[0:nr, 0:n], AF.Square, reads=[t_cs], writes=[t_sqv])
                ps, t_ps = ringN.next()
                kb.op("pe", "matmul", ps[0:nr, 0:n], blk[0:nr, 0:nr], sqv[0:nr, 0:n], start=True, stop=True,
                      reads=[st.t_mk, t_sqv], writes=[t_ps])
                kb.op("act", "activation", rn[0:nr, 0:n], ps[0:nr, 0:n], AF.Ln, bias=EPS, reads=[t_ps], writes=[t_rn])
                kb.op("act", "activation", rn[0:nr, 0:n], rn[0:nr, 0:n], AF.Exp, scale=-0.5, reads=[t_rn], writes=[t_rn])
                if i in (0, 2):
                    kb.op("dve", "scalar_tensor_tensor", st.qkvT[0:nr, i, tok0:tok0 + n], cs[0:nr, 0:n], 0.125,
                          rn[0:nr, 0:n], op0=ALU.mult, op1=ALU.mult, reads=[t_cs, t_rn], writes=[t_out])
                else:
                    kb.op("dve", "tensor_tensor", st.qkvT[0:nr, i, tok0:tok0 + n], cs[0:nr, 0:n], rn[0:nr, 0:n],
                          op=ALU.mult, reads=[t_cs, t_rn], writes=[t_out])
            if nr < 128:
                kb.op("pool", "tensor_copy", st.qkvT[64:128, i, tok0:tok0 + n], cs[64:128, 0:n],
                      reads=[t_cs], writes=[t_out])
            if i == 1:
                kb.op("pool", "tensor_copy", st.k1z[64:128, tok0:tok0 + n], st.qkvT[64:128, 1, tok0:tok0 + n],
                      reads=[t_out, st.t_k1z0], writes=[t_out])
    inproj(0)
    for bi in range(len(blocks)):
        if bi + 1 < len(blocks):
            inproj(bi + 1)
        conv(bi)


def emit_PM_c(kb, cm, io, st, layer, banks, noscan=False):
    pre = st.pre + "c_"
    mk, ix = st.mk, st.mix
    t_mk = st.t_mk
    M = lambda n: mk[:, ix[n], :]
    S2 = [kb.sbuf(pre + f"S{r}", [128, 3, 64], F32) for r in range(2)]
    t_S2 = [Tk(pre + f"S{r}") for r in range(2)]
    W = 384

    def buf(name, shape=(128, 3, 128), n=1, dt=F32):
        return Ring(kb, pre + name, list(shape), dt, n)
    r_sm = buf("sm", (128, 8, 3), 2)
    r_gsel = buf("gsel", (128, 2, 3), 2)
    r_egl = buf("egl", (128, 2, 3), 4)
    r_gbc = buf("gbc", n=2)
    r_dm = buf("dm", n=2)
    r_dmT = buf("dmT", n=2)
    r_egr = buf("egr", n=2)
    r_X = buf("X", n=4)
    r_XT = buf("XT", n=4)
    r_TT = buf("TT", n=2)
    r_vb = buf("vb", (128, 3, 64), 2, BF16)
    r_kbgA = buf("kbgA", (128, 2, 64), 2, BF16)
    r_kbgB = buf("kbgB", (128, 64), 2, BF16)
    r_TTb = buf("TTb", n=2, dt=BF16)
    r_kdA = buf("kdA", (128, 2, 2, 64), 4, BF16)
    r_kdB = buf("kdB", (128, 2, 64), 4, BF16)
    r_ekdc = buf("ekdc", (128, 2, 3), 2)
    r_u = buf("u", (128, 3, 64), 4)
    r_wT = buf("wT", n=4, dt=BF16)
    r_qdT = buf("qdT", n=4, dt=BF16)
    r_iT = buf("iT", n=4, dt=BF16)
    vnew2 = [kb.sbuf(pre + f"vnew{r}", [128, 3, 64], BF16) for r in range(2)]
    t_vnew2 = [Tk(pre + f"vnew{r}") for r in range(2)]
    Sb2 = [kb.sbuf(pre + f"Sb{r}", [128, 3, 64], BF16) for r in range(2)]
    t_Sb2 = [Tk(pre + f"Sb{r}") for r in range(2)]
    for r in range(2):
        kb.op("pool", "memset", vnew2[r][:], 0.0, writes=[t_vnew2[r]])
        kb.op("pool", "memset", S2[r][:], 0.0, writes=[t_S2[r]])
        kb.op("pool", "memset", Sb2[r][:], 0.0, writes=[t_Sb2[r]])
    for k in range(4):
        kb.op("pool", "memset", r_wT.t[k][:], 0.0, writes=[r_wT.tk[k]])
        kb.op("pool", "memset", r_qdT.t[k][:], 0.0, writes=[r_qdT.tk[k]])
    ringG = PsRing(banks, [4, 5, 6, 7])
    b_ws, t_ws = banks[0]
    b_o, t_o = banks[1]
    b_sn, t_sn = banks[2]
    b_misc, t_misc = banks[3]
    t_gcg = t_eglp = t_tk = t_ups = t_misc
    tk16 = b_misc[:].bitcast(BF16)
    flip = [0]

    def evac(dst, src, reads, writes):
        flip[0] ^= 1
        if flip[0]:
            kb.op("act", "copy", dst, src, reads=reads, writes=writes)
        else:
            kb.op("dve", "tensor_copy", dst, src, reads=reads, writes=writes)

    def qk_views(m):
        c0 = m * 128
        blk_i = min(m // 4, 8)
        tq = st.t_qkvT[blk_i]
        kTj = [st.qkvT[0:64, 1, c0:c0 + 128], st.qkvT[64:128, 1, c0:c0 + 128], st.qkvT[0:64, 3, c0:c0 + 128]]
        qTj = [st.qkvT[0:64, 0, c0:c0 + 128], st.qkvT[64:128, 0, c0:c0 + 128], st.qkvT[0:64, 2, c0:c0 + 128]]
        st_kL = [st.qkvT[0:64, 1, c0:c0 + 128], st.k1z[:, c0:c0 + 128], st.qkvT[0:64, 3, c0:c0 + 128]]
        st_kR = [st.qkvT[0:64, 1, c0:c0 + 128], st.qkvT[:, 1, c0:c0 + 128], st.qkvT[0:64, 3, c0:c0 + 128]]
        st_qR = [st.qkvT[0:64, 0, c0:c0 + 128], st.qkvT[:, 0, c0:c0 + 128], st.qkvT[0:64, 2, c0:c0 + 128]]
        qk_ops[0] = (st_kL, st_kR, st_qR)
        return tq, kTj, qTj

    rows = [(0, 64), (64, 128), (0, 64)]
    qk_ops = [None]

    def prep(m, r):
        tq, kTj, qTj = qk_views(m)
        kL, kR, qR = qk_ops[0]
        c0 = m * 128
        gr = st.gb[:, m, 3 * r:3 * r + 3]
        br = st.gb[:, m, 6 + 3 * r:9 + 3 * r]
        t_gb = st.t_gb[m]
        Tri = M("U") if r == 0 else M("UT")
        N1 = M("N1_0") if r == 0 else M("N1_1")
        M2 = M("M2_0") if r == 0 else M("M2_1")
        sm, t_sm = r_sm.next()
        gcs, ngc, ekd, egc, bgc, nbeta = (sm[:, k, :] for k in range(6))
        gsel, t_gsel = r_gsel.next()
        egl, t_egl = r_egl.next()
        gbc, t_gbc = r_gbc.next()
        dm, t_dm = r_dm.next()
        dmT, t_dmT = r_dmT.next()
        egr, t_egr = r_egr.next()
        TT, t_TT = r_TT.next()
        vb, t_vb = r_vb.next()
        kbgA, t_kbgA = r_kbgA.next()
        kbgB, t_kbgB = r_kbgB.next()
        kdA, t_kdA = r_kdA.next()
        kdB, t_kdB = r_kdB.next()
        ekdc, t_ekdc = r_ekdc.next()
        u, t_u = r_u.next()
        wT, t_wT = r_wT.next()
        qdT, t_qdT = r_qdT.next()
        iT, t_iT = r_iT.next()
        stages = []

        def s_small():
            kb.op("pe", "matmul", b_misc[:, 448:451], Tri, gr, start=True, stop=True, reads=[t_mk, t_gb], writes=[t_gcg])
            kb.op("pe", "matmul", b_misc[:, 451:454], M("Blk"), gr, start=True, stop=True, reads=[t_mk, t_gb], writes=[t_gcg])
            kb.op("dve", "tensor_copy", gcs, b_misc[:, 448:451], reads=[t_gcg], writes=[t_sm])
            kb.op("dve", "tensor_scalar", ngc, gcs, -1.0, None, op0=ALU.mult, reads=[t_sm], writes=[t_sm])
            kb.op("dve", "tensor_tensor", ekd, b_misc[:, 451:454], gcs, op=ALU.subtract, reads=[t_gcg, t_sm], writes=[t_sm])
            kb.op("act", "activation", ekd, ekd, AF.Exp, reads=[t_sm], writes=[t_sm])
            kb.op("act", "activation", egc, gcs, AF.Exp, reads=[t_sm], writes=[t_sm])
            kb.op("dve", "tensor_tensor", bgc, br, egc, op=ALU.mult, reads=[t_gb, t_sm], writes=[t_sm])
            kb.op("dve", "tensor_scalar", nbeta, br, -1.0, None, op0=ALU.mult, reads=[t_gb, t_sm], writes=[t_sm])
            cmask = M("Blk")[:, 0:128:64]
            kb.op("dve", "tensor_tensor", ekdc[:], ekd.unsqueeze(1).to_broadcast([128, 2, 3]),
                  cmask.unsqueeze(2).to_broadcast([128, 2, 3]), op=ALU.mult, reads=[t_sm, t_mk], writes=[t_ekdc])
            kb.op("dve", "tensor_tensor", gsel[:], gr.unsqueeze(1).to_broadcast([128, 2, 3]),
                  cmask.unsqueeze(2).to_broadcast([128, 2, 3]), op=ALU.mult, reads=[t_gb, t_mk], writes=[t_gsel])
            kb.op("pe", "matmul", b_misc[:, 456:462], M("onesf"), gsel[:].rearrange("p c j -> p (c j)"),
                  start=True, stop=True, reads=[t_mk, t_gsel], writes=[t_eglp])
            kb.op("act", "activation", egl[:].rearrange("p c j -> p (c j)"), b_misc[:, 456:462], AF.Exp,
                  reads=[t_eglp], writes=[t_egl])
            kb.op("pool", "tensor_copy", gbc[:], gr.unsqueeze(2).to_broadcast([128, 3, 128]), reads=[t_gb], writes=[t_gbc])
        stages.append(s_small)

        def s_tok():
            for bpos, ti in enumerate((1, 3, 2, 4)):
                kb.op("pe", "transpose", tk16[:, bpos * 128:(bpos + 1) * 128], st.qkvT[:, ti, c0:c0 + 128], cm.identb[:],
                      reads=[tq, cm.t_identb], writes=[t_tk])
            kA = tk16[:, 0:128].rearrange("p (j d) -> p j d", j=2)
            kB = tk16[:, 128:192]
            vv = tk16[:, 128:512].rearrange("p (j x) -> p j x", j=3)[:, :, 64:128]
            kb.op("dve", "tensor_tensor", vb[:], vv, br.unsqueeze(2).to_broadcast([128, 3, 64]), op=ALU.mult,
                  reads=[t_tk, t_gb], writes=[t_vb])
            kb.op("dve", "tensor_tensor", kbgA[:], kA, bgc[:, 0:2].unsqueeze(2).to_broadcast([128, 2, 64]), op=ALU.mult,
                  reads=[t_tk, t_sm], writes=[t_kbgA])
            kb.op("dve", "tensor_scalar", kbgB[:], kB, bgc[:, 2:3], None, op0=ALU.mult, reads=[t_tk, t_sm], writes=[t_kbgB])
            for c in range(2):
                kb.op("dve", "tensor_tensor", kdA[:, c, :, :], kA, ekdc[:, c, 0:2].unsqueeze(2).to_broadcast([128, 2, 64]),
                      op=ALU.mult, reads=[t_tk, t_ekdc], writes=[t_kdA])
                kb.op("dve", "tensor_scalar", kdB[:, c, :], kB, ekdc[:, c, 2:3], None, op0=ALU.mult,
                      reads=[t_tk, t_ekdc], writes=[t_kdB])
        stages.append(s_tok)

        def s_decay():
            P1, t_P1 = ringG.next()
            for j in range(3):
                kb.op("pe", "matmul", P1[:, j * 128:(j + 1) * 128], gbc[:, j, :], Tri, start=True, stop=False,
                      reads=[t_gbc, t_mk], writes=[t_P1])
                kb.op("pe", "matmul", P1[:, j * 128:(j + 1) * 128], cm.ident[:], N1, start=False, stop=True,
                      reads=[cm.t_ident, t_mk], writes=[t_P1])
            for j in range(3):
                kb.op("act", "activation", dm[:, j, :], P1[:, j * 128:(j + 1) * 128], AF.Exp, scale=-1.0,
                      bias=gcs[:, j:j + 1], reads=[t_P1, t_sm], writes=[t_dm])
            P2, t_P2 = ringG.next()
            for j in range(3):
                kb.op("pe", "matmul", P2[:, j * 128:(j + 1) * 128], gbc[:, j, :], Tri, start=True, stop=False,
                      reads=[t_gbc, t_mk], writes=[t_P2])
                kb.op("pe", "matmul", P2[:, j * 128:(j + 1) * 128], cm.ident[:], M2, start=False, stop=True,
                      reads=[cm.t_ident, t_mk], writes=[t_P2])
            for j in range(3):
                kb.op("act", "activation", dmT[:, j, :], P2[:, j * 128:(j + 1) * 128], AF.Exp, scale=1.0,
                      bias=ngc[:, j:j + 1], reads=[t_P2, t_sm], writes=[t_dmT])
            P3, t_P3 = ringG.next()
            for j in range(3):
                kb.op("pe", "matmul", P3[:, j * 128:(j + 1) * 128], gbc[:, j, :], Tri, start=True, stop=True,
                      reads=[t_gbc, t_mk], writes=[t_P3])
            kb.op("act", "activation", egr[:].rearrange("p j i -> p (j i)"), P3[:, 0:W], AF.Exp, reads=[t_P3], writes=[t_egr])
            for j in range(3):
                lo, hi = rows[j]
                kb.op("dve", "tensor_tensor", qdT[lo:hi, j, :], qTj[j], egr[lo:hi, j, :], op=ALU.mult,
                      reads=[tq, t_egr], writes=[t_qdT])
        stages.append(s_decay)

        Xc = [None]
        XTc = [None]

        def s_gram():
            KK, t_KK = ringG.next()
            for j in GJ:
                kb.op("pe", "matmul", KK[:, j * 128:(j + 1) * 128], kL[j], kR[j], start=True, stop=True,
                      reads=[tq], writes=[t_KK])
            X, t_X = r_X.next()
            if GRAM[0] < 1:
                Xc[0] = (X, t_X); XTc[0] = (X, t_X)
                return
            for j in range(3):
                kb.op("dve", "scalar_tensor_tensor", X[:, j, :], KK[:, j * 128:(j + 1) * 128], nbeta[:, j:j + 1], dm[:, j, :],
                      op0=ALU.mult, op1=ALU.mult, reads=[t_KK, t_sm, t_dm], writes=[t_X])
            if GRAM[0] < 2:
                Xc[0] = (X, t_X); XTc[0] = (X, t_X)
                return
            KQ, t_KQ = ringG.next()
            for j in range(3):
                kb.op("pe", "matmul", KQ[:, j * 128:(j + 1) * 128], kL[j], qR[j], start=True, stop=True,
                      reads=[tq], writes=[t_KQ])
            kb.op("dve", "tensor_tensor", iT[:].rearrange("p j i -> p (j i)"), KQ[:, 0:W], dmT[:].rearrange("p j i -> p (j i)"),
                  op=ALU.mult, reads=[t_KQ, t_dmT], writes=[t_iT])
            if GRAM[0] < 3:
                Xc[0] = (X, t_X); XTc[0] = (X, t_X)
                return
            P, t_P = ringG.next()
            for j in range(3):
                kb.op("pe", "transpose", P[:, j * 128:(j + 1) * 128], X[:, j, :], cm.ident[:], reads=[t_X, cm.t_ident], writes=[t_P])
            XT, t_XT = r_XT.next()
            evac(XT[:].rearrange("p j i -> p (j i)"), P[:, 0:W], [t_P], [t_XT])
            kb.op("dve", "tensor_tensor", TT[:], XT[:], cm.ident[:].unsqueeze(1).to_broadcast([128, 3, 128]), op=ALU.add,
                  reads=[t_XT, cm.t_ident], writes=[t_TT])
            Xc[0] = (X, t_X)
            XTc[0] = (XT, t_XT)
        stages.append(s_gram)

        def mk_level(k):
            def s_level():
                X, t_X = Xc[0]
                XT, t_XT = XTc[0]
                P, t_P = ringG.next()
                for j in range(3):
                    kb.op("pe", "matmul", P[:, j * 128:(j + 1) * 128], XT[:, j, :], X[:, j, :], start=True, stop=True,
                          reads=[t_X, t_XT], writes=[t_P])
                Xn, t_Xn = r_X.next()
                evac(Xn[:].rearrange("p j i -> p (j i)"), P[:, 0:W], [t_P], [t_Xn])
                if k < 5:
                    Q, t_Q = ringG.next()
                    for j in range(3):
                        kb.op("pe", "matmul", Q[:, j * 128:(j + 1) * 128], X[:, j, :], XT[:, j, :], start=True, stop=True,
                              reads=[t_X, t_XT], writes=[t_Q])
                    XTn, t_XTn = r_XT.next()
                    evac(XTn[:].rearrange("p j i -> p (j i)"), Q[:, 0:W], [t_Q], [t_XTn])
                    XTc[0] = (XTn, t_XTn)
                R, t_R = ringG.next()
                for j in range(3):
                    kb.op("pe", "matmul", R[:, j * 128:(j + 1) * 128], Xn[:, j, :], TT[:, j, :], start=True, stop=True,
                          reads=[t_Xn, t_TT], writes=[t_R])
                kb.op("dve", "tensor_tensor", TT[:].rearrange("p j i -> p (j i)"), TT[:].rearrange("p j i -> p (j i)"),
                      R[:, 0:W], op=ALU.add, reads=[t_TT, t_R], writes=[t_TT])
                Xc[0] = (Xn, t_Xn)
            return s_level
        for k in range(1, 6):
            stages.append(mk_level(k))

        def s_final():
            TTf, t_TTf = TT, t_TT
            TTb, t_TTb = r_TTb.next()
            kb.op("dve", "tensor_copy", TTb[:], TTf[:], reads=[t_TTf], writes=[t_TTb])
            for j in range(3):
                kb.op("pe", "matmul", b_misc[:, 256 + j * 64:256 + (j + 1) * 64], TTb[:, j, :], vb[:, j, :], start=True, stop=True,
                      reads=[t_TTb, t_vb], writes=[t_ups])
            kb.op("act", "copy", u[:].rearrange("p j d -> p (j d)"), b_misc[:, 256:448], reads=[t_ups], writes=[t_u])
            P, t_P = ringG.next()
            for j in range(2):
                kb.op("pe", "matmul", P[:, j * 128:(j + 1) * 128], kbgA[:].rearrange("p j d -> p (j d)"), TTb[:, j, :],
                      start=True, stop=True, reads=[t_kbgA, t_TTb], writes=[t_P])
            kb.op("pe", "matmul", P[0:64, 256:384], kbgB[:], TTb[:, 2, :], start=True, stop=True,
                  reads=[t_kbgB, t_TTb], writes=[t_P])
            kb.op("dve", "tensor_copy", wT[0:64, 0:3:2, :], P[0:64, 0:W].rearrange("p (j i) -> p j i", j=3)[:, 0:3:2, :],
                  reads=[t_P], writes=[t_wT])
            kb.op("act", "copy", wT[64:128, 1, :], P[64:128, 128:256], reads=[t_P], writes=[t_wT])
        stages.append(s_final)
        sc = dict(m=m, r=r, u=(u, t_u), wT=(wT, t_wT), qdT=(qdT, t_qdT), iT=(iT, t_iT), kdA=(kdA, t_kdA),
                  kdB=(kdB, t_kdB), egl=(egl, t_egl))
        return stages, sc

    def scan_stages(sc):
        m, r = sc["m"], sc["r"]
        u, t_u = sc["u"]; wT, t_wT = sc["wT"]; qdT, t_qdT = sc["qdT"]; iT, t_iT = sc["iT"]
        kdA, t_kdA = sc["kdA"]; kdB, t_kdB = sc["kdB"]; egl, t_egl = sc["egl"]
        t_Om = st.t_O[m]
        S, t_S = S2[r], t_S2[r]
        Sb, t_Sb = Sb2[r], t_Sb2[r]
        vnew, t_vnew = vnew2[r], t_vnew2[r]
        o_first = first_dir[m] == r
        stages = []
        for c in ((0, 1) if r == 0 else (1, 0)):
            p0, p1 = 64 * c, 64 * c + 64

            def s1(c=c, p0=p0, p1=p1):
                for j in range(3):
                    lo, hi = rows[j]
                    kb.op("pe", "matmul", b_ws[:, j * 64:(j + 1) * 64], wT[:, j, :], Sb[:, j, :], start=True, stop=True,
                          reads=[t_wT, t_Sb], writes=[t_ws])
                kb.op("dve", "tensor_tensor", vnew[p0:p1].rearrange("p j d -> p (j d)"), u[p0:p1].rearrange("p j d -> p (j d)"),
                      b_ws[p0:p1, 0:192], op=ALU.subtract, reads=[t_u, t_ws], writes=[t_vnew])

            def s2(c=c, p0=p0, p1=p1):
                for j in range(3):
                    lo, hi = rows[j]
                    kb.op("pe", "matmul", b_o[:, j * 64:(j + 1) * 64], qdT[:, j, :], Sb[:, j, :],
                          start=True, stop=False, reads=[t_qdT, t_Sb], writes=[t_o])
                    kb.op("pe", "matmul", b_o[:, j * 64:(j + 1) * 64], iT[:, j, :], vnew[:, j, :],
                          start=False, stop=True, reads=[t_iT, t_vnew], writes=[t_o])
                for j in range(2):
                    kb.op("pe", "matmul", b_sn[:, j * 64:(j + 1) * 64], kdA[:, c, :, :].rearrange("p j d -> p (j d)"),
                          vnew[:, j, :], start=True, stop=True, reads=[t_kdA, t_vnew], writes=[t_sn])
                kb.op("pe", "matmul", b_sn[0:64, 128:192], kdB[:, c, :], vnew[:, 2, :], start=True, stop=True,
                      reads=[t_kdB, t_vnew], writes=[t_sn])
                for j in range(3):
                    lo, hi = rows[j]
                    kb.op("dve", "scalar_tensor_tensor", S[lo:hi, j, :], S[lo:hi, j, :], egl[lo:hi, c, j:j + 1],
                          b_sn[lo:hi, j * 64:(j + 1) * 64], op0=ALU.mult, op1=ALU.add,
                          reads=[t_S, t_egl, t_sn], writes=[t_S])
                    kb.op("dve", "tensor_copy", Sb[lo:hi, j, :], S[lo:hi, j, :], reads=[t_S], writes=[t_Sb])
                if o_first:
                    kb.op("act", "copy", st.O[p0:p1, m, :], b_o[p0:p1, 0:192], reads=[t_o], writes=[t_Om])
                else:
                    kb.op("dve", "tensor_tensor", st.O[p0:p1, m, :], st.O[p0:p1, m, :], b_o[p0:p1, 0:192], op=ALU.add,
                          reads=[t_o, t_Om], writes=[t_Om])
            stages.append(s1)
            stages.append(s2)
        return stages

    orders = [[32, 33] + list(range(32)), [33, 32] + list(range(31, -1, -1))]
    first_dir = {}
    for m in range(NTILE):
        first_dir[m] = 0 if orders[0].index(m) < orders[1].index(m) else 1
    pending = [[], []]
    for k in range(NTILE):
        preps = [prep(orders[r][k], r) for r in range(2)]
        nst = len(preps[0][0])
        pk = [0, 0]
        for sidx in range(nst):
            if sidx >= NSTAGES[0]:
                break
            for r in range(2):
                preps[r][0][sidx]()
            if sidx >= 1 and (sidx - 1) % 2 == 0:
                for r in range(2):
                    if pk[r] < len(pending[r]):
                        pending[r][pk[r]]()
                        pk[r] += 1
        for r in range(2):
            while pk[r] < len(pending[r]):
                pending[r][pk[r]]()
                pk[r] += 1
            pending[r] = scan_stages(preps[r][1]) if not noscan else []
    for r in range(2):
        for f in pending[r]:
            f()


def emit_PM_d(kb, cm, io, st, layer, last, banks):
    pre = st.pre + "d_"
    yM, t_yM = io["yM"], io["t_yM"]
    grow = kb.sbuf(pre + "grow", [128, 64], F32); t_grow = Tk(pre + "grow")
    kb.dma(grow[:], io["dn_norm_g"].to_broadcast([128, 64]), writes=[t_grow])
    sq = kb.sbuf(pre + "sq", [128, 192], F32); t_sq = Tk(pre + "sq")
    ssq = kb.sbuf(pre + "ssq", [128, 3], F32); t_ssq = Tk(pre + "ssq")
    on = kb.sbuf(pre + "on", [128, 192], F32); t_on = Tk(pre + "on")
    yring = Ring(kb, pre + "y", [128, 256], BF16, 2)
    oring = Ring(kb, pre + "yo", [128, 2, 512], BF16, 2)
    ringT = PsRing(banks, [6, 7])
    ntiles = LAT_TILES if last else NTILE
    for b0 in range(0, ntiles, 4):
        nt = min(4, ntiles - b0)
        pT, t_pT = ringT.next()
        pT16 = pT[:].bitcast(BF16)
        for t in range(nt):
            m = b0 + t
            kb.op("act", "activation", sq[:], st.O[:, m, :], AF.Square, reads=[st.t_O[m]], writes=[t_sq])
            kb.op("dve", "tensor_reduce", ssq[:], sq[:].rearrange("p (h d) -> p h d", h=3), axis=AX.X, op=ALU.add,
                  reads=[t_sq], writes=[t_ssq])
            kb.op("dve", "tensor_scalar", ssq[:], ssq[:], 1.0 / 64.0, EPS, op0=ALU.mult, op1=ALU.add, reads=[t_ssq], writes=[t_ssq])
            kb.op("pool", "tensor_tensor", ssq[:], ssq[:], cm.mhalf[:, 0:3], op=ALU.pow, reads=[t_ssq, cm.t_mhalf], writes=[t_ssq])
            on3 = on[:].rearrange("p (h d) -> p h d", h=3)
            kb.op("dve", "tensor_tensor", on3, st.O[:, m, :].rearrange("p (h d) -> p h d", h=3),
                  ssq[:].unsqueeze(2).to_broadcast([128, 3, 64]), op=ALU.mult, reads=[st.t_O[m], t_ssq], writes=[t_on])
            kb.op("pool", "tensor_tensor", on3, on3, grow[:].unsqueeze(1).to_broadcast([128, 3, 64]), op=ALU.mult,
                  reads=[t_on, t_grow], writes=[t_on])
            y, t_y = yring.next()
            kb.op("pool", "memset", y[:, 192:256], 0.0, writes=[t_y])
            kb.op("dve", "tensor_tensor", y[:, 0:192], on[:], st.szb[:, m, :], op=ALU.mult, reads=[t_on, st.t_szb[m]], writes=[t_y])
            for c in range(2):
                kb.op("pe", "transpose", pT16[:, c * 512 + t * 128:c * 512 + (t + 1) * 128], y[:, c * 128:(c + 1) * 128],
                      cm.identb[:], reads=[t_y, cm.t_identb], writes=[t_pT])
        ob, t_ob = oring.next()
        n = nt * 128
        kb.op("act", "copy", ob[:, 0, 0:n], pT16[:, 0:n], reads=[t_pT], writes=[t_ob])
        kb.op("dve", "tensor_copy", ob[0:64, 1, 0:n], pT16[0:64, 512:512 + n], reads=[t_pT], writes=[t_ob])
        kb.dma_acc(yM[192:320, b0 * 128:b0 * 128 + n], ob[:, 0, 0:n], reads=[t_ob], writes=[t_yM], owner=t_ob)
        kb.dma_acc(yM[320:384, b0 * 128:b0 * 128 + n], ob[0:64, 1, 0:n], reads=[t_ob], writes=[t_yM], owner=t_ob)


def _new_nc():
    Tk.FENCE = {}
    return bass.Bass("TRN2", target_bir_lowering=False)


def _din(nc, name, shape, dt=F32):
    return nc.dram_tensor(name, list(shape), dt, kind="ExternalInput").ap()


def _dout(nc, name, shape, dt=F32):
    return nc.dram_tensor(name, list(shape), dt, kind="ExternalOutput").ap()


def _dint(nc, name, shape, dt=F32):
    return nc.dram_tensor(name, list(shape), dt, kind="Internal").ap()


def build_L1():
    nc = _new_nc()
    io = dict(c_b=_din(nc, "c_b", [D]), c_ctx=_din(nc, "c_ctx", [D]), mod_w=_din(nc, "mod_w", [L, D, 6 * D]),
              mod_b=_din(nc, "mod_b", [L, 6 * D]), x_in=_din(nc, "x_in", [OWN, D]),
              modv=_dout(nc, "modv", [L, 2, 6 * D]), xmT_out=_dout(nc, "xmT_out", [D, OWN], BF16))
    kb = KB(nc)
    cm = Common(kb)
    banks = make_banks(kb)
    io["t_modv"] = Tk("modv")
    io["t_xmT_out"] = Tk("xmT_out")
    emit_P0(kb, cm, io, banks[0])
    emit_PA(kb, cm, io, PsRing(banks, [6, 7]))
    kb.wait_all("sp", [io["t_modv"], io["t_xmT_out"]])
    kb.emit()
    kb.close()
    return nc


def build_LR(layer, last, debug=False):
    nc = _new_nc()
    n_out = OWN_LAT if last else OWN
    xmT = _din(nc, "xmT_own", [D, OWN], BF16)
    yM = _din(nc, "yM_own", [768, OWN], BF16)
    io = dict(x_in=_din(nc, "x_in", [OWN, D]), modv=_din(nc, "modv", [L, 2, 6 * D]),
              xmT_own=lambda t0, n: xmT[:, t0:t0 + n], yM=lambda t0, n: [yM[:, t0:t0 + n]],
              wA=_din(nc, "wA", [D, 512]), wout=_din(nc, "wout", [D, D]), wup=_din(nc, "wup", [D, DFF]),
              wdown=_din(nc, "wdown", [DFF, D]),
              ln1_g=_din(nc, "ln1_g", [1, D]), ln1_b=_din(nc, "ln1_b", [1, D]),
              ln2_g=_din(nc, "ln2_g", [1, D]), ln2_b=_din(nc, "ln2_b", [1, D]),
              gmlp_ln_g=_din(nc, "gmlp_ln_g", [1, 256]), gmlp_ln_b=_din(nc, "gmlp_ln_b", [1, 256]),
              gmlp_w_s=_din(nc, "gmlp_w_s", [4, 128, 128]), gmlp_b_s=_din(nc, "gmlp_b_s", [4, 128]),
              x1s=(_dout if debug else _dint)(nc, "x1s", [OWN, D]),
              xm2Ts=(_dout if debug else _dint)(nc, "xm2Ts", [D, OWN], BF16),
              x_out=_dout(nc, "x_out", [n_out, D]))
    outs = [Tk("x_out")]
    io["t_x_out"] = outs[0]
    if not last:
        io["xmT_next"] = _dout(nc, "xmT_next", [D, OWN], BF16)
        io["t_xmT_next"] = Tk("xmT_next")
        outs.append(io["t_xmT_next"])
    kb = KB(nc)
    cm = Common(kb)
    banks = make_banks(kb)
    emit_PR(kb, cm, io, layer, last, banks)
    kb.wait_all("sp", outs)
    kb.emit()
    kb.close()
    return nc


def m_host_weights(w_in, conv_w, h):
    qB = w_in[:, 512 + 192 * h:512 + 192 * h + 192]
    kB = w_in[:, 896 + 64 * h:896 + 64 * h + 64]
    vB = w_in[:, 1024 + 64 * h:1024 + 64 * h + 64]
    wB = np.ascontiguousarray(np.concatenate([qB, kB, vB], 1))
    base = 1152
    z = w_in[:, base + 1152 + 192 * h:base + 1152 + 192 * h + 192]
    a = [w_in[:, base + 1536 + dd * 6 + 3 * h:base + 1536 + dd * 6 + 3 * h + 3] for dd in range(2)]
    bt = [w_in[:, base + 1548 + dd * 6 + 3 * h:base + 1548 + dd * 6 + 3 * h + 3] for dd in range(2)]
    wCt = np.ascontiguousarray(np.concatenate([z] + a + bt, 1))

    def ch(typ, j):
        c0 = typ * 384 + (3 * h + j) * 64
        return c0, c0 + 64
    tiles = [((0, 0), (0, 1)), ((1, 0), (1, 1)), ((0, 2), (2, 1)), ((1, 2), (2, 0)), (None, (2, 2))]
    cols = []
    cw = np.zeros((128, 5, 5), np.float32)
    for i, (lo, hi) in enumerate(tiles):
        for half, sel in enumerate((lo, hi)):
            if sel is None:
                cols.append(np.zeros((w_in.shape[0], 64), np.float32))
            else:
                c0, c1 = ch(*sel)
                cols.append(w_in[:, base + c0:base + c1])
                cw[half * 64:(half + 1) * 64, i, :] = conv_w[:, c0:c1].T
    wCf = np.ascontiguousarray(np.concatenate(cols, 1))
    return wB, wCt, wCf, cw


def build_LM(layer, last, parts=("attn", "dn")):
    nc = _new_nc()
    io = dict(xmT_full=_din(nc, "xmT_full", [D, NTOK], BF16),
              wB=_din(nc, "wB", [D, 320]), wCt=_din(nc, "wCt", [D, 204]), wCf=_din(nc, "wCf", [D, 640]),
              convw=_din(nc, "convw", [128, 5, 5]),
              attn_q_g=_din(nc, "attn_q_g", [1, 64]), attn_k_g=_din(nc, "attn_k_g", [1, 64]),
              dn_dt_bias=_din(nc, "dn_dt_bias", [1, 6]), dn_a_log=_din(nc, "dn_a_log", [1, 6]),
              dn_norm_g=_din(nc, "dn_norm_g", [1, 64]),
              ropeC=_din(nc, "ropeC", [SEQ, 64]), ropeS=_din(nc, "ropeS", [SEQ, 64]),
              yM=_dout(nc, "yM", [384, NTOK], BF16))
    io["t_yM"] = Tk("yM")
    xfull = io["xmT_full"]
    io["xmT_pieces"] = lambda tok0, n: [(0, 8, 0, n, xfull[:, tok0:tok0 + n])]
    kb = KB(nc)
    cm = Common(kb)
    banks = make_banks(kb)
    emit_PM(kb, cm, io, layer, last, banks, parts)
    kb.wait_all("sp", [io["t_yM"]])
    kb.emit()
    kb.close()
    return nc


def emit_PM(kb, cm, io, layer, last, banks, parts=("attn", "dn")):
    st = emit_PM_setup(kb, cm, io, layer)
    mk = kb.mark()
    emit_PM_a(kb, cm, io, st, layer, banks)
    kb.release(mk)
    if "attn" in parts:
        emit_PM_attn(kb, cm, io, st, layer, last, banks)
        kb.release(mk)
    if "dn" in parts:
        pre = st.pre
        raw = [kb.sbuf(pre + f"raw{s}", [128, 5, 516], F32) for s in range(3)]
        t_raw = [Tk(pre + f"raw{s}") for s in range(3)]
        xring = Ring(kb, pre + "bxm", [128, 8, 512], BF16, 2)
        emit_PM_b(kb, cm, io, st, layer, banks, xring, raw, t_raw)
        kb.release(mk)
        if "nob" in parts:
            return st
        emit_PM_c(kb, cm, io, st, layer, banks, noscan=("noscan" in parts))
        kb.release(mk)
        if "noc" in parts:
            return st
        emit_PM_d(kb, cm, io, st, layer, last, banks)
        kb.release(mk)
    return st


PAIR_GROUPS = [[0, 1], [2, 3], [4, 5], [6, 7]]
_LAYER_IN = dict(wA=[D, 512], wout=[D, D], wup=[D, DFF], wdown=[DFF, D], ln1_g=[1, D], ln1_b=[1, D], ln2_g=[1, D],
                 ln2_b=[1, D], gmlp_ln_g=[1, 256], gmlp_ln_b=[1, 256], gmlp_w_s=[4, 128, 128], gmlp_b_s=[4, 128],
                 wB=[D, 320], wCt=[D, 204], wCf=[D, 640], convw=[128, 5, 5], attn_q_g=[1, 64], attn_k_g=[1, 64],
                 dn_dt_bias=[1, 6], dn_a_log=[1, 6], dn_norm_g=[1, 64])


def build_fused():
    nc = _new_nc()
    dscr = lambda name, shape, dt=F32: nc.dram_tensor(name, list(shape), dt).ap()
    g = dict(c_b=_din(nc, "c_b", [D]), c_ctx=_din(nc, "c_ctx", [D]), mod_w=_din(nc, "mod_w", [L, D, 6 * D]),
             mod_b=_din(nc, "mod_b", [L, 6 * D]), x_in=_din(nc, "x_in", [OWN, D]), hsel=_din(nc, "hsel", [1, 2]),
             ropeC=_din(nc, "ropeC", [SEQ, 64]), ropeS=_din(nc, "ropeS", [SEQ, 64]))
    lay = [{k: _din(nc, f"{k}{i}", shp) for k, shp in _LAYER_IN.items()} for i in range(L)]
    x_out = _dout(nc, "x_out", [OWN_LAT, D])
    modv = dscr("modv", [L, 2, 6 * D]); t_modv = Tk("modv")
    xmT_own = [dscr(f"xmT_own{i}", [D, OWN], BF16) for i in range(L)]
    t_xmT_own = [Tk(f"xmT_own{i}") for i in range(L)]
    NPX = 4
    RX = D // NPX
    xmT_g = [[dscr(f"xmT_g{i}_{pc}", [2 * RX, OWN], BF16) for pc in range(NPX)] for i in range(L)]
    t_xmT_g = [Tk(f"xmT_g{i}") for i in range(L)]
    yM = [dscr(f"yM{i}", [384, NTOK], BF16) for i in range(L)]
    t_yM = [Tk(f"yM{i}") for i in range(L)]
    yG = [[dscr(f"yG{i}_{pc}", [384, NTOK], BF16) for pc in range(2)] for i in range(L)]
    t_yG = [Tk(f"yG{i}") for i in range(L)]
    x1s = [dscr(f"x1s{i}", [OWN, D]) for i in range(L)]
    xm2Ts = [dscr(f"xm2Ts{i}", [D, OWN], BF16) for i in range(L)]
    xres = [dscr(f"xres{i}", [OWN, D]) for i in range(L - 1)]
    t_xres = [Tk(f"xres{i}") for i in range(L - 1)]
    t_xout = Tk("x_out")

    kb = KB(nc)
    cm = Common(kb)
    banks = make_banks(kb)
    mk0 = kb.mark()
    emit_P0(kb, cm, dict(c_b=g["c_b"], c_ctx=g["c_ctx"], mod_w=g["mod_w"], mod_b=g["mod_b"], modv=modv, t_modv=t_modv), banks[0])
    kb.release(mk0)
    emit_PA(kb, cm, dict(x_in=g["x_in"], modv=modv, t_modv=t_modv, xmT_out=xmT_own[0], t_xmT_out=t_xmT_own[0]),
            PsRing(banks, [6, 7]))
    kb.release(mk0)
    for i in range(L):
        last = i == L - 1
        for pc in range(NPX):
            kb.coll("AllGather", xmT_own[i][pc * RX:(pc + 1) * RX, :], xmT_g[i][pc], PAIR_GROUPS,
                    reads=[t_xmT_own[i]], writes=[t_xmT_g[i]], acc=(pc > 0))
        xg = xmT_g[i]
        cpp = RX // 128

        def pieces(tok0, n, xg=xg):
            out = []
            for pc in range(NPX):
                if tok0 < SEQ:
                    r, loc = divmod(tok0, OWN_LAT)
                    out.append((pc * cpp, cpp, 0, n, xg[pc][r * RX:(r + 1) * RX, loc:loc + n]))
                else:
                    for r in range(2):
                        out.append((pc * cpp, cpp, 128 * r, 128, xg[pc][r * RX:(r + 1) * RX, OWN_LAT:OWN]))
            return out
        ioM = dict(lay[i])
        ioM.update(xmT_pieces=pieces, t_xmT_full=t_xmT_g[i], ropeC=g["ropeC"], ropeS=g["ropeS"], yM=yM[i], t_yM=t_yM[i])
        emit_PM(kb, cm, ioM, i, last, banks)
        kb.release(mk0)
        for pc in range(2):
            kb.coll("AllGather", yM[i][pc * 192:(pc + 1) * 192, :], yG[i][pc], PAIR_GROUPS,
                    reads=[t_yM[i]], writes=[t_yG[i]], acc=(pc > 0))
        yg = yG[i]

        def ycands(tok0, n, yg=yg):
            out = []
            for h in range(2):
                col = OWN_LAT * h + tok0 if tok0 < OWN_LAT else SEQ + OWN_CTX * h + (tok0 - OWN_LAT)
                out.append([yg[0][:, col:col + n], yg[1][:, col:col + n]])
            return out
        xo = xmT_own[i]
        ioR = dict(lay[i])
        ioR.update(modv=modv, t_modv=t_modv, x_in=g["x_in"] if i == 0 else xres[i - 1],
                   t_x_in=None if i == 0 else t_xres[i - 1],
                   xmT_own=lambda t0, n, xo=xo: xo[:, t0:t0 + n], t_xmT_own=t_xmT_own[i],
                   yM=ycands, t_yM_in=t_yG[i], hsel=g["hsel"], x1s=x1s[i], xm2Ts=xm2Ts[i],
                   x_out=x_out if last else xres[i], t_x_out=t_xout if last else t_xres[i])
        if not last:
            ioR.update(xmT_next=xmT_own[i + 1], t_xmT_next=t_xmT_own[i + 1])
        emit_PR(kb, cm, ioR, i, last, banks)
        kb.release(mk0)
    outs = [t_xout]
    kb.wait_all("sp", outs)
    kb.emit()
    kb.close()
    return nc


WOUT_PERM = np.arange(D)


def kernel_fused(x, c, ctx, c_ctx, mod_w, mod_b, w_in, w_out, gmlp_ln_g, gmlp_ln_b, gmlp_w_s, gmlp_b_s,
                 attn_q_g, attn_k_g, dn_conv_w, dn_a_log, dn_dt_bias, dn_norm_g,
                 ln1_g, ln1_b, ln2_g, ln2_b, w_up, w_down):
    f32 = lambda a: np.ascontiguousarray(np.asarray(a), dtype=np.float32)
    (x, c, ctx, c_ctx, mod_w, mod_b, w_in, w_out, gmlp_ln_g, gmlp_ln_b, gmlp_w_s, gmlp_b_s, attn_q_g, attn_k_g,
     dn_conv_w, dn_a_log, dn_dt_bias, dn_norm_g, ln1_g, ln1_b, ln2_g, ln2_b, w_up, w_down) = map(
        f32, (x, c, ctx, c_ctx, mod_w, mod_b, w_in, w_out, gmlp_ln_g, gmlp_ln_b, gmlp_w_s, gmlp_b_s, attn_q_g, attn_k_g,
              dn_conv_w, dn_a_log, dn_dt_bias, dn_norm_g, ln1_g, ln1_b, ln2_g, ln2_b, w_up, w_down))
    cores = [(b, h) for b in range(4) for h in range(2)]
    ropeC, ropeS = rope_tables()
    shared = []
    for i in range(L):
        shared.append({f"wA{i}": np.ascontiguousarray(w_in[i][:, 0:512]), f"wout{i}": np.ascontiguousarray(w_out[i][WOUT_PERM]),
                       f"wup{i}": w_up[i], f"wdown{i}": w_down[i], f"ln1_g{i}": ln1_g[i][None], f"ln1_b{i}": ln1_b[i][None],
                       f"ln2_g{i}": ln2_g[i][None], f"ln2_b{i}": ln2_b[i][None], f"gmlp_ln_g{i}": gmlp_ln_g[i][None],
                       f"gmlp_ln_b{i}": gmlp_ln_b[i][None], f"gmlp_w_s{i}": gmlp_w_s[i], f"gmlp_b_s{i}": gmlp_b_s[i],
                       f"attn_q_g{i}": attn_q_g[i][None], f"attn_k_g{i}": attn_k_g[i][None], f"dn_norm_g{i}": dn_norm_g[i][None]})
    perh = []
    for h in range(2):
        dct = {}
        for i in range(L):
            wB, wCt, wCf, cw = m_host_weights(w_in[i], dn_conv_w[i], h)
            dct.update({f"wB{i}": wB, f"wCt{i}": wCt, f"wCf{i}": wCf, f"convw{i}": cw,
                        f"dn_dt_bias{i}": np.ascontiguousarray(dn_dt_bias[i][:, 3 * h:3 * h + 3].reshape(1, 6)),
                        f"dn_a_log{i}": np.ascontiguousarray(dn_a_log[i][:, 3 * h:3 * h + 3].reshape(1, 6))})
        perh.append(dct)
    in_maps = []
    for b, h in cores:
        m = dict(c_b=c[b], c_ctx=c_ctx, mod_w=mod_w, mod_b=mod_b, ropeC=ropeC, ropeS=ropeS,
                 hsel=np.array([[1.0 - h, float(h)]], np.float32),
                 x_in=np.ascontiguousarray(np.concatenate([x[b, OWN_LAT * h:OWN_LAT * (h + 1)],
                                                           ctx[b, OWN_CTX * h:OWN_CTX * (h + 1)]], 0)))
        for i in range(L):
            m.update(shared[i])
        m.update(perh[h])
        in_maps.append(m)
    res = _run(build_fused(), in_maps, "fused")
    out = np.stack([np.concatenate([res[2 * b]["x_out"], res[2 * b + 1]["x_out"]], 0) for b in range(4)], 0)
    return np.ascontiguousarray(out, dtype=np.float32)


_BF = ml_dtypes.bfloat16


def _run(nc, in_maps, tag=""):
    import sys, time
    t0 = time.time()
    res = run_bass_kernel_spmd(nc, in_maps, core_ids=list(range(len(in_maps))))
    print(f"[kernel] launch {tag} done in {time.time() - t0:.1f}s", file=sys.stderr, flush=True)
    return res.results


def kernel_unfused(x, c, ctx, c_ctx, mod_w, mod_b, w_in, w_out, gmlp_ln_g, gmlp_ln_b, gmlp_w_s, gmlp_b_s,
           attn_q_g, attn_k_g, dn_conv_w, dn_a_log, dn_dt_bias, dn_norm_g,
           ln1_g, ln1_b, ln2_g, ln2_b, w_up, w_down):
    f32 = lambda a: np.ascontiguousarray(np.asarray(a), dtype=np.float32)
    x, c, ctx, c_ctx, mod_w, mod_b, w_in, w_out = map(f32, (x, c, ctx, c_ctx, mod_w, mod_b, w_in, w_out))
    gmlp_ln_g, gmlp_ln_b, gmlp_w_s, gmlp_b_s = map(f32, (gmlp_ln_g, gmlp_ln_b, gmlp_w_s, gmlp_b_s))
    attn_q_g, attn_k_g, dn_conv_w, dn_a_log, dn_dt_bias, dn_norm_g = map(
        f32, (attn_q_g, attn_k_g, dn_conv_w, dn_a_log, dn_dt_bias, dn_norm_g))
    ln1_g, ln1_b, ln2_g, ln2_b, w_up, w_down = map(f32, (ln1_g, ln1_b, ln2_g, ln2_b, w_up, w_down))
    cores = [(b, h) for b in range(4) for h in range(2)]
    ropeC, ropeS = rope_tables()

    x_cur = [np.ascontiguousarray(np.concatenate([x[b, OWN_LAT * h:OWN_LAT * (h + 1)],
                                                  ctx[b, OWN_CTX * h:OWN_CTX * (h + 1)]], 0)) for b, h in cores]
    r1 = _run(build_L1(), tag="L1", in_maps=[dict(c_b=c[b], c_ctx=c_ctx, mod_w=mod_w, mod_b=mod_b, x_in=x_cur[k])
                           for k, (b, h) in enumerate(cores)])
    modv = [r["modv"] for r in r1]
    xmT = [r["xmT_out"] for r in r1]
    for i in range(L):
        last = i == L - 1
        mw = [m_host_weights(w_in[i], dn_conv_w[i], h) for h in range(2)]
        inM = []
        for k, (b, h) in enumerate(cores):
            a0, a1 = xmT[2 * b], xmT[2 * b + 1]
            full = np.ascontiguousarray(np.concatenate([a0[:, :OWN_LAT], a1[:, :OWN_LAT], a0[:, OWN_LAT:], a1[:, OWN_LAT:]], 1))
            wB, wCt, wCf, cw = mw[h]
            inM.append(dict(xmT_full=full, wB=wB, wCt=wCt, wCf=wCf, convw=cw,
                            attn_q_g=attn_q_g[i][None], attn_k_g=attn_k_g[i][None],
                            dn_dt_bias=np.ascontiguousarray(dn_dt_bias[i][:, 3 * h:3 * h + 3].reshape(1, 6)),
                            dn_a_log=np.ascontiguousarray(dn_a_log[i][:, 3 * h:3 * h + 3].reshape(1, 6)),
                            dn_norm_g=dn_norm_g[i][None], ropeC=ropeC, ropeS=ropeS))
        rM = _run(build_LM(i, last), inM, f"LM{i}")
        inR = []
        for k, (b, h) in enumerate(cores):
            y0, y1 = rM[2 * b]["yM"], rM[2 * b + 1]["yM"]
            cols = np.r_[OWN_LAT * h:OWN_LAT * (h + 1), SEQ + OWN_CTX * h:SEQ + OWN_CTX * (h + 1)]
            yown = np.ascontiguousarray(np.concatenate([y0[0:192], y1[0:192], y0[192:384], y1[192:384]], 0)[:, cols])
            inR.append(dict(x_in=x_cur[k], modv=modv[k], xmT_own=xmT[k], yM_own=yown,
                            wA=np.ascontiguousarray(w_in[i][:, 0:512]), wout=w_out[i], wup=w_up[i], wdown=w_down[i],
                            ln1_g=ln1_g[i][None], ln1_b=ln1_b[i][None], ln2_g=ln2_g[i][None], ln2_b=ln2_b[i][None],
                            gmlp_ln_g=gmlp_ln_g[i][None], gmlp_ln_b=gmlp_ln_b[i][None],
                            gmlp_w_s=gmlp_w_s[i], gmlp_b_s=gmlp_b_s[i]))
        rR = _run(build_LR(i, last), inR, f"LR{i}")
        x_cur = [r["x_out"] for r in rR]
        if not last:
            xmT = [r["xmT_next"] for r in rR]
    out = np.stack([np.concatenate([x_cur[2 * b][:OWN_LAT], x_cur[2 * b + 1][:OWN_LAT]], 0) for b in range(4)], 0)
    return np.ascontiguousarray(out, dtype=np.float32)


kernel = kernel_fused
```
